# Optimizing a Trainium2 kernel written in Bass

```python
import jax, jax.numpy as jnp
from jax import lax
import numpy as np

D_MODEL = 1024
BATCH = 8
SEQ = 2048
DEPTH = 2
DEC_BATCH = 8
DEC_SEQ = 32
PAST_LEN = 4096

CHUNK = 64
MIX_W = D_MODEL
HEAD_DIM = 64
SG_W = D_MODEL // 4
N_SG_HEADS = SG_W // HEAD_DIM
SG_HEAD_DIM = HEAD_DIM
SG_LEN = 128
SC_W = D_MODEL // 4
CONV_W = 3
ATT_W = D_MODEL // 2
N_HEADS = ATT_W // HEAD_DIM
N_KV = N_HEADS // 2
N_IDX = 8
IDX_DIM = 32
TOPK_MAX = 256
Q_BLOCK = 128
ROPE_THETA = 10000.0
IDX_W_SCALE = (N_IDX * IDX_DIM) ** -0.5
D_FF = 4 * D_MODEL
EPS = 1e-6
PROJ_SIZES = (2 * SG_W, 3 * SC_W, N_HEADS * HEAD_DIM, N_KV * HEAD_DIM, N_KV * HEAD_DIM,
              N_IDX * IDX_DIM, IDX_DIM, N_IDX)
IN_W = sum(PROJ_SIZES)

kernel_name = "hybrid_stream_sg_conv_dsa_step"


def rms_norm(x, g):
    xf = x.astype(jnp.float32)
    y = xf * lax.rsqrt(jnp.mean(xf * xf, axis=-1, keepdims=True) + EPS)
    return (y * g.astype(jnp.float32)).astype(x.dtype)


def rope(x, pos):
    half = x.shape[-1] // 2
    inv_freq = ROPE_THETA ** (-jnp.arange(half, dtype=jnp.float32) / half)
    ang = pos.astype(jnp.float32)[:, None] * inv_freq[None, :]
    cos = jnp.cos(ang)[None, :, None, :]
    sin = jnp.sin(ang)[None, :, None, :]
    xf = x.astype(jnp.float32)
    x1, x2 = xf[..., :half], xf[..., half:]
    return jnp.concatenate([x1 * cos - x2 * sin, x2 * cos + x1 * sin], axis=-1).astype(x.dtype)


def spatial_gate(za, w_s, b_s):
    z = jax.nn.gelu(za)
    u, v = jnp.split(z, 2, axis=-1)
    B, T, _ = v.shape
    n = min(T, SG_LEN)
    vc = v.reshape(B, T // n, n, N_SG_HEADS, SG_HEAD_DIM)
    i = jnp.arange(n)
    mask = (i[None, :] // CHUNK) <= (i[:, None] // CHUNK)
    wm = jnp.where(mask[None], w_s[:, :n, :n], 0.0)
    mixed = jnp.einsum('hij,bcjhd->bcihd', wm, vc) + b_s[:, :n].T[None, None, :, :, None]
    return u * mixed.reshape(B, T, SG_W), v


def short_conv(zb, conv_w, conv_state):
    gb, gc, h_in = jnp.split(zb, 3, axis=-1)
    xc = gc * h_in
    xp = jnp.concatenate([conv_state.astype(xc.dtype), xc], axis=1)
    T = xc.shape[1]
    y = conv_w[0] * xp[:, 0:T]
    for tap in range(1, CONV_W):
        y = y + conv_w[tap] * xp[:, tap:tap + T]
    return gb * y, xp[:, -(CONV_W - 1):]


_gather_rows = jax.vmap(lambda rows, ids: rows[ids])


def sparse_attend(q, qi, wi, k_all, v_all, ki_all, limit, k_top):
    B, Tq = q.shape[:2]
    S = k_all.shape[1]
    logits = jax.nn.relu(jnp.einsum('bqnd,bsd->bqns', qi, ki_all).astype(jnp.float32))
    iscore = jnp.einsum('bqn,bqns->bqs', wi.astype(jnp.float32), logits)
    admissible = jnp.arange(S)[None, :] < limit[:, None]
    iscore = jnp.where(admissible[None], iscore, -jnp.inf)
    _, sel = lax.top_k(iscore, k_top)
    valid = sel < limit[None, :, None]
    k_sel = _gather_rows(k_all, sel)
    v_sel = _gather_rows(v_all, sel)
    qg = q.reshape(B, Tq, N_KV, N_HEADS // N_KV, HEAD_DIM)
    s = jnp.einsum('bqhgd,bqkhd->bqhgk', qg, k_sel).astype(jnp.float32) * (HEAD_DIM ** -0.5)
    s = jnp.where(valid[:, :, None, None, :], s, -jnp.inf)
    p = jax.nn.softmax(s, axis=-1)
    o = jnp.einsum('bqhgk,bqkhd->bqhgd', p.astype(v_sel.dtype), v_sel)
    return o.reshape(B, Tq, N_HEADS * HEAD_DIM)


def sparse_attn_prompt(q, qi, wi, k, v, ki):
    B, T = q.shape[:2]
    k_top = min(TOPK_MAX, T // 4)
    nb = T // Q_BLOCK

    def to_blocks(a):
        return a.reshape((B, nb, Q_BLOCK) + a.shape[2:]).swapaxes(0, 1)

    def one_block(args):
        qb, qib, wib, start = args
        limit = ((start + jnp.arange(Q_BLOCK)) // CHUNK + 1) * CHUNK
        return sparse_attend(qb, qib, wib, k, v, ki, limit, k_top)

    out = lax.map(one_block, (to_blocks(q), to_blocks(qi), to_blocks(wi), jnp.arange(nb) * Q_BLOCK))
    return out.swapaxes(0, 1).reshape(B, T, ATT_W)


def layer(x, pos, conv_state, past_k, past_v, past_ki,
          g_mix, w_in, q_gain, k_gain, w_s, b_s, conv_w, w_out, g_ffn, w_up, w_down):
    B, T, _ = x.shape
    h = rms_norm(x, g_mix)
    z = h @ w_in
    points = np.cumsum(PROJ_SIZES)[:-1].tolist()
    za, zb, q, k, v, qi, ki, wi = jnp.split(z, points, axis=-1)
    y_a, sg_v = spatial_gate(za, w_s, b_s)
    y_b, conv_new = short_conv(zb, conv_w, conv_state)
    q = rope(rms_norm(q.reshape(B, T, N_HEADS, HEAD_DIM), q_gain), pos)
    k = rope(rms_norm(k.reshape(B, T, N_KV, HEAD_DIM), k_gain), pos)
    v = v.reshape(B, T, N_KV, HEAD_DIM)
    qi = rope(qi.reshape(B, T, N_IDX, IDX_DIM), pos)
    ki = rope(ki[:, :, None, :], pos)[:, :, 0, :]
    wi = wi * IDX_W_SCALE
    if past_k is None:
        y_c = sparse_attn_prompt(q, qi, wi, k, v, ki)
    else:
        k_all = jnp.concatenate([past_k.astype(k.dtype), k], axis=1)
        v_all = jnp.concatenate([past_v.astype(v.dtype), v], axis=1)
        ki_all = jnp.concatenate([past_ki.astype(ki.dtype), ki], axis=1)
        L = k_all.shape[1]
        limit = jnp.full((T,), L, dtype=jnp.int32)
        y_c = sparse_attend(q, qi, wi, k_all, v_all, ki_all, limit, min(TOPK_MAX, L // 4))
    x = x + jnp.concatenate([y_a, y_b, y_c], axis=-1) @ w_out
    hf = rms_norm(x, g_ffn)
    x = x + jnp.square(jax.nn.relu(hf @ w_up)) @ w_down
    return x, k, v, ki, conv_new, sg_v


def setup_inputs(seed: int = 0) -> dict:
    key = jax.random.key(seed)
    ks = jax.random.split(key, 20)

    def nrm(k, shape, scale=1.0):
        return jax.random.normal(k, shape, dtype=jnp.float32) * scale

    return {
        "x_prompt": nrm(ks[0], (BATCH, SEQ, D_MODEL)),
        "x_sample": nrm(ks[1], (DEC_BATCH, DEC_SEQ, D_MODEL)),
        "cache_k": nrm(ks[2], (DEPTH, DEC_BATCH, PAST_LEN, N_KV, HEAD_DIM)),
        "cache_v": nrm(ks[3], (DEPTH, DEC_BATCH, PAST_LEN, N_KV, HEAD_DIM)),
        "cache_kidx": nrm(ks[4], (DEPTH, DEC_BATCH, PAST_LEN, IDX_DIM)),
        "state_conv": nrm(ks[5], (DEPTH, DEC_BATCH, CONV_W - 1, SC_W)),
        "g_mix": 1.0 + nrm(ks[6], (DEPTH, D_MODEL), 0.02),
        "w_in": nrm(ks[7], (DEPTH, D_MODEL, IN_W), D_MODEL ** -0.5),
        "q_gain": 1.0 + nrm(ks[8], (DEPTH, HEAD_DIM), 0.02),
        "k_gain": 1.0 + nrm(ks[9], (DEPTH, HEAD_DIM), 0.02),
        "w_s": nrm(ks[10], (DEPTH, N_SG_HEADS, SG_LEN, SG_LEN), 0.05),
        "b_s": 1.0 + nrm(ks[11], (DEPTH, N_SG_HEADS, SG_LEN), 0.02),
        "conv_w": nrm(ks[12], (DEPTH, CONV_W, SC_W), CONV_W ** -0.5),
        "w_out": nrm(ks[13], (DEPTH, MIX_W, D_MODEL), MIX_W ** -0.5),
        "g_ffn": 1.0 + nrm(ks[14], (DEPTH, D_MODEL), 0.02),
        "w_up": nrm(ks[15], (DEPTH, D_MODEL, D_FF), D_MODEL ** -0.5),
        "w_down": nrm(ks[16], (DEPTH, D_FF, D_MODEL), D_FF ** -0.5),
    }


def reference(x_prompt, x_sample, cache_k, cache_v, cache_kidx, state_conv,
              g_mix, w_in, q_gain, k_gain, w_s, b_s, conv_w, w_out, g_ffn, w_up, w_down):
    Bp, Tp, _ = x_prompt.shape
    Ts = x_sample.shape[1]
    past_len = cache_k.shape[2]
    pos_p = jnp.arange(Tp)
    pos_s = past_len + jnp.arange(Ts)
    zero_conv = jnp.zeros((Bp, CONV_W - 1, SC_W), dtype=x_prompt.dtype)
    yp, ys = x_prompt, x_sample
    kp_l, vp_l, kip_l, cp_l = [], [], [], []
    ks_l, vs_l, kis_l, cs_l, sgs_l = [], [], [], [], []
    for l in range(DEPTH):
        params = (g_mix[l], w_in[l], q_gain[l], k_gain[l], w_s[l], b_s[l], conv_w[l],
                  w_out[l], g_ffn[l], w_up[l], w_down[l])
        yp, kp, vp, kip, cp, _ = layer(yp, pos_p, zero_conv, None, None, None, *params)
        ys, kss, vss, kis, cs, sgs = layer(ys, pos_s, state_conv[l], cache_k[l], cache_v[l],
                                           cache_kidx[l], *params)
        kp_l.append(kp); vp_l.append(vp); kip_l.append(kip); cp_l.append(cp)
        ks_l.append(kss); vs_l.append(vss); kis_l.append(kis); cs_l.append(cs); sgs_l.append(sgs)
    new_k_prompt = jnp.stack(kp_l)
    new_v_prompt = jnp.stack(vp_l)
    new_kidx_prompt = jnp.stack(kip_l)
    new_conv_prompt = jnp.stack(cp_l)
    new_k_sample = jnp.stack(ks_l)
    new_v_sample = jnp.stack(vs_l)
    new_kidx_sample = jnp.stack(kis_l)
    new_conv_sample = jnp.stack(cs_l)
    new_sgv_sample = jnp.stack(sgs_l)
    return (yp, ys, new_k_prompt, new_v_prompt, new_kidx_prompt, new_conv_prompt,
            new_k_sample, new_v_sample, new_kidx_sample, new_conv_sample, new_sgv_sample)
```

```python
import contextlib
import numpy as np
import concourse.bass as bass
import concourse.mybir as mybir
from concourse.bass_utils import run_bass_kernel_spmd

F32 = mybir.dt.float32
BF16 = mybir.dt.bfloat16
ALU = mybir.AluOpType
AF = mybir.ActivationFunctionType
AX = mybir.AxisListType

D = 1024
SEQ = 2048
NT = 17
TS = 32
PAST = 4096
DEPTH = 2
INW = 2600
NIT = 16
EPS = 1e-6
NEG = -1.0e30
TMW = 1576
GC = 0.7978845608028654


class Buf:
    __slots__ = ("name", "w", "r", "xr")

    def __init__(self, name="b", xr=False):
        self.name = name
        self.w = None
        self.r = []
        self.xr = xr


def seed(new, olds):
    lst = []
    for o in olds:
        if o.w is not None:
            lst.append(o.w)
        lst.extend(o.r)
    new.w = None
    new.r = lst


class Op:
    __slots__ = ("eng", "fn", "deps", "signal", "dma", "semi", "val", "lane")


class Sched:
    ENGS = ("pe", "act", "dve", "pool", "sp")
    LIMIT = 30000
    NLANES = 8

    def __init__(self):
        self.ops = {e: [] for e in self.ENGS}

    def add(self, eng, fn, reads=(), writes=(), dma=False):
        op = Op()
        op.eng = eng
        op.fn = fn
        op.signal = False
        op.dma = dma
        deps = {}
        for b in reads:
            if b.w is not None:
                deps[id(b.w)] = (b.w, True)
            if b.xr:
                for r in b.r:
                    if r.eng != eng and id(r) not in deps:
                        deps[id(r)] = (r, False)
        for b in writes:
            if b.w is not None and id(b.w) not in deps:
                deps[id(b.w)] = (b.w, False)
            for r in b.r:
                if id(r) not in deps:
                    deps[id(r)] = (r, False)
        op.deps = []
        for d, raw in deps.values():
            if d is op:
                continue
            if not d.dma and not dma and d.eng == eng:
                if eng == "pe":
                    continue
            if not d.dma:
                d.signal = True
            op.deps.append(d)
        for b in reads:
            b.r.append(op)
        for b in writes:
            b.w = op
            b.r = []
        self.ops[eng].append(op)
        return op

    def finalize(self, nc, stack):
        self.sems = {}
        self.final_dma = []
        keys = set()
        for e in self.ENGS:
            cnt = 0
            nd = 0
            lane_cnt = [0] * self.NLANES
            nl = 2 if e == "pool" else self.NLANES
            for op in self.ops[e]:
                if op.dma:
                    op.lane = nd % nl
                    nd += 1
                    lane_cnt[op.lane] += 1
                    op.val = 16 * lane_cnt[op.lane]
                    op.semi = ("dma", e, op.lane)
                    keys.add(op.semi)
                elif op.signal:
                    cnt += 1
                    op.semi = ("c", e, (cnt - 1) // self.LIMIT)
                    op.val = (cnt - 1) % self.LIMIT + 1
                    keys.add(op.semi)
            for ln in range(self.NLANES):
                if lane_cnt[ln]:
                    self.final_dma.append((("dma", e, ln), 16 * lane_cnt[ln]))
        for k in sorted(keys, key=str):
            self.sems[k] = stack.enter_context(nc.semaphore("s_" + "_".join(map(str, k))))

    def emit(self, block):
        def run(engname):
            def body(eng):
                seen = {}
                for op in self.ops[engname]:
                    waits = [(d.semi, d.val) for d in op.deps]
                    if op.dma and op.val > 16:
                        waits.append((op.semi, op.val - 16))
                    for k, v in waits:
                        if seen.get(k, 0) < v:
                            eng.wait_ge(self.sems[k], v)
                            seen[k] = v
                    ins = op.fn(eng)
                    if op.dma:
                        ins.then_inc(self.sems[op.semi], 16)
                    elif op.signal:
                        ins.then_inc(self.sems[op.semi], 1)
                if engname == "sp":
                    for k, v in self.final_dma:
                        if seen.get(k, 0) < v:
                            eng.wait_ge(self.sems[k], v)
            return body
        block.tensor(run("pe"))
        block.scalar(run("act"))
        block.vector(run("dve"))
        block.gpsimd(run("pool"))
        block.sync(run("sp"))


class _Stop(Exception):
    pass


DBG = {"tiles": NT, "stage": 99, "ffn": True, "depth": DEPTH, "sample": True, "tm": True, "fm": True}


def build_program(depth=DEPTH):
    depth = min(depth, DBG["depth"])
    nc = bass.Bass("TRN2", target_bir_lowering=False, dynamic_dma_scratch_size=6144)
    S = Sched()

    def din(name, shape):
        return nc.dram_tensor(name, list(shape), F32, kind="ExternalInput").ap()

    def dout(name, shape):
        return nc.dram_tensor(name, list(shape), F32, kind="ExternalOutput").ap()

    xp = din("xp", [SEQ, D]); xs = din("xs", [TS, D])
    ck = din("ck", [DEPTH, PAST, 256]); cv = din("cv", [DEPTH, PAST, 256]); cki = din("cki", [DEPTH, PAST, 32])
    sconvT = din("sconvT", [DEPTH, 128, 2, 2])
    w_in = din("w_in", [DEPTH, D, INW]); w_out = din("w_out", [DEPTH, D, D])
    w_up = din("w_up", [DEPTH, D, 4 * D]); w_down = din("w_down", [DEPTH, 4 * D, D])
    gvec = din("gvec", [128, 32]); wsT = din("wsT", [DEPTH, 128, 4, 128]); bs = din("bs", [DEPTH, 4, 128])
    cwv = din("cwv", [128, 12]); qg = din("qg", [DEPTH, 64]); kg = din("kg", [DEPTH, 64])
    rope = din("rope", [128, NT, 96]); p2 = din("p2", [1, 32])
    yp = dout("yp", [SEQ, D]); ys = dout("ys", [TS, D])
    nkp = dout("nkp", [DEPTH, SEQ, 256]); nvp = dout("nvp", [DEPTH, SEQ, 256]); nkip = dout("nkip", [DEPTH, SEQ, 32])
    ncp = dout("ncp", [DEPTH, 128, 2, 2])
    nks = dout("nks", [DEPTH, TS, 256]); nvs = dout("nvs", [DEPTH, TS, 256]); nkis = dout("nkis", [DEPTH, TS, 32])
    ncs = dout("ncs", [DEPTH, 128, 2, 2]); nsgv = dout("nsgv", [DEPTH, TS, 256])

    st = contextlib.ExitStack()
    cnt = [0]

    def sb(shape, dt=F32, name=None):
        cnt[0] += 1
        return st.enter_context(nc.sbuf_tensor(name or f"sb{cnt[0]}", list(shape), dt))

    def ps(shape, dt=F32):
        cnt[0] += 1
        return st.enter_context(nc.psum_tensor(f"ps{cnt[0]}", list(shape), dt))

    X = sb([128, NT, D], F32, "X")
    WA = sb([128, 8 * INW], BF16, "WA")
    WB = sb([128, 8 * D], BF16, "WB")
    AR = sb([128, 21504], BF16, "AR")
    SA = sb([128, 1024], F32, "SA"); SB_ = sb([128, 1024], F32, "SBs"); SC = sb([128, 1024], F32, "SC")
    hbf = sb([128, 1024], BF16, "hbf")
    hT = sb([128, 8, 128], BF16, "hT")
    P12 = sb([128, 2304], BF16, "P12")
    qkr = P12[:, 0:1536].bitcast(F32); qkbf = P12[:, 1536:2304]
    v32 = SB_[:, 768:1024]; vbf = sb([128, 256], BF16, "vbf"); sgv = SC[:, 768:1024]
    qki = sb([128, 288], F32, "qki"); qkir = sb([128, 288], F32, "qkir"); qibf = sb([128, 256], BF16, "qibf")
    ki4 = sb([128, 128], BF16, "ki4")
    w8 = sb([128, 8], F32, "w8")
    xpad = sb([128, 2, 130], F32, "xpad"); hin = sb([128, 2, 128], F32, "hin"); cacc = sb([128, 2, 128], F32, "cacc")
    ccT = [sb([128, 8, 128], BF16, f"concatT{i}") for i in range(2)]
    qTzs = [sb([128, 2, 2, 2, 128], BF16, f"qTz{i}") for i in range(2)]; qiT = sb([128, 2, 128], BF16, "qiT")
    rt = [sb([128, 512], F32, f"rt{i}") for i in range(2)]
    pT = [sb([128, 256], BF16, f"pT{i}") for i in range(2)]
    rec = sb([128, 256], F32, "rec")
    accs = SC[:, 0:256].rearrange("p (g c) -> p g c", g=4)
    actT0 = P12[:, 0:2048].rearrange("p (m s) -> p m s", m=4)
    actT = [actT0, actT0]
    ident = sb([128, 128], BF16, "ident")
    ropeB = [sb([128, 96], F32, f"ropeB{i}") for i in range(2)]
    gv = sb([128, 32], F32, "gv"); cw = sb([128, 12], F32, "cw")
    wsf = SC[:, 0:512]; wmT = sb([128, 4, 128], BF16, "wmT")
    bP = sb([128, 2, 128], F32, "bP"); gqk = sb([128, 128], F32, "gqk")
    p2b = sb([128, 32], F32, "p2b"); nh = sb([128, 12], F32, "nh")
    sm = sb([128, 64], F32, "sm")
    steps = sb([128, 32], F32, "steps"); nsteps = sb([128, 32], F32, "nsteps")
    KTn = sb([128, 2, 32], BF16, "KTn"); Vn = sb([128, 448], BF16, "Vn"); kiTn = sb([128, 32], BF16, "kiTn")
    kstage = SB_[:, :].bitcast(BF16).rearrange("p (k c) -> p k c", k=8)
    SAb = SA[:, :].bitcast(BF16)
    ki4s = SAb[:, 0:1024].rearrange("p (k c) -> p k c", k=8)
    kistage = SAb[:, 1024:1280].rearrange("p (k c) -> p k c", k=8)

    KT = AR[:, 0:4096].rearrange("p (g s) -> p g s", g=2)
    Vb = AR[:, 4096:11264].rearrange("p (k c) -> p k c", k=16)
    kiT = AR[:, 11264:13312]
    isc = AR[:, 13312:17408].bitcast(F32)
    maskbs = [AR[:, 17408:19456], AR[:, 19456:21504]]
    isc_s = AR[:, 0:8256].bitcast(F32)
    maskb_s = AR[:, 8256:12384]
    KT_s = AR[:, 12384:14432].rearrange("p (g s) -> p g s", g=2)
    Vb_s = AR[:, 14432:18016].rearrange("p (k c) -> p k c", k=8)
    kiT_s = AR[:, 18016:19040]
    hT2 = AR[:, 0:16640].rearrange("p (k s) -> p k s", k=8)

    win = WA[:, :].rearrange("p (k n) -> p k n", k=8)
    wout = WB[:, :].rearrange("p (k n) -> p k n", k=8)
    FSL = [(WA, 0), (WA, 8192), (WB, 0)]

    PB = [ps([128, 512]) for _ in range(6)]
    PT = ps([128, 1024], BF16)
    PB7 = ps([128, 512])
    bPB = [Buf(f"PB{i}", xr=True) for i in range(6)]
    bPT = Buf("PT", xr=True)
    bPB7 = Buf("PB7", xr=True)

    bX = [Buf(f"X{t}") for t in range(NT)]
    bWA, bWB = Buf("WA"), Buf("WB")
    bF = [Buf("FA"), Buf("FB"), Buf("FC")]
    bSA, bSB, bSC = Buf("SA"), Buf("SB"), Buf("SC")
    bhbf, bhT = Buf("hbf"), Buf("hT")
    bqkr, bqkbf, bv32, bvbf = Buf(), Buf(), Buf(), Buf()
    bsgv = bSC
    bqki, bqkir, bqibf, bki4, bw8 = Buf(), Buf(), Buf(), Buf(), Buf()
    bxpad, bhin, bcacc = Buf("xpad"), Buf(), Buf()
    bccs = [[Buf(f"cc{i}_{c}") for c in range(8)] for i in range(2)]
    bqTs = [Buf("qT0"), Buf("qT1")]
    bqiT = Buf("qiT")
    brt = [Buf(), Buf()]
    bpT = [Buf(), Buf()]
    brec = Buf()
    baccs = bSC
    _ba = Buf()
    bactT = [_ba, _ba]
    bident, bgv, bcw, bwmT, bbP, bgqk, bp2b, bnh = [Buf() for _ in range(8)]
    bwsf = bSC
    bropeB = [Buf(), Buf()]
    bss, brstd, bss12, br12, brmax, brmin, bw0, bmid, bcnt, btq, bthr, bsteps = [Buf() for _ in range(12)]
    bKT = [Buf(f"KT{t}") for t in range(16)]; bVb = [Buf(f"Vb{t}") for t in range(16)]; bkiT = [Buf(f"kiT{t}") for t in range(16)]
    bisc = Buf("isc"); bmaskbs = [Buf("maskb0"), Buf("maskb1")]
    bisc_s, bmaskb_s, bKT_s, bVb_s, bkiT_s = Buf(), Buf(), Buf(), Buf(), Buf()
    bKTn, bVn, bkiTn = Buf(), Buf(), Buf()
    bkstage, bkistage, bki4s = bSB, bSA, bSA
    bhT2 = [Buf(f"hT2_{t}") for t in range(NT)]
    PROMPT_AR = bKT + bVb + bkiT + [bisc] + bmaskbs
    SAMPLE_AR = [bisc_s, bmaskb_s, bKT_s, bVb_s, bkiT_s]

    ss = sm[:, 0:1]; msq = sm[:, 1:2]; rstd = sm[:, 2:3]
    ss12 = sm[:, 4:16]; ms12 = sm[:, 16:28]; r12 = sm[:, 28:40]
    rmax = sm[:, 40:41]; rmin = sm[:, 41:42]; w0 = sm[:, 42:43]; mid = sm[:, 43:44]
    cntc = sm[:, 44:45]; tq = sm[:, 45:46]; thr = sm[:, 46:47]

    add = S.add

    def nrows(t):
        return 128 if t < 16 else TS

    add("pool", lambda e: e.memset(ident[:], 0.0), writes=[bident])
    add("pool", lambda e: e.affine_select(out=ident[:], in_=ident[:], pattern=[[-1, 128]], compare_op=ALU.not_equal,
                                          fill=1.0, base=0, channel_multiplier=1), reads=[bident], writes=[bident])
    add("pool", lambda e: e.memset(nh[:], -0.5), writes=[bnh])
    for i_ in range(2):
        add("pool", (lambda i_: lambda e: e.memset(qTzs[i_][:].rearrange("p a b c d -> p (a b c d)"), 0.0))(i_), writes=[bqTs[i_]])
    add("sp", lambda e: e.dma_start(out=gv[:], in_=gvec), writes=[bgv], dma=True)
    add("sp", lambda e: e.dma_start(out=cw[:], in_=cwv), writes=[bcw], dma=True)
    add("sp", lambda e: e.dma_start(out=p2b[:], in_=p2.partition_broadcast(128)), writes=[bp2b], dma=True)
    for t in range(16):
        add("sp", (lambda t: lambda e: e.dma_start(out=X[:, t, :], in_=xp[t * 128:(t + 1) * 128, :]))(t), writes=[bX[t]], dma=True)
    add("sp", lambda e: e.dma_start(out=X[0:TS, 16, :], in_=xs), writes=[bX[16]], dma=True)

    def load_mix_weights(l):
        src = w_in[l].rearrange("(k p) n -> p k n", p=128)
        add("pool", lambda e: e.dma_start(out=win[:, :, 0:1300], in_=src[:, :, 0:1300]), writes=[bWA], dma=True)
        add("pool", lambda e: e.dma_start(out=win[:, :, 1300:2600], in_=src[:, :, 1300:2600]), writes=[bWA], dma=True)
        for k in range(8):
            add("dve", (lambda k: lambda e: e.tensor_scalar(out=win[:, k, :], in0=win[:, k, :], scalar1=gv[:, l * 8 + k:l * 8 + k + 1],
                                                            scalar2=None, op0=ALU.mult))(k), reads=[bWA, bgv], writes=[bWA])
        src2 = w_out[l].rearrange("(k p) n -> p k n", p=128)
        add("pool", lambda e: e.dma_start(out=wout[:, :, :], in_=src2), writes=[bWB], dma=True)

    def ffn_views(slot):
        tns, off = FSL[slot]
        wu = tns[:, off:off + 4096].rearrange("p (k n) -> p k n", k=8)
        wd = tns[:, off + 4096:off + 8192].rearrange("p (m n) -> p m n", m=4)
        return wu, wd

    def load_ffn_dma(l, f, slot):
        wu, wd = ffn_views(slot)
        srcu = w_up[l].rearrange("(k p) n -> p k n", p=128)[:, :, f * 512:(f + 1) * 512]
        srcd = w_down[l, f * 512:(f + 1) * 512, :].rearrange("(m p) n -> p m n", p=128)
        b = bF[slot]
        add("pool", lambda e: e.dma_start(out=wu, in_=srcu), writes=[b], dma=True)
        add("pool", lambda e: e.dma_start(out=wd, in_=srcd), writes=[b], dma=True)

    def fold_ffn(l, f, slot):
        wu, wd = ffn_views(slot)
        b = bF[slot]
        for k in range(8):
            add("dve", (lambda k: lambda e: e.tensor_scalar(out=wu[:, k, :], in0=wu[:, k, :],
                                                            scalar1=gv[:, 16 + l * 8 + k:16 + l * 8 + k + 1], scalar2=None,
                                                            op0=ALU.mult))(k), reads=[b, bgv], writes=[b])

    def load_ffn_chunk(l, f, slot):
        load_ffn_dma(l, f, slot)
        fold_ffn(l, f, slot)

    PREF = {}

    def load_layer_params(l):
        add("sp", lambda e: e.dma_start(out=wsf.rearrange("p (h i) -> p h i", h=4), in_=wsT[l]), writes=[bwsf], dma=True)
        add("dve", lambda e: e.tensor_copy(out=wmT[:].rearrange("p h i -> p (h i)"), in_=wsf), reads=[bwsf], writes=[bwmT])
        add("dve", lambda e: e.memset(wmT[64:128, :, 0:64], 0.0), writes=[bwmT])
        for c in range(2):
            for hh in range(2):
                add("sp", (lambda c, hh: lambda e: e.dma_start(out=bP[hh * 64:(hh + 1) * 64, c, :],
                                                               in_=bs[l, 2 * c + hh:2 * c + hh + 1, :].partition_broadcast(64)))(c, hh),
                    writes=[bbP], dma=True)
        add("sp", lambda e: e.dma_start(out=gqk[:, 0:64], in_=qg[l:l + 1, :].partition_broadcast(128)), writes=[bgqk], dma=True)
        add("sp", lambda e: e.dma_start(out=gqk[:, 64:128], in_=kg[l:l + 1, :].partition_broadcast(128)), writes=[bgqk], dma=True)

    def norm_to_hT(t, dstT, bdst, col0):
        nt = nrows(t)
        xt = X[0:nt, t, :]
        add("act", lambda e: e.activation(out=hbf[0:nt, :], in_=xt, func=AF.Square, accum_out=ss[0:nt, :]),
            reads=[bX[t]], writes=[bhbf, bss])
        add("dve", lambda e: e.tensor_scalar(out=msq[0:nt, :], in0=ss[0:nt, :], scalar1=1.0 / D, scalar2=EPS, op0=ALU.mult, op1=ALU.add),
            reads=[bss], writes=[brstd])
        add("pool", lambda e: e.tensor_tensor(out=rstd[0:nt, :], in0=msq[0:nt, :], in1=nh[0:nt, 0:1], op=ALU.pow),
            reads=[brstd, bnh], writes=[brstd])
        add("act", lambda e: e.activation(out=hbf[0:nt, :], in_=xt, func=AF.Copy, scale=rstd[0:nt, :]),
            reads=[bX[t], brstd], writes=[bhbf])

        def tr(e):
            ins = None
            for k in range(8):
                ins = e.transpose(PT[:, k * 128:k * 128 + nt], hbf[0:nt, k * 128:(k + 1) * 128], ident[0:nt, 0:nt])
            return ins
        add("pe", tr, reads=[bhbf, bident], writes=[bPT])
        add("dve", lambda e: e.tensor_copy(out=dstT[:, :, col0:col0 + nt],
                                           in_=PT[:, :].rearrange("p (k s) -> p k s", k=8)[:, :, 0:nt]),
            reads=[bPT], writes=[bdst])

    def rope_ops(src, dst, nt, nh_, half, cos, sin, bsrc, bdst, brope):
        s3 = src.rearrange("p (h d) -> p h d", h=nh_)
        d3 = dst.rearrange("p (h d) -> p h d", h=nh_)
        x1 = s3[:, :, 0:half]; x2 = s3[:, :, half:2 * half]
        cb = cos.unsqueeze(1).to_broadcast([nt, nh_, half])
        sbb = sin.unsqueeze(1).to_broadcast([nt, nh_, half])
        ta = SA[0:nt, 0:nh_ * half].rearrange("p (h d) -> p h d", h=nh_)
        tb = SA[0:nt, 512:512 + nh_ * half].rearrange("p (h d) -> p h d", h=nh_)
        add("dve", lambda e: e.tensor_tensor(out=ta, in0=x1, in1=cb, op=ALU.mult), reads=[bsrc, brope], writes=[bSA])
        add("dve", lambda e: e.tensor_tensor(out=tb, in0=x2, in1=sbb, op=ALU.mult), reads=[bsrc, brope], writes=[bSA])
        add("dve", lambda e: e.tensor_tensor(out=d3[:, :, 0:half], in0=ta, in1=tb, op=ALU.subtract), reads=[bSA], writes=[bdst])
        add("dve", lambda e: e.tensor_tensor(out=ta, in0=x2, in1=cb, op=ALU.mult), reads=[bsrc, brope], writes=[bSA])
        add("dve", lambda e: e.tensor_tensor(out=tb, in0=x1, in1=sbb, op=ALU.mult), reads=[bsrc, brope], writes=[bSA])
        add("dve", lambda e: e.tensor_tensor(out=d3[:, :, half:2 * half], in0=ta, in1=tb, op=ALU.add), reads=[bSA], writes=[bdst])

    def gelu_core(src, T, bsrc_list, bT):
        add("act", lambda e: e.activation(out=T, in_=src, func=AF.Square), reads=bsrc_list, writes=[bT])
        add("dve", lambda e: e.tensor_scalar(out=T, in0=T, scalar1=0.044715, scalar2=1.0, op0=ALU.mult, op1=ALU.add), reads=[bT], writes=[bT])
        add("dve", lambda e: e.tensor_tensor(out=T, in0=T, in1=src, op=ALU.mult), reads=[bT] + bsrc_list, writes=[bT])
        add("act", lambda e: e.activation(out=T, in_=T, func=AF.Tanh, scale=GC), reads=[bT], writes=[bT])
        add("dve", lambda e: e.scalar_tensor_tensor(out=T, in0=T, scalar=1.0, in1=src, op0=ALU.add, op1=ALU.mult),
            reads=[bT] + bsrc_list, writes=[bT])

    VOFF = [0, 128, 192, 320]
    O_ROWS = [64, 0, 64, 0]

    def attn_block(nt, g, tiles, KTsrc, Vsrc, mask_ap, bk_fn, first, last_norm, bmask, qTz, bqT, concatT, bcc):
        gp, gl = g // 2, g % 2
        scb = PB[4]
        bscb = bPB[4]
        PO = PB[5]
        n2 = 2 * nt
        ntiles = len(tiles)
        for i, (kt, ks, mc0) in enumerate(tiles):
            half = (i % 2) * 256
            pti = i % 2

            def sc_fn(e, kt=kt, ks=ks, mc0=mc0, half=half):
                e.matmul(scb[0:ks, half:half + n2].rearrange("p (j t) -> p j t", j=2),
                         lhsT=KTsrc[:, gp, kt * 128:kt * 128 + ks],
                         rhs=qTz[:, gl, gp, :, 0:nt], start=True, stop=False)
                ins = None
                for j in range(2):
                    ins = e.matmul(scb[0:ks, half + j * nt:half + (j + 1) * nt], lhsT=mask_ap[0:nt, mc0:mc0 + ks],
                                   rhs=ident[0:nt, 0:nt], start=False, stop=(j == 1))
                return ins
            add("pe", sc_fn, reads=[bqT, bmask, bident] + bk_fn(kt), writes=[bscb])
            add("act", (lambda ks, half, pti: lambda e: e.activation(out=pT[pti][0:ks, 0:n2], in_=scb[0:ks, half:half + n2],
                                                                      func=AF.Exp, scale=0.125))(ks, half, pti),
                reads=[bscb], writes=[bpT[pti]])
            add("pe", (lambda kt, ks, pti, i: lambda e: e.matmul(PO[:, 0:n2], lhsT=Vsrc[0:ks, kt, VOFF[g]:VOFF[g] + 128],
                                                                 rhs=pT[pti][0:ks, 0:n2], start=(i == 0), stop=(i == ntiles - 1)))(kt, ks, pti, i),
                reads=[bpT[pti]] + bk_fn(kt), writes=[bPB[5]])
            yield
        if last_norm and first:
            src_all = PO
            bsrc = bPB[5]
        else:
            if first:
                add("act", lambda e: e.activation(out=accs[:, g, 0:n2], in_=PO[:, 0:n2], func=AF.Copy), reads=[bPB[5]], writes=[baccs])
            else:
                add("dve", lambda e: e.tensor_tensor(out=accs[:, g, 0:n2], in0=accs[:, g, 0:n2], in1=PO[:, 0:n2], op=ALU.add),
                    reads=[bPB[5], baccs], writes=[baccs])
            src_all = accs[:, g, :]
            bsrc = baccs
        if last_norm:
            orow = O_ROWS[g]
            srow = 64 - orow
            add("dve", lambda e: e.reciprocal(out=rec[orow:orow + 64, 0:n2], in_=src_all[srow:srow + 64, 0:n2]), reads=[bsrc], writes=[brec])
            for j in range(2):
                add("dve", (lambda j: lambda e: e.tensor_tensor(out=concatT[64 * j:64 * j + 64, 4 + g, 0:nt],
                                                                in0=src_all[orow:orow + 64, j * nt:(j + 1) * nt],
                                                                in1=rec[orow:orow + 64, j * nt:(j + 1) * nt], op=ALU.mult))(j),
                    reads=[bsrc, brec], writes=[bcc[4 + g]])

    def indexer_chunk(nt, n0, w, kiTsrc, kcol0, isc_ap, bki_list, bisc_):
        for h in range(8):
            r, c = h % 4, h // 4
            add("pe", (lambda r, c: lambda e: e.matmul(PB[r][0:nt, 0:w], lhsT=qiT[32 * r:32 * r + 32, c, 0:nt],
                                                       rhs=kiTsrc[32 * r:32 * r + 32, kcol0:kcol0 + w], start=True, stop=True,
                                                       tile_position=(32 * r, 0)))(r, c),
                reads=[bqiT] + bki_list, writes=[bPB[r]])
            add("act", (lambda r, h: lambda e: e.activation(out=rt[h % 2][0:nt, 0:w], in_=PB[r][0:nt, 0:w], func=AF.Relu))(r, h),
                reads=[bPB[r]], writes=[brt[h % 2]])
            if h == 0:
                add("dve", lambda e: e.tensor_scalar(out=isc_ap[0:nt, n0:n0 + w], in0=rt[0][0:nt, 0:w], scalar1=w8[0:nt, 0:1],
                                                     scalar2=None, op0=ALU.mult), reads=[brt[0], bw8], writes=[bisc_])
            else:
                add("dve", (lambda h: lambda e: e.scalar_tensor_tensor(out=isc_ap[0:nt, n0:n0 + w], in0=rt[h % 2][0:nt, 0:w],
                                                                       scalar=w8[0:nt, h:h + 1], in1=isc_ap[0:nt, n0:n0 + w],
                                                                       op0=ALU.mult, op1=ALU.add))(h),
                    reads=[brt[h % 2], bw8, bisc_], writes=[bisc_])
            yield

    def topk_mask(nt, Stot, isc_ap, mask_ap, bisc_, bmask, inadm=None):
        iv = isc_ap[0:nt, 0:Stot]
        yield
        if Stot > 256:
            add("dve", lambda e: e.tensor_reduce(out=rmax[0:nt, :], in_=iv, axis=AX.X, op=ALU.max), reads=[bisc_], writes=[brmax])
            add("dve", lambda e: e.tensor_reduce(out=rmin[0:nt, :], in_=iv, axis=AX.X, op=ALU.min), reads=[bisc_], writes=[brmin])
        if inadm is not None:
            add("dve", lambda e: e.memset(isc_ap[0:64, inadm:inadm + 64], NEG), writes=[bisc_])
        if Stot <= 256:
            add("dve", lambda e: e.memset(thr[0:nt, :], 0.5 * NEG), writes=[bthr])
        else:
            add("dve", lambda e: e.tensor_tensor(out=w0[0:nt, :], in0=rmax[0:nt, :], in1=rmin[0:nt, :], op=ALU.subtract),
                reads=[brmax, brmin], writes=[bw0])
            add("dve", lambda e: e.tensor_scalar(out=steps[0:nt, :], in0=p2b[0:nt, :], scalar1=w0[0:nt, :], scalar2=None, op0=ALU.mult),
                reads=[bp2b, bw0], writes=[bsteps])
            add("dve", lambda e: e.tensor_scalar(out=nsteps[0:nt, :], in0=steps[0:nt, :], scalar1=-1.0, scalar2=None, op0=ALU.mult),
                reads=[bsteps], writes=[bsteps])
            add("dve", lambda e: e.tensor_scalar(out=mid[0:nt, :], in0=rmin[0:nt, :], scalar1=steps[0:nt, 0:1], scalar2=-1.0,
                                                 op0=ALU.add, op1=ALU.mult), reads=[brmin, bsteps], writes=[bmid])
            for n in range(NIT):
                add("act", lambda e: e.activation(out=mask_ap[0:nt, 0:Stot], in_=iv, func=AF.Sign, bias=mid[0:nt, :],
                                                  accum_out=cntc[0:nt, :]), reads=[bisc_, bmid], writes=[bmask, bcnt])
                add("act", lambda e: e.activation(out=tq[0:nt, :], in_=cntc[0:nt, :], func=AF.Sign, bias=float(Stot - 511)),
                    reads=[bcnt], writes=[btq])
                add("act", (lambda n: lambda e: e.activation(out=mid[0:nt, :], in_=tq[0:nt, :], func=AF.Identity,
                                                             scale=nsteps[0:nt, n + 1:n + 2], bias=mid[0:nt, :]))(n),
                    reads=[btq, bsteps, bmid], writes=[bmid])
                yield
            add("dve", lambda e: e.tensor_scalar(out=thr[0:nt, :], in0=mid[0:nt, :], scalar1=-1.0, scalar2=steps[0:nt, NIT:NIT + 1],
                                                 op0=ALU.mult, op1=ALU.subtract), reads=[bmid, bsteps], writes=[bthr])
        add("dve", lambda e: e.tensor_scalar(out=mask_ap[0:nt, 0:Stot], in0=iv, scalar1=thr[0:nt, :], scalar2=-30000.0,
                                             op0=ALU.is_lt, op1=ALU.mult), reads=[bisc_, bthr], writes=[bmask])

    def mixer_tile(l, t, part="AB"):
        nt = nrows(t)
        sample = (t == 16)
        par = t % 2
        qTz, bqT, concatT, bcc = qTzs[par], bqTs[par], ccT[par], bccs[par]
        maskb, bmaskb = maskbs[par], bmaskbs[par]
        if "A" in part:
            yield from _mixer_A(l, t, nt, sample, qTz, bqT, concatT, bcc, maskb, bmaskb)
        if "B" in part:
            yield from _mixer_B(l, t, nt, sample, qTz, bqT, concatT, bcc, maskb, bmaskb)

    def _mixer_A(l, t, nt, sample, qTz, bqT, concatT, bcc, maskb, bmaskb):
        rT = ropeB[t % 2]; brT = bropeB[t % 2]
        add("sp", lambda e: e.dma_start(out=rT[:, :], in_=rope[:, t, :]), writes=[brT], dma=True)
        norm_to_hT(t, hT, bhT, 0)
        if DBG["stage"] < 2:
            return
        tm = [(0, 512), (512, 512), (1024, 512), (1536, 40)]
        for n, (c0, w) in enumerate(tm):
            if not DBG["tm"]:
                break

            def f(e, c0=c0, w=w, n=n):
                ins = None
                for k in range(8):
                    ins = e.matmul(PB[n][0:nt, 0:w], lhsT=hT[:, k, 0:nt], rhs=win[:, k, c0:c0 + w], start=(k == 0), stop=(k == 7))
                return ins
            add("pe", f, reads=[bhT, bWA], writes=[bPB[n]])
        for fi in range(8):
            if not DBG["fm"]:
                break
            bank = PB[4] if fi < 4 else PB[5]
            bb = bPB[4] if fi < 4 else bPB[5]

            def f(e, fi=fi, bank=bank):
                ins = None
                for k in range(8):
                    ins = e.matmul(bank[:, (fi % 4) * 128:(fi % 4) * 128 + nt], lhsT=win[:, k, TMW + fi * 128:TMW + (fi + 1) * 128],
                                   rhs=hT[:, k, 0:nt], start=(k == 0), stop=(k == 7))
                return ins
            add("pe", f, reads=[bhT, bWA], writes=[bb])
        PF0 = PB[4][:, :].rearrange("p (f s) -> p f s", f=4)
        PF1 = PB[5][:, :].rearrange("p (f s) -> p f s", f=4)
        if DBG["stage"] < 3:
            return
        add("act", lambda e: e.activation(out=SB_[0:nt, 0:512], in_=PB[0][0:nt, 0:512], func=AF.Square), reads=[bPB[0]], writes=[bSB])
        add("act", lambda e: e.activation(out=SB_[0:nt, 512:768], in_=PB[1][0:nt, 0:256], func=AF.Square), reads=[bPB[1]], writes=[bSB])
        add("dve", lambda e: e.tensor_reduce(out=ss12[0:nt, :], in_=SB_[0:nt, 0:768].rearrange("p (h d) -> p h d", h=12), axis=AX.X, op=ALU.add),
            reads=[bSB], writes=[bss12])
        add("dve", lambda e: e.tensor_scalar(out=ms12[0:nt, :], in0=ss12[0:nt, :], scalar1=1.0 / 64, scalar2=EPS, op0=ALU.mult, op1=ALU.add),
            reads=[bss12], writes=[bss12])
        add("pool", lambda e: e.tensor_tensor(out=r12[0:nt, :], in0=ms12[0:nt, :], in1=nh[0:nt, :], op=ALU.pow), reads=[bss12, bnh], writes=[br12])
        add("dve", lambda e: e.tensor_tensor(out=SB_[0:nt, 0:512].rearrange("p (h d) -> p h d", h=8),
                                             in0=PB[0][0:nt, 0:512].rearrange("p (h d) -> p h d", h=8),
                                             in1=r12[0:nt, 0:8].unsqueeze(2).to_broadcast([nt, 8, 64]), op=ALU.mult),
            reads=[bPB[0], br12], writes=[bSB])
        add("dve", lambda e: e.tensor_tensor(out=SB_[0:nt, 512:768].rearrange("p (h d) -> p h d", h=4),
                                             in0=PB[1][0:nt, 0:256].rearrange("p (h d) -> p h d", h=4),
                                             in1=r12[0:nt, 8:12].unsqueeze(2).to_broadcast([nt, 4, 64]), op=ALU.mult),
            reads=[bPB[1], br12], writes=[bSB])
        add("dve", lambda e: e.tensor_tensor(out=SB_[0:nt, 0:512].rearrange("p (h d) -> p h d", h=8),
                                             in0=SB_[0:nt, 0:512].rearrange("p (h d) -> p h d", h=8),
                                             in1=gqk[0:nt, 0:64].unsqueeze(1).to_broadcast([nt, 8, 64]), op=ALU.mult),
            reads=[bSB, bgqk], writes=[bSB])
        add("dve", lambda e: e.tensor_tensor(out=SB_[0:nt, 512:768].rearrange("p (h d) -> p h d", h=4),
                                             in0=SB_[0:nt, 512:768].rearrange("p (h d) -> p h d", h=4),
                                             in1=gqk[0:nt, 64:128].unsqueeze(1).to_broadcast([nt, 4, 64]), op=ALU.mult),
            reads=[bSB, bgqk], writes=[bSB])
        if DBG["stage"] < 3.1:
            return
        rope_ops(SB_[0:nt, 0:768], qkr[0:nt, :], nt, 12, 32, rT[0:nt, 0:32], rT[0:nt, 32:64], bSB, bqkr, brT)
        if DBG["stage"] < 3.2:
            return
        add("act", lambda e: e.activation(out=v32[0:nt, :], in_=PB[1][0:nt, 256:512], func=AF.Copy), reads=[bPB[1]], writes=[bv32])
        if not sample:
            Vdst = Vb[0:nt, t, 64:448].rearrange("p (a b) -> p a b", a=2)[:, :, 0:128]
            bVdst = bVb[t]
        else:
            Vdst = Vn[0:nt, 64:448].rearrange("p (a b) -> p a b", a=2)[:, :, 0:128]
            bVdst = bVn
        add("dve", lambda e: e.tensor_copy(out=Vdst, in_=PB[1][0:nt, 256:512].rearrange("p (a b) -> p a b", a=2)), reads=[bPB[1]], writes=[bVdst])
        if DBG["stage"] < 3.3:
            return
        add("act", lambda e: e.activation(out=qki[0:nt, 0:256], in_=PB[2][0:nt, 256:512], func=AF.Copy), reads=[bPB[2]], writes=[bqki])
        add("act", lambda e: e.activation(out=qki[0:nt, 256:288], in_=PB[3][0:nt, 0:32], func=AF.Copy), reads=[bPB[3]], writes=[bqki])
        add("act", lambda e: e.activation(out=w8[0:nt, :], in_=PB[3][0:nt, 32:40], func=AF.Copy, scale=0.0625), reads=[bPB[3]], writes=[bw8])
        if DBG["stage"] < 3.4:
            return
        TV = SC[0:nt, 0:256]
        gelu_core(PB[2][0:nt, 0:256], TV, [bPB[2]], bSC)
        add("act", lambda e: e.activation(out=vbf[0:nt, :], in_=TV, func=AF.Copy, scale=0.5), reads=[bSC], writes=[bvbf])
        if sample:
            add("act", lambda e: e.activation(out=sgv[0:nt, :], in_=TV, func=AF.Copy, scale=0.5), reads=[bSC], writes=[bsgv])
            add("sp", lambda e: e.dma_start(out=nsgv[l], in_=sgv[0:nt, :]), reads=[bsgv], dma=True)
        if DBG["stage"] < 3.5:
            return
        rope_ops(qki[0:nt, :], qkir[0:nt, :], nt, 9, 16, rT[0:nt, 64:80], rT[0:nt, 80:96], bqki, bqkir, brT)
        if DBG["stage"] < 3.6:
            return
        if not sample:
            row0 = t * 128
            add("sp", lambda e: e.dma_start(out=nkp[l, row0:row0 + nt, :], in_=qkr[0:nt, 512:768]), reads=[bqkr], dma=True)
            add("sp", lambda e: e.dma_start(out=nvp[l, row0:row0 + nt, :], in_=v32[0:nt, :]), reads=[bv32], dma=True)
            add("sp", lambda e: e.dma_start(out=nkip[l, row0:row0 + nt, :], in_=qkir[0:nt, 256:288]), reads=[bqkir], dma=True)
        else:
            add("sp", lambda e: e.dma_start(out=nks[l], in_=qkr[0:nt, 512:768]), reads=[bqkr], dma=True)
            add("sp", lambda e: e.dma_start(out=nvs[l], in_=v32[0:nt, :]), reads=[bv32], dma=True)
            add("sp", lambda e: e.dma_start(out=nkis[l], in_=qkir[0:nt, 256:288]), reads=[bqkir], dma=True)
        if DBG["stage"] < 3.7:
            return
        add("act", lambda e: e.activation(out=qkbf[0:nt, :], in_=qkr[0:nt, :], func=AF.Copy), reads=[bqkr], writes=[bqkbf])

        def trq(e):
            ins = None
            for b_ in range(6):
                ins = e.transpose(PT[:, b_ * 128:b_ * 128 + nt], qkbf[0:nt, b_ * 128:(b_ + 1) * 128], ident[0:nt, 0:nt])
            return ins
        add("pe", trq, reads=[bqkbf, bident], writes=[bPT])
        PT3 = PT[:, :].rearrange("p (k s) -> p k s", k=8)
        for gl in range(2):
            if DBG.get("skip") == "qTz" or (DBG.get("skip") == "qTz1" and gl == 1):
                continue
            add("dve", (lambda gl: lambda e: e.tensor_copy(
                out=qTz[64 * gl:64 * gl + 64, gl, :, :, 0:nt],
                in_=PT3[64 * gl:64 * gl + 64, 0:4, 0:nt].rearrange("p (a b) s -> p a b s", a=2)))(gl), reads=[bPT], writes=[bqT])
        if DBG.get("skip") == "KT":
            return
        if not sample:
            add("dve", lambda e: e.tensor_copy(out=KT[:, :, t * 128:t * 128 + nt], in_=PT3[:, 4:6, 0:nt]), reads=[bPT], writes=[bKT[t]])
        else:
            add("dve", lambda e: e.tensor_copy(out=KTn[:, :, 0:nt], in_=PT3[:, 4:6, 0:nt]), reads=[bPT], writes=[bKTn])
        if DBG["stage"] < 3.8:
            return
        if DBG.get("skip") == "qibf":
            return
        add("act", lambda e: e.activation(out=qibf[0:nt, :], in_=qkir[0:nt, 0:256], func=AF.Copy), reads=[bqkir], writes=[bqibf])
        if DBG.get("skip") == "ki4":
            return
        add("dve", lambda e: e.tensor_copy(out=ki4[0:nt, :].rearrange("p (r d) -> p r d", r=4),
                                           in_=qkir[0:nt, 256:288].unsqueeze(1).to_broadcast([nt, 4, 32])), reads=[bqkir], writes=[bki4])

        def tri(e):
            e.transpose(PT[:, 0:nt], qibf[0:nt, 0:128], ident[0:nt, 0:nt])
            e.transpose(PT[:, 128:128 + nt], qibf[0:nt, 128:256], ident[0:nt, 0:nt])
            return e.transpose(PT[:, 256:256 + nt], ki4[0:nt, :], ident[0:nt, 0:nt])
        if DBG.get("skip") == "tri":
            return
        add("pe", tri, reads=[bqibf, bki4, bident], writes=[bPT])
        if DBG.get("skip") == "qiT":
            return
        add("dve", lambda e: e.tensor_copy(out=qiT[:, :, 0:nt], in_=PT3[:, 0:2, 0:nt]), reads=[bPT], writes=[bqiT])
        if DBG.get("skip") == "kiT":
            return
        if not sample:
            add("dve", lambda e: e.tensor_copy(out=kiT[:, t * 128:t * 128 + nt], in_=PT[:, 256:256 + nt]), reads=[bPT], writes=[bkiT[t]])
        else:
            add("dve", lambda e: e.tensor_copy(out=kiTn[:, 0:nt], in_=PT[:, 256:256 + nt]), reads=[bPT], writes=[bkiTn])

        if DBG["stage"] < 4:
            return
        TU = SC[:, 256:512].rearrange("p (c s) -> p c s", c=2)[:, :, 0:nt]
        gelu_core(PF0[:, 0:2, 0:nt], TU, [bPB[4]], bSC)
        for c in range(2):
            for hh in range(2):
                h = 2 * c + hh
                add("pe", (lambda c, h: lambda e: e.matmul(PB7[:, h * 128:h * 128 + nt], lhsT=vbf[0:nt, c * 128:(c + 1) * 128],
                                                           rhs=wmT[0:nt, h, 0:nt], start=True, stop=True))(c, h),
                    reads=[bvbf, bwmT], writes=[bPB7])
        for c in range(2):
            for hh in range(2):
                h = 2 * c + hh
                r0 = hh * 64
                tmpv = SC[r0:r0 + 64, 512 + h * 128:512 + h * 128 + nt]
                add("dve", (lambda h, r0, c, tmpv: lambda e: e.tensor_tensor(out=tmpv, in0=PB7[r0:r0 + 64, h * 128:h * 128 + nt],
                                                                             in1=bP[r0:r0 + 64, c, 0:nt], op=ALU.add))(h, r0, c, tmpv),
                    reads=[bPB7, bbP], writes=[bSC])
                add("dve", (lambda r0, c, tmpv: lambda e: e.scalar_tensor_tensor(out=concatT[r0:r0 + 64, c, 0:nt], in0=tmpv, scalar=0.5,
                                                                                 in1=SC[r0:r0 + 64, 256 + c * 128:256 + c * 128 + nt],
                                                                                 op0=ALU.mult, op1=ALU.mult))(r0, c, tmpv),
                    reads=[bSC], writes=[bcc[c]])
        if DBG["stage"] < 5:
            return
        if t == 0:
            add("dve", lambda e: e.memset(xpad[:, :, 0:2], 0.0), writes=[bxpad])
        elif sample:
            add("sp", lambda e: e.dma_start(out=xpad[:, :, 0:2], in_=sconvT[l]), writes=[bxpad], dma=True)
        add("act", lambda e: e.activation(out=hin[:, :, 0:nt], in_=PF1[:, 2:4, 0:nt], func=AF.Copy), reads=[bPB[5]], writes=[bhin])
        add("dve", lambda e: e.tensor_tensor(out=xpad[:, :, 2:2 + nt], in0=PF1[:, 0:2, 0:nt], in1=hin[:, :, 0:nt], op=ALU.mult),
            reads=[bPB[5], bhin], writes=[bxpad])
        for c in range(2):
            ci = l * 6 + c * 3
            add("dve", (lambda c, ci: lambda e: e.tensor_scalar(out=cacc[:, c, 0:nt], in0=xpad[:, c, 0:nt], scalar1=cw[:, ci:ci + 1],
                                                                scalar2=None, op0=ALU.mult))(c, ci), reads=[bxpad, bcw], writes=[bcacc])
            for tap in (1, 2):
                add("dve", (lambda c, ci, tap: lambda e: e.scalar_tensor_tensor(out=cacc[:, c, 0:nt], in0=xpad[:, c, tap:tap + nt],
                                                                                scalar=cw[:, ci + tap:ci + tap + 1], in1=cacc[:, c, 0:nt],
                                                                                op0=ALU.mult, op1=ALU.add))(c, ci, tap),
                    reads=[bxpad, bcw, bcacc], writes=[bcacc])
        add("dve", lambda e: e.tensor_tensor(out=concatT[:, 2:4, 0:nt], in0=PF0[:, 2:4, 0:nt], in1=cacc[:, :, 0:nt], op=ALU.mult),
            reads=[bPB[4], bcacc], writes=[bcc[2], bcc[3]])
        if t == 15:
            add("sp", lambda e: e.dma_start(out=ncp[l], in_=xpad[:, :, nt:nt + 2]), reads=[bxpad], dma=True)
        elif sample:
            add("sp", lambda e: e.dma_start(out=ncs[l], in_=xpad[:, :, nt:nt + 2]), reads=[bxpad], dma=True)
        if t < 15:
            add("act", lambda e: e.activation(out=xpad[:, :, 0:2], in_=xpad[:, :, nt:nt + 2], func=AF.Copy), reads=[bxpad], writes=[bxpad])

        if DBG["stage"] < 6:
            return
        if not sample:
            Stot = 128 * (t + 1)
            for n0 in range(0, Stot, 512):
                w_ = min(512, Stot - n0)
                yield from indexer_chunk(nt, n0, w_, kiT, n0, isc, [bkiT[i_] for i_ in range(n0 // 128, (n0 + w_) // 128)], bisc)
            yield from topk_mask(nt, Stot, isc, maskb, bisc, bmaskb, inadm=t * 128 + 64)
            return
        if True:
            for b_ in SAMPLE_AR:
                seed(b_, PROMPT_AR)
            for blk in range(4):
                s0 = blk * 1024
                add("pool", (lambda s0: lambda e: e.dma_start(out=kistage[:, :, :], in_=cki[l, s0:s0 + 1024, :].rearrange("(k p) d -> p k d", p=128)))(s0),
                    writes=[bkistage], dma=True)
                add("dve", lambda e: e.tensor_copy(out=ki4s[:, :, :].rearrange("p k (r d) -> p k r d", r=4),
                                                   in_=kistage[:, :, :].unsqueeze(2).to_broadcast([128, 8, 4, 32])), reads=[bkistage], writes=[bki4s])

                def trk(e):
                    ins = None
                    for k in range(8):
                        ins = e.transpose(PT[:, k * 128:(k + 1) * 128], ki4s[:, k, :], ident[:, :])
                    return ins
                add("pe", trk, reads=[bki4s, bident], writes=[bPT])
                add("act", lambda e: e.activation(out=kiT_s[:, :], in_=PT[:, :], func=AF.Copy), reads=[bPT], writes=[bkiT_s])
                for n0 in (0, 512):
                    yield from indexer_chunk(nt, s0 + n0, 512, kiT_s, n0, isc_s, [bkiT_s], bisc_s)
            yield from indexer_chunk(nt, PAST, TS, kiTn, 0, isc_s, [bkiTn], bisc_s)
            Stot = PAST + TS
            if DBG["ffn"]:
                seed(bF[0], [bWA]); seed(bF[1], [bWA])
                load_ffn_dma(l, 0, 0); load_ffn_dma(l, 1, 1)
                PREF[l] = True
            yield from topk_mask(nt, Stot, isc_s, maskb_s, bisc_s, bmaskb_s, inadm=None)
            for blk in range(4):
                s0 = blk * 1024
                add("pool", (lambda s0: lambda e: e.dma_start(out=kstage[:, :, :], in_=ck[l, s0:s0 + 1024, :].rearrange("(k p) d -> p k d", p=128)))(s0),
                    writes=[bkstage, bv32], dma=True)
                for a_ in range(2):
                    add("pool", (lambda s0, a_: lambda e: e.dma_start(
                        out=Vb_s[:, :, 64 + a_ * 192:64 + a_ * 192 + 128],
                        in_=cv[l, s0:s0 + 1024, a_ * 128:(a_ + 1) * 128].rearrange("(k p) d -> p k d", p=128)))(s0, a_),
                        writes=[bVb_s], dma=True)
                for half in range(2):
                    def trK(e, half=half):
                        ins = None
                        for k in range(4):
                            for gp in range(2):
                                ins = e.transpose(PT[:, (gp * 4 + k) * 128:(gp * 4 + k + 1) * 128],
                                                  kstage[:, half * 4 + k, gp * 128:(gp + 1) * 128], ident[:, :])
                        return ins
                    add("pe", trK, reads=[bkstage, bident], writes=[bPT])
                    add("act", (lambda half: lambda e: e.activation(out=KT_s[:, :, half * 512:(half + 1) * 512],
                                                                    in_=PT[:, :].rearrange("p (g s) -> p g s", g=2), func=AF.Copy))(half),
                        reads=[bPT], writes=[bKT_s])
                tiles = [(kt, 128, s0 + kt * 128) for kt in range(8)]
                for g in range(4):
                    yield from attn_block(nt, g, tiles, KT_s, Vb_s, maskb_s, (lambda kt: [bKT_s, bVb_s]), blk == 0, False, bmaskb_s, qTz, bqT, concatT, bcc)
            for g in range(4):
                yield from attn_block(nt, g, [(0, TS, PAST)], KTn, Vn[:, :].rearrange("p (k c) -> p k c", k=1), maskb_s, (lambda kt: [bKTn, bVn]), False, True, bmaskb_s,
                           qTz, bqT, concatT, bcc)

        _out_proj(t, nt, concatT, bcc)

    def _mixer_B(l, t, nt, sample, qTz, bqT, concatT, bcc, maskb, bmaskb):
        if sample:
            return
        tiles = [(kt, 128, kt * 128) for kt in range(t + 1)]
        for g in range(4):
            yield from attn_block(nt, g, tiles, KT, Vb, maskb, (lambda kt: [bKT[kt], bVb[kt]]), True, True, bmaskb, qTz, bqT, concatT, bcc)
        _out_proj(t, nt, concatT, bcc)

    def _out_proj(t, nt, concatT, bcc):
        if DBG["stage"] < 8:
            return
        for n in range(2):
            def f(e, n=n):
                ins = None
                for c in range(8):
                    ins = e.matmul(PB[n][0:nt, 0:512], lhsT=concatT[:, c, 0:nt], rhs=wout[:, c, n * 512:(n + 1) * 512],
                                   start=(c == 0), stop=(c == 7))
                return ins
            add("pe", f, reads=bcc + [bWB], writes=[bPB[n]])
            add("dve", (lambda n: lambda e: e.tensor_tensor(out=X[0:nt, t, n * 512:(n + 1) * 512], in0=X[0:nt, t, n * 512:(n + 1) * 512],
                                                            in1=PB[n][0:nt, 0:512], op=ALU.add))(n), reads=[bX[t], bPB[n]], writes=[bX[t]])

    def ffn_layer(l, last):
        seed(_ba, [bqkr, bqkbf])
        allh = Buf("hT2seed")
        seed(allh, PROMPT_AR + SAMPLE_AR)
        for t in range(NT):
            bhT2[t].w = None
            bhT2[t].r = list(allh.r)
        groups = [(0, 512, [0, 1, 2, 3]), (512, 512, [4, 5, 6, 7]), (1024, 512, [8, 9, 10, 11]), (1536, 512, [12, 13, 14, 15]), (2048, TS, [16])]

        def norm_group(gi_):
            for t_ in groups[gi_][2]:
                norm_to_hT(t_, hT2, bhT2[t_], t_ * 128)
        norm_group(0)
        for f in range(8):
            slot = f % 3
            wu, wd = ffn_views(slot)
            for gi, (tok0, gw, tl) in enumerate(groups):
                ab = (f * 5 + gi) % 2
                for m in range(4):
                    def fu(e, m=m, gw=gw, wu=wu, tok0=tok0):
                        ins = None
                        for k in range(8):
                            ins = e.matmul(PB[m][:, 0:gw], lhsT=wu[:, k, m * 128:(m + 1) * 128], rhs=hT2[:, k, tok0:tok0 + gw],
                                           start=(k == 0), stop=(k == 7))
                        return ins
                    add("pe", fu, reads=[bF[slot]] + [bhT2[t] for t in tl], writes=[bPB[m]])
                    add("act", (lambda m, gw: lambda e: e.activation(out=rt[m % 2][:, 0:gw], in_=PB[m][:, 0:gw], func=AF.Relu))(m, gw),
                        reads=[bPB[m]], writes=[brt[m % 2]])
                    add("act", (lambda m, gw, ab: lambda e: e.activation(out=actT[ab][:, m, 0:gw], in_=rt[m % 2][:, 0:gw], func=AF.Square))(m, gw, ab),
                        reads=[brt[m % 2]], writes=[bactT[ab]])
                for ti, t in enumerate(tl):
                    nt = nrows(t)
                    for n in range(2):
                        pd = PB[4 + n] if ti % 2 == 0 else (PB7 if n == 1 else PB[4 + n])
                        bpd = bPB[4 + n] if ti % 2 == 0 else (bPB7 if n == 1 else bPB[4 + n])

                        def fd(e, n=n, ti=ti, pd=pd, nt=nt, ab=ab, wd=wd):
                            ins = None
                            for m in range(4):
                                ins = e.matmul(pd[0:nt, 0:512], lhsT=actT[ab][:, m, ti * 128:ti * 128 + nt], rhs=wd[:, m, n * 512:(n + 1) * 512],
                                               start=(m == 0), stop=(m == 3))
                            return ins
                        add("pe", fd, reads=[bactT[ab], bF[slot]], writes=[bpd])
                        add("dve", (lambda n, t, pd, nt: lambda e: e.tensor_tensor(out=X[0:nt, t, n * 512:(n + 1) * 512],
                                                                               in0=X[0:nt, t, n * 512:(n + 1) * 512], in1=pd[0:nt, 0:512],
                                                                               op=ALU.add))(n, t, pd, nt), reads=[bX[t], bpd], writes=[bX[t]])
                if f == 0 and gi + 1 < len(groups):
                    norm_group(gi + 1)
            if f + 3 < 8:
                load_ffn_chunk(l, f + 3, slot)

    for l in range(depth):
        if l == 0:
            load_mix_weights(0)
        load_layer_params(l)
        for b_ in PROMPT_AR:
            if l > 0:
                seed(b_, bhT2 + SAMPLE_AR)
        if l > 0:
            seed(bqkr, [_ba]); seed(bqkbf, [_ba])
        for c0 in (0, 192, 384):
            add("dve", (lambda c0: lambda e: e.memset(Vb[:, :, c0:c0 + 64], 1.0))(c0), writes=bVb)
            if l == 0:
                add("dve", (lambda c0: lambda e: e.memset(Vn[:, c0:c0 + 64], 1.0))(c0), writes=[bVn])
        npt = min(16, DBG["tiles"])

        def drive(gens):
            gens = list(gens)
            while gens:
                for g_ in list(gens):
                    try:
                        next(g_)
                    except StopIteration:
                        gens.remove(g_)
        if DBG.get("pipe", True):
            if npt > 0:
                drive([mixer_tile(l, 0, "A")])
            for t in range(npt):
                gl_ = []
                if t + 1 < npt:
                    gl_.append(mixer_tile(l, t + 1, "A"))
                gl_.append(mixer_tile(l, t, "B"))
                drive(gl_)
        else:
            for t in range(npt):
                drive([mixer_tile(l, t, "AB")])
        if DBG["sample"]:
            for c0 in (0, 192, 384):
                add("dve", (lambda c0: lambda e: e.memset(Vb_s[:, :, c0:c0 + 64], 1.0))(c0), writes=[bVb_s], reads=[])
            drive([mixer_tile(l, 16, "AB")])
        if PREF.get(l):
            seed(bF[2], [bWB])
            fold_ffn(l, 0, 0); fold_ffn(l, 1, 1)
            load_ffn_chunk(l, 2, 2)
        else:
            seed(bF[0], [bWA]); seed(bF[1], [bWA]); seed(bF[2], [bWB])
            for f in range(3):
                load_ffn_chunk(l, f, f)
        if DBG["ffn"]:
            ffn_layer(l, l == depth - 1)
        if l + 1 < depth:
            seed(bWA, [bF[0], bF[1]]); seed(bWB, [bF[2]])
            load_mix_weights(l + 1)
    for t in range(16):
        add("sp", (lambda t: lambda e: e.dma_start(out=yp[t * 128:(t + 1) * 128, :], in_=X[:, t, :]))(t), reads=[bX[t]], dma=True)
    add("sp", lambda e: e.dma_start(out=ys, in_=X[0:TS, 16, :]), reads=[bX[16]], dma=True)

    S.finalize(nc, st)
    with nc.Block() as block:
        S.emit(block)
    st.close()
    return nc


def _rope_tables():
    tab = np.zeros((128, NT, 96), np.float32)
    for t in range(NT):
        n = 128 if t < 16 else TS
        pos = (np.arange(n) + (t * 128 if t < 16 else PAST)).astype(np.float32)
        for half, off in ((32, 0), (16, 64)):
            inv = (np.float32(10000.0) ** (-np.arange(half, dtype=np.float32) / np.float32(half))).astype(np.float32)
            ang = (pos[:, None] * inv[None, :]).astype(np.float32)
            tab[:n, t, off:off + half] = np.cos(ang).astype(np.float32)
            tab[:n, t, off + half:off + 2 * half] = np.sin(ang).astype(np.float32)
    return tab


def _win_perm():
    qcols = []
    for gp in range(2):
        for j in range(2):
            for gl in range(2):
                h = 2 * (2 * gp + gl) + j
                qcols.extend(range(1280 + h * 64, 1280 + (h + 1) * 64))
    tmc = qcols + list(range(1792, 2048)) + list(range(2048, 2304)) + list(range(256, 512)) + list(range(2304, 2560)) \
        + list(range(2560, 2592)) + list(range(2592, 2600))
    fmc = list(range(0, 256)) + list(range(512, 1280))
    return np.array(tmc + fmc, dtype=np.int64)


_NC_CACHE = {}


def kernel(x_prompt, x_sample, cache_k, cache_v, cache_kidx, state_conv, g_mix, w_in, q_gain, k_gain, w_s, b_s, conv_w,
           w_out, g_ffn, w_up, w_down):
    f = lambda a: np.ascontiguousarray(np.asarray(a, dtype=np.float32))
    x_prompt, x_sample, cache_k, cache_v, cache_kidx, state_conv = map(f, (x_prompt, x_sample, cache_k, cache_v, cache_kidx, state_conv))
    g_mix, w_in, q_gain, k_gain, w_s, b_s, conv_w, w_out, g_ffn, w_up, w_down = map(
        f, (g_mix, w_in, q_gain, k_gain, w_s, b_s, conv_w, w_out, g_ffn, w_up, w_down))
    if "nc" not in _NC_CACHE:
        _NC_CACHE["nc"] = build_program()
    nc = _NC_CACHE["nc"]
    w_in_p = np.ascontiguousarray(w_in[:, :, _win_perm()])
    gvec = np.ascontiguousarray(np.concatenate([g_mix.reshape(2, 8, 128), g_ffn.reshape(2, 8, 128)], 0).reshape(32, 128).T)
    wsT = np.ascontiguousarray(w_s.transpose(0, 3, 1, 2))
    cwv = np.ascontiguousarray(conv_w.reshape(2, 3, 2, 128).transpose(3, 0, 2, 1).reshape(128, 12))
    rope = _rope_tables()
    p2 = (2.0 ** -(np.arange(32, dtype=np.float64) + 1)).astype(np.float32).reshape(1, 32)
    shared = dict(w_in=w_in_p, w_out=w_out, w_up=w_up, w_down=w_down, gvec=gvec, wsT=wsT, bs=b_s, cwv=cwv, qg=q_gain, kg=k_gain,
                  rope=rope, p2=p2)
    in_maps = []
    for c in range(8):
        m = dict(shared)
        m["xp"] = x_prompt[c]
        m["xs"] = x_sample[c]
        m["ck"] = np.ascontiguousarray(cache_k[:, c].reshape(2, PAST, 256))
        m["cv"] = np.ascontiguousarray(cache_v[:, c].reshape(2, PAST, 256))
        m["cki"] = np.ascontiguousarray(cache_kidx[:, c])
        m["sconvT"] = np.ascontiguousarray(state_conv[:, c].reshape(2, 2, 2, 128).transpose(0, 3, 2, 1))
        in_maps.append(m)
    res = run_bass_kernel_spmd(nc, in_maps, core_ids=list(range(8)))
    R = res.results
    st_ = lambda k: np.stack([np.asarray(R[c][k]) for c in range(8)])
    yp = st_("yp"); ys = st_("ys")
    nkp = st_("nkp").transpose(1, 0, 2, 3).reshape(2, 8, SEQ, 4, 64)
    nvp = st_("nvp").transpose(1, 0, 2, 3).reshape(2, 8, SEQ, 4, 64)
    nkip = st_("nkip").transpose(1, 0, 2, 3)
    cvt = lambda a: a.transpose(1, 0, 4, 3, 2).reshape(2, 8, 2, 256)
    ncp = cvt(st_("ncp")); ncs = cvt(st_("ncs"))
    nks = st_("nks").transpose(1, 0, 2, 3).reshape(2, 8, TS, 4, 64)
    nvs = st_("nvs").transpose(1, 0, 2, 3).reshape(2, 8, TS, 4, 64)
    nkis = st_("nkis").transpose(1, 0, 2, 3)
    nsgv = st_("nsgv").transpose(1, 0, 2, 3)
    out = (yp, ys, nkp, nvp, nkip, ncp, nks, nvs, nkis, ncs, nsgv)
    return tuple(np.ascontiguousarray(o, dtype=np.float32) for o in out)
```

```python
import contextlib
import numpy as np
import concourse.bass as bass
import concourse.mybir as mybir
from concourse.bass_utils import run_bass_kernel_spmd

F32 = mybir.dt.float32
BF16 = mybir.dt.bfloat16
ALU = mybir.AluOpType
AF = mybir.ActivationFunctionType
AX = mybir.AxisListType

D = 1024
SEQ = 2048
NT = 17
TS = 32
PAST = 4096
DEPTH = 2
INW = 2600
NIT = 16
EPS = 1e-6
NEG = -1.0e30
TMW = 1576
GC = 0.7978845608028654


class Buf:
    __slots__ = ("name", "w", "r", "xr")

    def __init__(self, name="b", xr=False):
        self.name = name
        self.w = None
        self.r = []
        self.xr = xr


def seed(new, olds):
    lst = []
    for o in olds:
        if o.w is not None:
            lst.append(o.w)
        lst.extend(o.r)
    new.w = None
    new.r = lst


class Op:
    __slots__ = ("eng", "fn", "deps", "signal", "dma", "semi", "val", "lane")


class Sched:
    ENGS = ("pe", "act", "dve", "pool", "sp")
    LIMIT = 30000
    NLANES = 8

    def __init__(self):
        self.ops = {e: [] for e in self.ENGS}

    def add(self, eng, fn, reads=(), writes=(), dma=False):
        op = Op()
        op.eng = eng
        op.fn = fn
        op.signal = False
        op.dma = dma
        deps = {}
        for b in reads:
            if b.w is not None:
                deps[id(b.w)] = (b.w, True)
            if b.xr:
                for r in b.r:
                    if r.eng != eng and id(r) not in deps:
                        deps[id(r)] = (r, False)
        for b in writes:
            if b.w is not None and id(b.w) not in deps:
                deps[id(b.w)] = (b.w, False)
            for r in b.r:
                if id(r) not in deps:
                    deps[id(r)] = (r, False)
        op.deps = []
        for d, raw in deps.values():
            if d is op:
                continue
            if not d.dma and not dma and d.eng == eng:
                if eng == "pe":
                    continue
            if not d.dma:
                d.signal = True
            op.deps.append(d)
        for b in reads:
            b.r.append(op)
        for b in writes:
            b.w = op
            b.r = []
        self.ops[eng].append(op)
        return op

    def finalize(self, nc, stack):
        self.sems = {}
        self.final_dma = []
        keys = set()
        for e in self.ENGS:
            cnt = 0
            nd = 0
            lane_cnt = [0] * self.NLANES
            nl = 2 if e == "pool" else self.NLANES
            for op in self.ops[e]:
                if op.dma:
                    op.lane = nd % nl
                    nd += 1
                    lane_cnt[op.lane] += 1
                    op.val = 16 * lane_cnt[op.lane]
                    op.semi = ("dma", e, op.lane)
                    keys.add(op.semi)
                elif op.signal:
                    cnt += 1
                    op.semi = ("c", e, (cnt - 1) // self.LIMIT)
                    op.val = (cnt - 1) % self.LIMIT + 1
                    keys.add(op.semi)
            for ln in range(self.NLANES):
                if lane_cnt[ln]:
                    self.final_dma.append((("dma", e, ln), 16 * lane_cnt[ln]))
        for k in sorted(keys, key=str):
            self.sems[k] = stack.enter_context(nc.semaphore("s_" + "_".join(map(str, k))))

    def emit(self, block):
        def run(engname):
            def body(eng):
                seen = {}
                for op in self.ops[engname]:
                    waits = [(d.semi, d.val) for d in op.deps]
                    if op.dma and op.val > 16:
                        waits.append((op.semi, op.val - 16))
                    for k, v in waits:
                        if seen.get(k, 0) < v:
                            eng.wait_ge(self.sems[k], v)
                            seen[k] = v
                    ins = op.fn(eng)
                    if op.dma:
                        ins.then_inc(self.sems[op.semi], 16)
                    elif op.signal:
                        ins.then_inc(self.sems[op.semi], 1)
                if engname == "sp":
                    for k, v in self.final_dma:
                        if seen.get(k, 0) < v:
                            eng.wait_ge(self.sems[k], v)
            return body
        block.tensor(run("pe"))
        block.scalar(run("act"))
        block.vector(run("dve"))
        block.gpsimd(run("pool"))
        block.sync(run("sp"))


class _Stop(Exception):
    pass


DBG = {"tiles": NT, "stage": 99, "ffn": True, "depth": DEPTH, "sample": True, "tm": True, "fm": True}


def build_program(depth=DEPTH):
    depth = min(depth, DBG["depth"])
    nc = bass.Bass("TRN2", target_bir_lowering=False, dynamic_dma_scratch_size=6144)
    S = Sched()

    def din(name, shape):
        return nc.dram_tensor(name, list(shape), F32, kind="ExternalInput").ap()

    def dout(name, shape):
        return nc.dram_tensor(name, list(shape), F32, kind="ExternalOutput").ap()

    xp = din("xp", [SEQ, D]); xs = din("xs", [TS, D])
    ck = din("ck", [DEPTH, PAST, 256]); cv = din("cv", [DEPTH, PAST, 256]); cki = din("cki", [DEPTH, PAST, 32])
    sconvT = din("sconvT", [DEPTH, 128, 2, 2])
    w_in = din("w_in", [DEPTH, D, INW]); w_out = din("w_out", [DEPTH, D, D])
    w_up = din("w_up", [DEPTH, D, 4 * D]); w_down = din("w_down", [DEPTH, 4 * D, D])
    gvec = din("gvec", [128, 32]); wsT = din("wsT", [DEPTH, 128, 4, 128]); bs = din("bs", [DEPTH, 4, 128])
    cwv = din("cwv", [128, 12]); qg = din("qg", [DEPTH, 64]); kg = din("kg", [DEPTH, 64])
    rope = din("rope", [128, NT, 96]); p2 = din("p2", [1, 32])
    yp = dout("yp", [SEQ, D]); ys = dout("ys", [TS, D])
    nkp = dout("nkp", [DEPTH, SEQ, 256]); nvp = dout("nvp", [DEPTH, SEQ, 256]); nkip = dout("nkip", [DEPTH, SEQ, 32])
    ncp = dout("ncp", [DEPTH, 128, 2, 2])
    nks = dout("nks", [DEPTH, TS, 256]); nvs = dout("nvs", [DEPTH, TS, 256]); nkis = dout("nkis", [DEPTH, TS, 32])
    ncs = dout("ncs", [DEPTH, 128, 2, 2]); nsgv = dout("nsgv", [DEPTH, TS, 256])

    st = contextlib.ExitStack()
    cnt = [0]

    def sb(shape, dt=F32, name=None):
        cnt[0] += 1
        return st.enter_context(nc.sbuf_tensor(name or f"sb{cnt[0]}", list(shape), dt))

    def ps(shape, dt=F32):
        cnt[0] += 1
        return st.enter_context(nc.psum_tensor(f"ps{cnt[0]}", list(shape), dt))

    X = sb([128, NT, D], F32, "X")
    WA = sb([128, 8 * INW], BF16, "WA")
    WB = sb([128, 8 * D], BF16, "WB")
    AR = sb([128, 21504], BF16, "AR")
    SA = sb([128, 1024], F32, "SA"); SB_ = sb([128, 1024], F32, "SBs"); SC = sb([128, 1024], F32, "SC")
    hbf = sb([128, 1024], BF16, "hbf")
    hT = sb([128, 8, 128], BF16, "hT")
    P12 = sb([128, 2304], BF16, "P12")
    qkr = P12[:, 0:1536].bitcast(F32); qkbf = P12[:, 1536:2304]
    v32 = SB_[:, 768:1024]; vbf = sb([128, 256], BF16, "vbf"); sgv = SC[:, 768:1024]
    qki = sb([128, 288], F32, "qki"); qkir = sb([128, 288], F32, "qkir"); qibf = sb([128, 256], BF16, "qibf")
    ki4 = sb([128, 128], BF16, "ki4")
    w8 = sb([128, 8], F32, "w8")
    xpad = sb([128, 2, 130], F32, "xpad"); hin = sb([128, 2, 128], F32, "hin"); cacc = sb([128, 2, 128], F32, "cacc")
    ccT = [sb([128, 8, 128], BF16, f"concatT{i}") for i in range(2)]
    qTzs = [sb([128, 2, 2, 2, 128], BF16, f"qTz{i}") for i in range(2)]; qiT = sb([128, 2, 128], BF16, "qiT")
    rt = [sb([128, 512], F32, f"rt{i}") for i in range(2)]
    pT = [sb([128, 256], BF16, f"pT{i}") for i in range(2)]
    rec = sb([128, 256], F32, "rec")
    accs = SC[:, 0:256].rearrange("p (g c) -> p g c", g=4)
    actT0 = P12[:, 0:2048].rearrange("p (m s) -> p m s", m=4)
    actT = [actT0, actT0]
    ident = sb([128, 128], BF16, "ident")
    ropeB = [sb([128, 96], F32, f"ropeB{i}") for i in range(2)]
    gv = sb([128, 32], F32, "gv"); cw = sb([128, 12], F32, "cw")
    wsf = SC[:, 0:512]; wmT = sb([128, 4, 128], BF16, "wmT")
    bP = sb([128, 2, 128], F32, "bP"); gqk = sb([128, 128], F32, "gqk")
    p2b = sb([128, 32], F32, "p2b"); nh = sb([128, 12], F32, "nh")
    sm = sb([128, 64], F32, "sm")
    steps = sb([128, 32], F32, "steps"); nsteps = sb([128, 32], F32, "nsteps")
    KTn = sb([128, 2, 32], BF16, "KTn"); Vn = sb([128, 448], BF16, "Vn"); kiTn = sb([128, 32], BF16, "kiTn")
    kstage = SB_[:, :].bitcast(BF16).rearrange("p (k c) -> p k c", k=8)
    SAb = SA[:, :].bitcast(BF16)
    ki4s = SAb[:, 0:1024].rearrange("p (k c) -> p k c", k=8)
    kistage = SAb[:, 1024:1280].rearrange("p (k c) -> p k c", k=8)

    KT = AR[:, 0:4096].rearrange("p (g s) -> p g s", g=2)
    Vb = AR[:, 4096:11264].rearrange("p (k c) -> p k c", k=16)
    kiT = AR[:, 11264:13312]
    isc = AR[:, 13312:17408].bitcast(F32)
    maskbs = [AR[:, 17408:19456], AR[:, 19456:21504]]
    isc_s = AR[:, 0:8256].bitcast(F32)
    maskb_s = AR[:, 8256:12384]
    KT_s = AR[:, 12384:14432].rearrange("p (g s) -> p g s", g=2)
    Vb_s = AR[:, 14432:18016].rearrange("p (k c) -> p k c", k=8)
    kiT_s = AR[:, 18016:19040]
    hT2 = AR[:, 0:16640].rearrange("p (k s) -> p k s", k=8)

    win = WA[:, :].rearrange("p (k n) -> p k n", k=8)
    wout = WB[:, :].rearrange("p (k n) -> p k n", k=8)
    FSL = [(WA, 0), (WA, 8192), (WB, 0)]

    PB = [ps([128, 512]) for _ in range(6)]
    PT = ps([128, 1024], BF16)
    PB7 = ps([128, 512])
    bPB = [Buf(f"PB{i}", xr=True) for i in range(6)]
    bPT = Buf("PT", xr=True)
    bPB7 = Buf("PB7", xr=True)

    bX = [Buf(f"X{t}") for t in range(NT)]
    bWA, bWB = Buf("WA"), Buf("WB")
    bF = [Buf("FA"), Buf("FB"), Buf("FC")]
    bSA, bSB, bSC = Buf("SA"), Buf("SB"), Buf("SC")
    bhbf, bhT = Buf("hbf"), Buf("hT")
    bqkr, bqkbf, bv32, bvbf = Buf(), Buf(), Buf(), Buf()
    bsgv = bSC
    bqki, bqkir, bqibf, bki4, bw8 = Buf(), Buf(), Buf(), Buf(), Buf()
    bxpad, bhin, bcacc = Buf("xpad"), Buf(), Buf()
    bccs = [[Buf(f"cc{i}_{c}") for c in range(8)] for i in range(2)]
    bqTs = [Buf("qT0"), Buf("qT1")]
    bqiT = Buf("qiT")
    brt = [Buf(), Buf()]
    bpT = [Buf(), Buf()]
    brec = Buf()
    baccs = bSC
    _ba = Buf()
    bactT = [_ba, _ba]
    bident, bgv, bcw, bwmT, bbP, bgqk, bp2b, bnh = [Buf() for _ in range(8)]
    bwsf = bSC
    bropeB = [Buf(), Buf()]
    bss, brstd, bss12, br12, brmax, brmin, bw0, bmid, bcnt, btq, bthr, bsteps = [Buf() for _ in range(12)]
    bKT = [Buf(f"KT{t}") for t in range(16)]; bVb = [Buf(f"Vb{t}") for t in range(16)]; bkiT = [Buf(f"kiT{t}") for t in range(16)]
    bisc = Buf("isc"); bmaskbs = [Buf("maskb0"), Buf("maskb1")]
    bisc_s, bmaskb_s, bKT_s, bVb_s, bkiT_s = Buf(), Buf(), Buf(), Buf(), Buf()
    bKTn, bVn, bkiTn = Buf(), Buf(), Buf()
    bkstage, bkistage, bki4s = bSB, bSA, bSA
    bhT2 = [Buf(f"hT2_{t}") for t in range(NT)]
    PROMPT_AR = bKT + bVb + bkiT + [bisc] + bmaskbs
    SAMPLE_AR = [bisc_s, bmaskb_s, bKT_s, bVb_s, bkiT_s]

    ss = sm[:, 0:1]; msq = sm[:, 1:2]; rstd = sm[:, 2:3]
    ss12 = sm[:, 4:16]; ms12 = sm[:, 16:28]; r12 = sm[:, 28:40]
    rmax = sm[:, 40:41]; rmin = sm[:, 41:42]; w0 = sm[:, 42:43]; mid = sm[:, 43:44]
    cntc = sm[:, 44:45]; tq = sm[:, 45:46]; thr = sm[:, 46:47]

    add = S.add

    def nrows(t):
        return 128 if t < 16 else TS

    add("pool", lambda e: e.memset(ident[:], 0.0), writes=[bident])
    add("pool", lambda e: e.affine_select(out=ident[:], in_=ident[:], pattern=[[-1, 128]], compare_op=ALU.not_equal,
                                          fill=1.0, base=0, channel_multiplier=1), reads=[bident], writes=[bident])
    add("pool", lambda e: e.memset(nh[:], -0.5), writes=[bnh])
    for i_ in range(2):
        add("pool", (lambda i_: lambda e: e.memset(qTzs[i_][:].rearrange("p a b c d -> p (a b c d)"), 0.0))(i_), writes=[bqTs[i_]])
    add("sp", lambda e: e.dma_start(out=gv[:], in_=gvec), writes=[bgv], dma=True)
    add("sp", lambda e: e.dma_start(out=cw[:], in_=cwv), writes=[bcw], dma=True)
    add("sp", lambda e: e.dma_start(out=p2b[:], in_=p2.partition_broadcast(128)), writes=[bp2b], dma=True)
    for t in range(16):
        add("sp", (lambda t: lambda e: e.dma_start(out=X[:, t, :], in_=xp[t * 128:(t + 1) * 128, :]))(t), writes=[bX[t]], dma=True)
    add("sp", lambda e: e.dma_start(out=X[0:TS, 16, :], in_=xs), writes=[bX[16]], dma=True)

    def load_mix_weights(l):
        src = w_in[l].rearrange("(k p) n -> p k n", p=128)
        add("pool", lambda e: e.dma_start(out=win[:, :, 0:1300], in_=src[:, :, 0:1300]), writes=[bWA], dma=True)
        add("pool", lambda e: e.dma_start(out=win[:, :, 1300:2600], in_=src[:, :, 1300:2600]), writes=[bWA], dma=True)
        for k in range(8):
            add("dve", (lambda k: lambda e: e.tensor_scalar(out=win[:, k, :], in0=win[:, k, :], scalar1=gv[:, l * 8 + k:l * 8 + k + 1],
                                                            scalar2=None, op0=ALU.mult))(k), reads=[bWA, bgv], writes=[bWA])
        src2 = w_out[l].rearrange("(k p) n -> p k n", p=128)
        add("pool", lambda e: e.dma_start(out=wout[:, :, :], in_=src2), writes=[bWB], dma=True)

    def ffn_views(slot):
        tns, off = FSL[slot]
        wu = tns[:, off:off + 4096].rearrange("p (k n) -> p k n", k=8)
        wd = tns[:, off + 4096:off + 8192].rearrange("p (m n) -> p m n", m=4)
        return wu, wd

    def load_ffn_dma(l, f, slot):
        wu, wd = ffn_views(slot)
        srcu = w_up[l].rearrange("(k p) n -> p k n", p=128)[:, :, f * 512:(f + 1) * 512]
        srcd = w_down[l, f * 512:(f + 1) * 512, :].rearrange("(m p) n -> p m n", p=128)
        b = bF[slot]
        add("pool", lambda e: e.dma_start(out=wu, in_=srcu), writes=[b], dma=True)
        add("pool", lambda e: e.dma_start(out=wd, in_=srcd), writes=[b], dma=True)

    def fold_ffn(l, f, slot):
        wu, wd = ffn_views(slot)
        b = bF[slot]
        for k in range(8):
            add("dve", (lambda k: lambda e: e.tensor_scalar(out=wu[:, k, :], in0=wu[:, k, :],
                                                            scalar1=gv[:, 16 + l * 8 + k:16 + l * 8 + k + 1], scalar2=None,
                                                            op0=ALU.mult))(k), reads=[b, bgv], writes=[b])

    def load_ffn_chunk(l, f, slot):
        load_ffn_dma(l, f, slot)
        fold_ffn(l, f, slot)

    PREF = {}

    def load_layer_params(l):
        add("sp", lambda e: e.dma_start(out=wsf.rearrange("p (h i) -> p h i", h=4), in_=wsT[l]), writes=[bwsf], dma=True)
        add("dve", lambda e: e.tensor_copy(out=wmT[:].rearrange("p h i -> p (h i)"), in_=wsf), reads=[bwsf], writes=[bwmT])
        add("dve", lambda e: e.memset(wmT[64:128, :, 0:64], 0.0), writes=[bwmT])
        for c in range(2):
            for hh in range(2):
                add("sp", (lambda c, hh: lambda e: e.dma_start(out=bP[hh * 64:(hh + 1) * 64, c, :],
                                                               in_=bs[l, 2 * c + hh:2 * c + hh + 1, :].partition_broadcast(64)))(c, hh),
                    writes=[bbP], dma=True)
        add("sp", lambda e: e.dma_start(out=gqk[:, 0:64], in_=qg[l:l + 1, :].partition_broadcast(128)), writes=[bgqk], dma=True)
        add("sp", lambda e: e.dma_start(out=gqk[:, 64:128], in_=kg[l:l + 1, :].partition_broadcast(128)), writes=[bgqk], dma=True)

    def norm_to_hT(t, dstT, bdst, col0):
        nt = nrows(t)
        xt = X[0:nt, t, :]
        add("act", lambda e: e.activation(out=hbf[0:nt, :], in_=xt, func=AF.Square, accum_out=ss[0:nt, :]),
            reads=[bX[t]], writes=[bhbf, bss])
        add("dve", lambda e: e.tensor_scalar(out=msq[0:nt, :], in0=ss[0:nt, :], scalar1=1.0 / D, scalar2=EPS, op0=ALU.mult, op1=ALU.add),
            reads=[bss], writes=[brstd])
        add("pool", lambda e: e.tensor_tensor(out=rstd[0:nt, :], in0=msq[0:nt, :], in1=nh[0:nt, 0:1], op=ALU.pow),
            reads=[brstd, bnh], writes=[brstd])
        add("act", lambda e: e.activation(out=hbf[0:nt, :], in_=xt, func=AF.Copy, scale=rstd[0:nt, :]),
            reads=[bX[t], brstd], writes=[bhbf])

        def tr(e):
            ins = None
            for k in range(8):
                ins = e.transpose(PT[:, k * 128:k * 128 + nt], hbf[0:nt, k * 128:(k + 1) * 128], ident[0:nt, 0:nt])
            return ins
        add("pe", tr, reads=[bhbf, bident], writes=[bPT])
        add("dve", lambda e: e.tensor_copy(out=dstT[:, :, col0:col0 + nt],
                                           in_=PT[:, :].rearrange("p (k s) -> p k s", k=8)[:, :, 0:nt]),
            reads=[bPT], writes=[bdst])

    def rope_ops(src, dst, nt, nh_, half, cos, sin, bsrc, bdst, brope):
        s3 = src.rearrange("p (h d) -> p h d", h=nh_)
        d3 = dst.rearrange("p (h d) -> p h d", h=nh_)
        x1 = s3[:, :, 0:half]; x2 = s3[:, :, half:2 * half]
        cb = cos.unsqueeze(1).to_broadcast([nt, nh_, half])
        sbb = sin.unsqueeze(1).to_broadcast([nt, nh_, half])
        ta = SA[0:nt, 0:nh_ * half].rearrange("p (h d) -> p h d", h=nh_)
        tb = SA[0:nt, 512:512 + nh_ * half].rearrange("p (h d) -> p h d", h=nh_)
        add("dve", lambda e: e.tensor_tensor(out=ta, in0=x1, in1=cb, op=ALU.mult), reads=[bsrc, brope], writes=[bSA])
        add("dve", lambda e: e.tensor_tensor(out=tb, in0=x2, in1=sbb, op=ALU.mult), reads=[bsrc, brope], writes=[bSA])
        add("dve", lambda e: e.tensor_tensor(out=d3[:, :, 0:half], in0=ta, in1=tb, op=ALU.subtract), reads=[bSA], writes=[bdst])
        add("dve", lambda e: e.tensor_tensor(out=ta, in0=x2, in1=cb, op=ALU.mult), reads=[bsrc, brope], writes=[bSA])
        add("dve", lambda e: e.tensor_tensor(out=tb, in0=x1, in1=sbb, op=ALU.mult), reads=[bsrc, brope], writes=[bSA])
        add("dve", lambda e: e.tensor_tensor(out=d3[:, :, half:2 * half], in0=ta, in1=tb, op=ALU.add), reads=[bSA], writes=[bdst])

    def gelu_core(src, T, bsrc_list, bT):
        add("act", lambda e: e.activation(out=T, in_=src, func=AF.Square), reads=bsrc_list, writes=[bT])
        add("dve", lambda e: e.tensor_scalar(out=T, in0=T, scalar1=0.044715, scalar2=1.0, op0=ALU.mult, op1=ALU.add), reads=[bT], writes=[bT])
        add("dve", lambda e: e.tensor_tensor(out=T, in0=T, in1=src, op=ALU.mult), reads=[bT] + bsrc_list, writes=[bT])
        add("act", lambda e: e.activation(out=T, in_=T, func=AF.Tanh, scale=GC), reads=[bT], writes=[bT])
        add("dve", lambda e: e.scalar_tensor_tensor(out=T, in0=T, scalar=1.0, in1=src, op0=ALU.add, op1=ALU.mult),
            reads=[bT] + bsrc_list, writes=[bT])

    VOFF = [0, 128, 192, 320]
    O_ROWS = [64, 0, 64, 0]

    def attn_block(nt, g, tiles, KTsrc, Vsrc, mask_ap, bk_fn, first, last_norm, bmask, qTz, bqT, concatT, bcc):
        gp, gl = g // 2, g % 2
        scb = PB[4]
        bscb = bPB[4]
        PO = PB[5]
        n2 = 2 * nt
        ntiles = len(tiles)
        for i, (kt, ks, mc0) in enumerate(tiles):
            half = (i % 2) * 256
            pti = i % 2

            def sc_fn(e, kt=kt, ks=ks, mc0=mc0, half=half):
                e.matmul(scb[0:ks, half:half + n2].rearrange("p (j t) -> p j t", j=2),
                         lhsT=KTsrc[:, gp, kt * 128:kt * 128 + ks],
                         rhs=qTz[:, gl, gp, :, 0:nt], start=True, stop=False)
                ins = None
                for j in range(2):
                    ins = e.matmul(scb[0:ks, half + j * nt:half + (j + 1) * nt], lhsT=mask_ap[0:nt, mc0:mc0 + ks],
                                   rhs=ident[0:nt, 0:nt], start=False, stop=(j == 1))
                return ins
            add("pe", sc_fn, reads=[bqT, bmask, bident] + bk_fn(kt), writes=[bscb])
            add("act", (lambda ks, half, pti: lambda e: e.activation(out=pT[pti][0:ks, 0:n2], in_=scb[0:ks, half:half + n2],
                                                                      func=AF.Exp, scale=0.125))(ks, half, pti),
                reads=[bscb], writes=[bpT[pti]])
            add("pe", (lambda kt, ks, pti, i: lambda e: e.matmul(PO[:, 0:n2], lhsT=Vsrc[0:ks, kt, VOFF[g]:VOFF[g] + 128],
                                                                 rhs=pT[pti][0:ks, 0:n2], start=(i == 0), stop=(i == ntiles - 1)))(kt, ks, pti, i),
                reads=[bpT[pti]] + bk_fn(kt), writes=[bPB[5]])
            yield
        if last_norm and first:
            src_all = PO
            bsrc = bPB[5]
        else:
            if first:
                add("act", lambda e: e.activation(out=accs[:, g, 0:n2], in_=PO[:, 0:n2], func=AF.Copy), reads=[bPB[5]], writes=[baccs])
            else:
                add("dve", lambda e: e.tensor_tensor(out=accs[:, g, 0:n2], in0=accs[:, g, 0:n2], in1=PO[:, 0:n2], op=ALU.add),
                    reads=[bPB[5], baccs], writes=[baccs])
            src_all = accs[:, g, :]
            bsrc = baccs
        if last_norm:
            orow = O_ROWS[g]
            srow = 64 - orow
            add("dve", lambda e: e.reciprocal(out=rec[orow:orow + 64, 0:n2], in_=src_all[srow:srow + 64, 0:n2]), reads=[bsrc], writes=[brec])
            for j in range(2):
                add("dve", (lambda j: lambda e: e.tensor_tensor(out=concatT[64 * j:64 * j + 64, 4 + g, 0:nt],
                                                                in0=src_all[orow:orow + 64, j * nt:(j + 1) * nt],
                                                                in1=rec[orow:orow + 64, j * nt:(j + 1) * nt], op=ALU.mult))(j),
                    reads=[bsrc, brec], writes=[bcc[4 + g]])

    def indexer_chunk(nt, n0, w, kiTsrc, kcol0, isc_ap, bki_list, bisc_):
        for h in range(8):
            r, c = h % 4, h // 4
            add("pe", (lambda r, c: lambda e: e.matmul(PB[r][0:nt, 0:w], lhsT=qiT[32 * r:32 * r + 32, c, 0:nt],
                                                       rhs=kiTsrc[32 * r:32 * r + 32, kcol0:kcol0 + w], start=True, stop=True,
                                                       tile_position=(32 * r, 0)))(r, c),
                reads=[bqiT] + bki_list, writes=[bPB[r]])
            add("act", (lambda r, h: lambda e: e.activation(out=rt[h % 2][0:nt, 0:w], in_=PB[r][0:nt, 0:w], func=AF.Relu))(r, h),
                reads=[bPB[r]], writes=[brt[h % 2]])
            if h == 0:
                add("dve", lambda e: e.tensor_scalar(out=isc_ap[0:nt, n0:n0 + w], in0=rt[0][0:nt, 0:w], scalar1=w8[0:nt, 0:1],
                                                     scalar2=None, op0=ALU.mult), reads=[brt[0], bw8], writes=[bisc_])
            else:
                add("dve", (lambda h: lambda e: e.scalar_tensor_tensor(out=isc_ap[0:nt, n0:n0 + w], in0=rt[h % 2][0:nt, 0:w],
                                                                       scalar=w8[0:nt, h:h + 1], in1=isc_ap[0:nt, n0:n0 + w],
                                                                       op0=ALU.mult, op1=ALU.add))(h),
                    reads=[brt[h % 2], bw8, bisc_], writes=[bisc_])
            yield

    def topk_mask(nt, Stot, isc_ap, mask_ap, bisc_, bmask, inadm=None):
        iv = isc_ap[0:nt, 0:Stot]
        yield
        if Stot > 256:
            add("dve", lambda e: e.tensor_reduce(out=rmax[0:nt, :], in_=iv, axis=AX.X, op=ALU.max), reads=[bisc_], writes=[brmax])
            add("dve", lambda e: e.tensor_reduce(out=rmin[0:nt, :], in_=iv, axis=AX.X, op=ALU.min), reads=[bisc_], writes=[brmin])
        if inadm is not None:
            add("dve", lambda e: e.memset(isc_ap[0:64, inadm:inadm + 64], NEG), writes=[bisc_])
        if Stot <= 256:
            add("dve", lambda e: e.memset(thr[0:nt, :], 0.5 * NEG), writes=[bthr])
        else:
            add("dve", lambda e: e.tensor_tensor(out=w0[0:nt, :], in0=rmax[0:nt, :], in1=rmin[0:nt, :], op=ALU.subtract),
                reads=[brmax, brmin], writes=[bw0])
            add("dve", lambda e: e.tensor_scalar(out=steps[0:nt, :], in0=p2b[0:nt, :], scalar1=w0[0:nt, :], scalar2=None, op0=ALU.mult),
                reads=[bp2b, bw0], writes=[bsteps])
            add("dve", lambda e: e.tensor_scalar(out=nsteps[0:nt, :], in0=steps[0:nt, :], scalar1=-1.0, scalar2=None, op0=ALU.mult),
                reads=[bsteps], writes=[bsteps])
            add("dve", lambda e: e.tensor_scalar(out=mid[0:nt, :], in0=rmin[0:nt, :], scalar1=steps[0:nt, 0:1], scalar2=-1.0,
                                                 op0=ALU.add, op1=ALU.mult), reads=[brmin, bsteps], writes=[bmid])
            for n in range(NIT):
                add("act", lambda e: e.activation(out=mask_ap[0:nt, 0:Stot], in_=iv, func=AF.Sign, bias=mid[0:nt, :],
                                                  accum_out=cntc[0:nt, :]), reads=[bisc_, bmid], writes=[bmask, bcnt])
                add("act", lambda e: e.activation(out=tq[0:nt, :], in_=cntc[0:nt, :], func=AF.Sign, bias=float(Stot - 511)),
                    reads=[bcnt], writes=[btq])
                add("act", (lambda n: lambda e: e.activation(out=mid[0:nt, :], in_=tq[0:nt, :], func=AF.Identity,
                                                             scale=nsteps[0:nt, n + 1:n + 2], bias=mid[0:nt, :]))(n),
                    reads=[btq, bsteps, bmid], writes=[bmid])
                yield
            add("dve", lambda e: e.tensor_scalar(out=thr[0:nt, :], in0=mid[0:nt, :], scalar1=-1.0, scalar2=steps[0:nt, NIT:NIT + 1],
                                                 op0=ALU.mult, op1=ALU.subtract), reads=[bmid, bsteps], writes=[bthr])
        add("dve", lambda e: e.tensor_scalar(out=mask_ap[0:nt, 0:Stot], in0=iv, scalar1=thr[0:nt, :], scalar2=-30000.0,
                                             op0=ALU.is_lt, op1=ALU.mult), reads=[bisc_, bthr], writes=[bmask])

    def mixer_tile(l, t, part="AB"):
        nt = nrows(t)
        sample = (t == 16)
        par = t % 2
        qTz, bqT, concatT, bcc = qTzs[par], bqTs[par], ccT[par], bccs[par]
        maskb, bmaskb = maskbs[par], bmaskbs[par]
        if "A" in part:
            yield from _mixer_A(l, t, nt, sample, qTz, bqT, concatT, bcc, maskb, bmaskb)
        if "B" in part:
            yield from _mixer_B(l, t, nt, sample, qTz, bqT, concatT, bcc, maskb, bmaskb)

    def _mixer_A(l, t, nt, sample, qTz, bqT, concatT, bcc, maskb, bmaskb):
        rT = ropeB[t % 2]; brT = bropeB[t % 2]
        add("sp", lambda e: e.dma_start(out=rT[:, :], in_=rope[:, t, :]), writes=[brT], dma=True)
        norm_to_hT(t, hT, bhT, 0)
        if DBG["stage"] < 2:
            return
        tm = [(0, 512), (512, 512), (1024, 512), (1536, 40)]
        for n, (c0, w) in enumerate(tm):
            if not DBG["tm"]:
                break

            def f(e, c0=c0, w=w, n=n):
                ins = None
                for k in range(8):
                    ins = e.matmul(PB[n][0:nt, 0:w], lhsT=hT[:, k, 0:nt], rhs=win[:, k, c0:c0 + w], start=(k == 0), stop=(k == 7))
                return ins
            add("pe", f, reads=[bhT, bWA], writes=[bPB[n]])
        for fi in range(8):
            if not DBG["fm"]:
                break
            bank = PB[4] if fi < 4 else PB[5]
            bb = bPB[4] if fi < 4 else bPB[5]

            def f(e, fi=fi, bank=bank):
                ins = None
                for k in range(8):
                    ins = e.matmul(bank[:, (fi % 4) * 128:(fi % 4) * 128 + nt], lhsT=win[:, k, TMW + fi * 128:TMW + (fi + 1) * 128],
                                   rhs=hT[:, k, 0:nt], start=(k == 0), stop=(k == 7))
                return ins
            add("pe", f, reads=[bhT, bWA], writes=[bb])
        PF0 = PB[4][:, :].rearrange("p (f s) -> p f s", f=4)
        PF1 = PB[5][:, :].rearrange("p (f s) -> p f s", f=4)
        if DBG["stage"] < 3:
            return
        add("act", lambda e: e.activation(out=SB_[0:nt, 0:512], in_=PB[0][0:nt, 0:512], func=AF.Square), reads=[bPB[0]], writes=[bSB])
        add("act", lambda e: e.activation(out=SB_[0:nt, 512:768], in_=PB[1][0:nt, 0:256], func=AF.Square), reads=[bPB[1]], writes=[bSB])
        add("dve", lambda e: e.tensor_reduce(out=ss12[0:nt, :], in_=SB_[0:nt, 0:768].rearrange("p (h d) -> p h d", h=12), axis=AX.X, op=ALU.add),
            reads=[bSB], writes=[bss12])
        add("dve", lambda e: e.tensor_scalar(out=ms12[0:nt, :], in0=ss12[0:nt, :], scalar1=1.0 / 64, scalar2=EPS, op0=ALU.mult, op1=ALU.add),
            reads=[bss12], writes=[bss12])
        add("pool", lambda e: e.tensor_tensor(out=r12[0:nt, :], in0=ms12[0:nt, :], in1=nh[0:nt, :], op=ALU.pow), reads=[bss12, bnh], writes=[br12])
        add("dve", lambda e: e.tensor_tensor(out=SB_[0:nt, 0:512].rearrange("p (h d) -> p h d", h=8),
                                             in0=PB[0][0:nt, 0:512].rearrange("p (h d) -> p h d", h=8),
                                             in1=r12[0:nt, 0:8].unsqueeze(2).to_broadcast([nt, 8, 64]), op=ALU.mult),
            reads=[bPB[0], br12], writes=[bSB])
        add("dve", lambda e: e.tensor_tensor(out=SB_[0:nt, 512:768].rearrange("p (h d) -> p h d", h=4),
                                             in0=PB[1][0:nt, 0:256].rearrange("p (h d) -> p h d", h=4),
                                             in1=r12[0:nt, 8:12].unsqueeze(2).to_broadcast([nt, 4, 64]), op=ALU.mult),
            reads=[bPB[1], br12], writes=[bSB])
        add("dve", lambda e: e.tensor_tensor(out=SB_[0:nt, 0:512].rearrange("p (h d) -> p h d", h=8),
                                             in0=SB_[0:nt, 0:512].rearrange("p (h d) -> p h d", h=8),
                                             in1=gqk[0:nt, 0:64].unsqueeze(1).to_broadcast([nt, 8, 64]), op=ALU.mult),
            reads=[bSB, bgqk], writes=[bSB])
        add("dve", lambda e: e.tensor_tensor(out=SB_[0:nt, 512:768].rearrange("p (h d) -> p h d", h=4),
                                             in0=SB_[0:nt, 512:768].rearrange("p (h d) -> p h d", h=4),
                                             in1=gqk[0:nt, 64:128].unsqueeze(1).to_broadcast([nt, 4, 64]), op=ALU.mult),
            reads=[bSB, bgqk], writes=[bSB])
        if DBG["stage"] < 3.1:
            return
        rope_ops(SB_[0:nt, 0:768], qkr[0:nt, :], nt, 12, 32, rT[0:nt, 0:32], rT[0:nt, 32:64], bSB, bqkr, brT)
        if DBG["stage"] < 3.2:
            return
        add("act", lambda e: e.activation(out=v32[0:nt, :], in_=PB[1][0:nt, 256:512], func=AF.Copy), reads=[bPB[1]], writes=[bv32])
        if not sample:
            Vdst = Vb[0:nt, t, 64:448].rearrange("p (a b) -> p a b", a=2)[:, :, 0:128]
            bVdst = bVb[t]
        else:
            Vdst = Vn[0:nt, 64:448].rearrange("p (a b) -> p a b", a=2)[:, :, 0:128]
            bVdst = bVn
        add("dve", lambda e: e.tensor_copy(out=Vdst, in_=PB[1][0:nt, 256:512].rearrange("p (a b) -> p a b", a=2)), reads=[bPB[1]], writes=[bVdst])
        if DBG["stage"] < 3.3:
            return
        add("act", lambda e: e.activation(out=qki[0:nt, 0:256], in_=PB[2][0:nt, 256:512], func=AF.Copy), reads=[bPB[2]], writes=[bqki])
        add("act", lambda e: e.activation(out=qki[0:nt, 256:288], in_=PB[3][0:nt, 0:32], func=AF.Copy), reads=[bPB[3]], writes=[bqki])
        add("act", lambda e: e.activation(out=w8[0:nt, :], in_=PB[3][0:nt, 32:40], func=AF.Copy, scale=0.0625), reads=[bPB[3]], writes=[bw8])
        if DBG["stage"] < 3.4:
            return
        TV = SC[0:nt, 0:256]
        gelu_core(PB[2][0:nt, 0:256], TV, [bPB[2]], bSC)
        add("act", lambda e: e.activation(out=vbf[0:nt, :], in_=TV, func=AF.Copy, scale=0.5), reads=[bSC], writes=[bvbf])
        if sample:
            add("act", lambda e: e.activation(out=sgv[0:nt, :], in_=TV, func=AF.Copy, scale=0.5), reads=[bSC], writes=[bsgv])
            add("sp", lambda e: e.dma_start(out=nsgv[l], in_=sgv[0:nt, :]), reads=[bsgv], dma=True)
        if DBG["stage"] < 3.5:
            return
        rope_ops(qki[0:nt, :], qkir[0:nt, :], nt, 9, 16, rT[0:nt, 64:80], rT[0:nt, 80:96], bqki, bqkir, brT)
        if DBG["stage"] < 3.6:
            return
        if not sample:
            row0 = t * 128
            add("sp", lambda e: e.dma_start(out=nkp[l, row0:row0 + nt, :], in_=qkr[0:nt, 512:768]), reads=[bqkr], dma=True)
            add("sp", lambda e: e.dma_start(out=nvp[l, row0:row0 + nt, :], in_=v32[0:nt, :]), reads=[bv32], dma=True)
            add("sp", lambda e: e.dma_start(out=nkip[l, row0:row0 + nt, :], in_=qkir[0:nt, 256:288]), reads=[bqkir], dma=True)
        else:
            add("sp", lambda e: e.dma_start(out=nks[l], in_=qkr[0:nt, 512:768]), reads=[bqkr], dma=True)
            add("sp", lambda e: e.dma_start(out=nvs[l], in_=v32[0:nt, :]), reads=[bv32], dma=True)
            add("sp", lambda e: e.dma_start(out=nkis[l], in_=qkir[0:nt, 256:288]), reads=[bqkir], dma=True)
        if DBG["stage"] < 3.7:
            return
        add("act", lambda e: e.activation(out=qkbf[0:nt, :], in_=qkr[0:nt, :], func=AF.Copy), reads=[bqkr], writes=[bqkbf])

        def trq(e):
            ins = None
            for b_ in range(6):
                ins = e.transpose(PT[:, b_ * 128:b_ * 128 + nt], qkbf[0:nt, b_ * 128:(b_ + 1) * 128], ident[0:nt, 0:nt])
            return ins
        add("pe", trq, reads=[bqkbf, bident], writes=[bPT])
        PT3 = PT[:, :].rearrange("p (k s) -> p k s", k=8)
        for gl in range(2):
            if DBG.get("skip") == "qTz" or (DBG.get("skip") == "qTz1" and gl == 1):
                continue
            add("dve", (lambda gl: lambda e: e.tensor_copy(
                out=qTz[64 * gl:64 * gl + 64, gl, :, :, 0:nt],
                in_=PT3[64 * gl:64 * gl + 64, 0:4, 0:nt].rearrange("p (a b) s -> p a b s", a=2)))(gl), reads=[bPT], writes=[bqT])
        if DBG.get("skip") == "KT":
            return
        if not sample:
            add("dve", lambda e: e.tensor_copy(out=KT[:, :, t * 128:t * 128 + nt], in_=PT3[:, 4:6, 0:nt]), reads=[bPT], writes=[bKT[t]])
        else:
            add("dve", lambda e: e.tensor_copy(out=KTn[:, :, 0:nt], in_=PT3[:, 4:6, 0:nt]), reads=[bPT], writes=[bKTn])
        if DBG["stage"] < 3.8:
            return
        if DBG.get("skip") == "qibf":
            return
        add("act", lambda e: e.activation(out=qibf[0:nt, :], in_=qkir[0:nt, 0:256], func=AF.Copy), reads=[bqkir], writes=[bqibf])
        if DBG.get("skip") == "ki4":
            return
        add("dve", lambda e: e.tensor_copy(out=ki4[0:nt, :].rearrange("p (r d) -> p r d", r=4),
                                           in_=qkir[0:nt, 256:288].unsqueeze(1).to_broadcast([nt, 4, 32])), reads=[bqkir], writes=[bki4])

        def tri(e):
            e.transpose(PT[:, 0:nt], qibf[0:nt, 0:128], ident[0:nt, 0:nt])
            e.transpose(PT[:, 128:128 + nt], qibf[0:nt, 128:256], ident[0:nt, 0:nt])
            return e.transpose(PT[:, 256:256 + nt], ki4[0:nt, :], ident[0:nt, 0:nt])
        if DBG.get("skip") == "tri":
            return
        add("pe", tri, reads=[bqibf, bki4, bident], writes=[bPT])
        if DBG.get("skip") == "qiT":
            return
        add("dve", lambda e: e.tensor_copy(out=qiT[:, :, 0:nt], in_=PT3[:, 0:2, 0:nt]), reads=[bPT], writes=[bqiT])
        if DBG.get("skip") == "kiT":
            return
        if not sample:
            add("dve", lambda e: e.tensor_copy(out=kiT[:, t * 128:t * 128 + nt], in_=PT[:, 256:256 + nt]), reads=[bPT], writes=[bkiT[t]])
        else:
            add("dve", lambda e: e.tensor_copy(out=kiTn[:, 0:nt], in_=PT[:, 256:256 + nt]), reads=[bPT], writes=[bkiTn])

        if DBG["stage"] < 4:
            return
        TU = SC[:, 256:512].rearrange("p (c s) -> p c s", c=2)[:, :, 0:nt]
        gelu_core(PF0[:, 0:2, 0:nt], TU, [bPB[4]], bSC)
        for c in range(2):
            for hh in range(2):
                h = 2 * c + hh
                add("pe", (lambda c, h: lambda e: e.matmul(PB7[:, h * 128:h * 128 + nt], lhsT=vbf[0:nt, c * 128:(c + 1) * 128],
                                                           rhs=wmT[0:nt, h, 0:nt], start=True, stop=True))(c, h),
                    reads=[bvbf, bwmT], writes=[bPB7])
        for c in range(2):
            for hh in range(2):
                h = 2 * c + hh
                r0 = hh * 64
                tmpv = SC[r0:r0 + 64, 512 + h * 128:512 + h * 128 + nt]
                add("dve", (lambda h, r0, c, tmpv: lambda e: e.tensor_tensor(out=tmpv, in0=PB7[r0:r0 + 64, h * 128:h * 128 + nt],
                                                                             in1=bP[r0:r0 + 64, c, 0:nt], op=ALU.add))(h, r0, c, tmpv),
                    reads=[bPB7, bbP], writes=[bSC])
                add("dve", (lambda r0, c, tmpv: lambda e: e.scalar_tensor_tensor(out=concatT[r0:r0 + 64, c, 0:nt], in0=tmpv, scalar=0.5,
                                                                                 in1=SC[r0:r0 + 64, 256 + c * 128:256 + c * 128 + nt],
                                                                                 op0=ALU.mult, op1=ALU.mult))(r0, c, tmpv),
                    reads=[bSC], writes=[bcc[c]])
        if DBG["stage"] < 5:
            return
        if t == 0:
            add("dve", lambda e: e.memset(xpad[:, :, 0:2], 0.0), writes=[bxpad])
        elif sample:
            add("sp", lambda e: e.dma_start(out=xpad[:, :, 0:2], in_=sconvT[l]), writes=[bxpad], dma=True)
        add("act", lambda e: e.activation(out=hin[:, :, 0:nt], in_=PF1[:, 2:4, 0:nt], func=AF.Copy), reads=[bPB[5]], writes=[bhin])
        add("dve", lambda e: e.tensor_tensor(out=xpad[:, :, 2:2 + nt], in0=PF1[:, 0:2, 0:nt], in1=hin[:, :, 0:nt], op=ALU.mult),
            reads=[bPB[5], bhin], writes=[bxpad])
        for c in range(2):
            ci = l * 6 + c * 3
            add("dve", (lambda c, ci: lambda e: e.tensor_scalar(out=cacc[:, c, 0:nt], in0=xpad[:, c, 0:nt], scalar1=cw[:, ci:ci + 1],
                                                                scalar2=None, op0=ALU.mult))(c, ci), reads=[bxpad, bcw], writes=[bcacc])
            for tap in (1, 2):
                add("dve", (lambda c, ci, tap: lambda e: e.scalar_tensor_tensor(out=cacc[:, c, 0:nt], in0=xpad[:, c, tap:tap + nt],
                                                                                scalar=cw[:, ci + tap:ci + tap + 1], in1=cacc[:, c, 0:nt],
                                                                                op0=ALU.mult, op1=ALU.add))(c, ci, tap),
                    reads=[bxpad, bcw, bcacc], writes=[bcacc])
        add("dve", lambda e: e.tensor_tensor(out=concatT[:, 2:4, 0:nt], in0=PF0[:, 2:4, 0:nt], in1=cacc[:, :, 0:nt], op=ALU.mult),
            reads=[bPB[4], bcacc], writes=[bcc[2], bcc[3]])
        if t == 15:
            add("sp", lambda e: e.dma_start(out=ncp[l], in_=xpad[:, :, nt:nt + 2]), reads=[bxpad], dma=True)
        elif sample:
            add("sp", lambda e: e.dma_start(out=ncs[l], in_=xpad[:, :, nt:nt + 2]), reads=[bxpad], dma=True)
        if t < 15:
            add("act", lambda e: e.activation(out=xpad[:, :, 0:2], in_=xpad[:, :, nt:nt + 2], func=AF.Copy), reads=[bxpad], writes=[bxpad])

        if DBG["stage"] < 6:
            return
        if not sample:
            Stot = 128 * (t + 1)
            if Stot <= 256:
                add("dve", lambda e: e.memset(isc[0:nt, 0:Stot], 0.0), writes=[bisc])
            else:
                for n0 in range(0, Stot, 512):
                    w_ = min(512, Stot - n0)
                    yield from indexer_chunk(nt, n0, w_, kiT, n0, isc, [bkiT[i_] for i_ in range(n0 // 128, (n0 + w_) // 128)], bisc)
            yield from topk_mask(nt, Stot, isc, maskb, bisc, bmaskb, inadm=t * 128 + 64)
            return
        if True:
            for b_ in SAMPLE_AR:
                seed(b_, PROMPT_AR)
            for blk in range(4):
                s0 = blk * 1024
                add("pool", (lambda s0: lambda e: e.dma_start(out=kistage[:, :, :], in_=cki[l, s0:s0 + 1024, :].rearrange("(k p) d -> p k d", p=128)))(s0),
                    writes=[bkistage], dma=True)
                add("dve", lambda e: e.tensor_copy(out=ki4s[:, :, :].rearrange("p k (r d) -> p k r d", r=4),
                                                   in_=kistage[:, :, :].unsqueeze(2).to_broadcast([128, 8, 4, 32])), reads=[bkistage], writes=[bki4s])

                def trk(e):
                    ins = None
                    for k in range(8):
                        ins = e.transpose(PT[:, k * 128:(k + 1) * 128], ki4s[:, k, :], ident[:, :])
                    return ins
                add("pe", trk, reads=[bki4s, bident], writes=[bPT])
                add("act", lambda e: e.activation(out=kiT_s[:, :], in_=PT[:, :], func=AF.Copy), reads=[bPT], writes=[bkiT_s])
                for n0 in (0, 512):
                    yield from indexer_chunk(nt, s0 + n0, 512, kiT_s, n0, isc_s, [bkiT_s], bisc_s)
            yield from indexer_chunk(nt, PAST, TS, kiTn, 0, isc_s, [bkiTn], bisc_s)
            Stot = PAST + TS
            if DBG["ffn"]:
                seed(bF[0], [bWA]); seed(bF[1], [bWA])
                load_ffn_dma(l, 0, 0); load_ffn_dma(l, 1, 1)
                PREF[l] = True
            yield from topk_mask(nt, Stot, isc_s, maskb_s, bisc_s, bmaskb_s, inadm=None)
            for blk in range(4):
                s0 = blk * 1024
                add("pool", (lambda s0: lambda e: e.dma_start(out=kstage[:, :, :], in_=ck[l, s0:s0 + 1024, :].rearrange("(k p) d -> p k d", p=128)))(s0),
                    writes=[bkstage, bv32], dma=True)
                for a_ in range(2):
                    add("pool", (lambda s0, a_: lambda e: e.dma_start(
                        out=Vb_s[:, :, 64 + a_ * 192:64 + a_ * 192 + 128],
                        in_=cv[l, s0:s0 + 1024, a_ * 128:(a_ + 1) * 128].rearrange("(k p) d -> p k d", p=128)))(s0, a_),
                        writes=[bVb_s], dma=True)
                for half in range(2):
                    def trK(e, half=half):
                        ins = None
                        for k in range(4):
                            for gp in range(2):
                                ins = e.transpose(PT[:, (gp * 4 + k) * 128:(gp * 4 + k + 1) * 128],
                                                  kstage[:, half * 4 + k, gp * 128:(gp + 1) * 128], ident[:, :])
                        return ins
                    add("pe", trK, reads=[bkstage, bident], writes=[bPT])
                    add("act", (lambda half: lambda e: e.activation(out=KT_s[:, :, half * 512:(half + 1) * 512],
                                                                    in_=PT[:, :].rearrange("p (g s) -> p g s", g=2), func=AF.Copy))(half),
                        reads=[bPT], writes=[bKT_s])
                tiles = [(kt, 128, s0 + kt * 128) for kt in range(8)]
                for g in range(4):
                    yield from attn_block(nt, g, tiles, KT_s, Vb_s, maskb_s, (lambda kt: [bKT_s, bVb_s]), blk == 0, False, bmaskb_s, qTz, bqT, concatT, bcc)
            for g in range(4):
                yield from attn_block(nt, g, [(0, TS, PAST)], KTn, Vn[:, :].rearrange("p (k c) -> p k c", k=1), maskb_s, (lambda kt: [bKTn, bVn]), False, True, bmaskb_s,
                           qTz, bqT, concatT, bcc)

        _out_proj(t, nt, concatT, bcc)

    def _mixer_B(l, t, nt, sample, qTz, bqT, concatT, bcc, maskb, bmaskb):
        if sample:
            return
        tiles = [(kt, 128, kt * 128) for kt in range(t + 1)]
        for g in range(4):
            yield from attn_block(nt, g, tiles, KT, Vb, maskb, (lambda kt: [bKT[kt], bVb[kt]]), True, True, bmaskb, qTz, bqT, concatT, bcc)
        _out_proj(t, nt, concatT, bcc)

    def _out_proj(t, nt, concatT, bcc):
        if DBG["stage"] < 8:
            return
        for n in range(2):
            def f(e, n=n):
                ins = None
                for c in range(8):
                    ins = e.matmul(PB[n][0:nt, 0:512], lhsT=concatT[:, c, 0:nt], rhs=wout[:, c, n * 512:(n + 1) * 512],
                                   start=(c == 0), stop=(c == 7))
                return ins
            add("pe", f, reads=bcc + [bWB], writes=[bPB[n]])
            add("dve", (lambda n: lambda e: e.tensor_tensor(out=X[0:nt, t, n * 512:(n + 1) * 512], in0=X[0:nt, t, n * 512:(n + 1) * 512],
                                                            in1=PB[n][0:nt, 0:512], op=ALU.add))(n), reads=[bX[t], bPB[n]], writes=[bX[t]])

    def ffn_layer(l, last):
        seed(_ba, [bqkr, bqkbf])
        allh = Buf("hT2seed")
        seed(allh, PROMPT_AR + SAMPLE_AR)
        for t in range(NT):
            bhT2[t].w = None
            bhT2[t].r = list(allh.r)
            norm_to_hT(t, hT2, bhT2[t], t * 128)
        groups = [(0, 512, [0, 1, 2, 3]), (512, 512, [4, 5, 6, 7]), (1024, 512, [8, 9, 10, 11]), (1536, 512, [12, 13, 14, 15]), (2048, TS, [16])]
        for f in range(8):
            slot = f % 3
            wu, wd = ffn_views(slot)
            for gi, (tok0, gw, tl) in enumerate(groups):
                ab = (f * 5 + gi) % 2
                for m in range(4):
                    def fu(e, m=m, gw=gw, wu=wu, tok0=tok0):
                        ins = None
                        for k in range(8):
                            ins = e.matmul(PB[m][:, 0:gw], lhsT=wu[:, k, m * 128:(m + 1) * 128], rhs=hT2[:, k, tok0:tok0 + gw],
                                           start=(k == 0), stop=(k == 7))
                        return ins
                    add("pe", fu, reads=[bF[slot]] + [bhT2[t] for t in tl], writes=[bPB[m]])
                    add("act", (lambda m, gw: lambda e: e.activation(out=rt[m % 2][:, 0:gw], in_=PB[m][:, 0:gw], func=AF.Relu))(m, gw),
                        reads=[bPB[m]], writes=[brt[m % 2]])
                    add("act", (lambda m, gw, ab: lambda e: e.activation(out=actT[ab][:, m, 0:gw], in_=rt[m % 2][:, 0:gw], func=AF.Square))(m, gw, ab),
                        reads=[brt[m % 2]], writes=[bactT[ab]])
                for ti, t in enumerate(tl):
                    nt = nrows(t)
                    for n in range(2):
                        pd = PB[4 + n] if ti % 2 == 0 else (PB7 if n == 1 else PB[4 + n])
                        bpd = bPB[4 + n] if ti % 2 == 0 else (bPB7 if n == 1 else bPB[4 + n])

                        def fd(e, n=n, ti=ti, pd=pd, nt=nt, ab=ab, wd=wd):
                            ins = None
                            for m in range(4):
                                ins = e.matmul(pd[0:nt, 0:512], lhsT=actT[ab][:, m, ti * 128:ti * 128 + nt], rhs=wd[:, m, n * 512:(n + 1) * 512],
                                               start=(m == 0), stop=(m == 3))
                            return ins
                        add("pe", fd, reads=[bactT[ab], bF[slot]], writes=[bpd])
                        add("dve", (lambda n, t, pd, nt: lambda e: e.tensor_tensor(out=X[0:nt, t, n * 512:(n + 1) * 512],
                                                                               in0=X[0:nt, t, n * 512:(n + 1) * 512], in1=pd[0:nt, 0:512],
                                                                               op=ALU.add))(n, t, pd, nt), reads=[bX[t], bpd], writes=[bX[t]])
            if f + 3 < 8:
                load_ffn_chunk(l, f + 3, slot)

    for l in range(depth):
        if l == 0:
            load_mix_weights(0)
        load_layer_params(l)
        for b_ in PROMPT_AR:
            if l > 0:
                seed(b_, bhT2 + SAMPLE_AR)
        if l > 0:
            seed(bqkr, [_ba]); seed(bqkbf, [_ba])
        for c0 in (0, 192, 384):
            add("dve", (lambda c0: lambda e: e.memset(Vb[:, :, c0:c0 + 64], 1.0))(c0), writes=bVb)
            if l == 0:
                add("dve", (lambda c0: lambda e: e.memset(Vn[:, c0:c0 + 64], 1.0))(c0), writes=[bVn])
        npt = min(16, DBG["tiles"])

        def drive(gens):
            gens = list(gens)
            while gens:
                for g_ in list(gens):
                    try:
                        next(g_)
                    except StopIteration:
                        gens.remove(g_)
        if DBG.get("pipe", True):
            if npt > 0:
                drive([mixer_tile(l, 0, "A")])
            for t in range(npt):
                gl_ = []
                if t + 1 < npt:
                    gl_.append(mixer_tile(l, t + 1, "A"))
                gl_.append(mixer_tile(l, t, "B"))
                drive(gl_)
        else:
            for t in range(npt):
                drive([mixer_tile(l, t, "AB")])
        if DBG["sample"]:
            for c0 in (0, 192, 384):
                add("dve", (lambda c0: lambda e: e.memset(Vb_s[:, :, c0:c0 + 64], 1.0))(c0), writes=[bVb_s], reads=[])
            drive([mixer_tile(l, 16, "AB")])
        if PREF.get(l):
            seed(bF[2], [bWB])
            fold_ffn(l, 0, 0); fold_ffn(l, 1, 1)
            load_ffn_chunk(l, 2, 2)
        else:
            seed(bF[0], [bWA]); seed(bF[1], [bWA]); seed(bF[2], [bWB])
            for f in range(3):
                load_ffn_chunk(l, f, f)
        if DBG["ffn"]:
            ffn_layer(l, l == depth - 1)
        if l + 1 < depth:
            seed(bWA, [bF[0], bF[1]]); seed(bWB, [bF[2]])
            load_mix_weights(l + 1)
    for t in range(16):
        add("sp", (lambda t: lambda e: e.dma_start(out=yp[t * 128:(t + 1) * 128, :], in_=X[:, t, :]))(t), reads=[bX[t]], dma=True)
    add("sp", lambda e: e.dma_start(out=ys, in_=X[0:TS, 16, :]), reads=[bX[16]], dma=True)

    S.finalize(nc, st)
    with nc.Block() as block:
        S.emit(block)
    st.close()
    return nc


def _rope_tables():
    tab = np.zeros((128, NT, 96), np.float32)
    for t in range(NT):
        n = 128 if t < 16 else TS
        pos = (np.arange(n) + (t * 128 if t < 16 else PAST)).astype(np.float32)
        for half, off in ((32, 0), (16, 64)):
            inv = (np.float32(10000.0) ** (-np.arange(half, dtype=np.float32) / np.float32(half))).astype(np.float32)
            ang = (pos[:, None] * inv[None, :]).astype(np.float32)
            tab[:n, t, off:off + half] = np.cos(ang).astype(np.float32)
            tab[:n, t, off + half:off + 2 * half] = np.sin(ang).astype(np.float32)
    return tab


def _win_perm():
    qcols = []
    for gp in range(2):
        for j in range(2):
            for gl in range(2):
                h = 2 * (2 * gp + gl) + j
                qcols.extend(range(1280 + h * 64, 1280 + (h + 1) * 64))
    tmc = qcols + list(range(1792, 2048)) + list(range(2048, 2304)) + list(range(256, 512)) + list(range(2304, 2560)) \
        + list(range(2560, 2592)) + list(range(2592, 2600))
    fmc = list(range(0, 256)) + list(range(512, 1280))
    return np.array(tmc + fmc, dtype=np.int64)


_NC_CACHE = {}


def kernel(x_prompt, x_sample, cache_k, cache_v, cache_kidx, state_conv, g_mix, w_in, q_gain, k_gain, w_s, b_s, conv_w,
           w_out, g_ffn, w_up, w_down):
    f = lambda a: np.ascontiguousarray(np.asarray(a, dtype=np.float32))
    x_prompt, x_sample, cache_k, cache_v, cache_kidx, state_conv = map(f, (x_prompt, x_sample, cache_k, cache_v, cache_kidx, state_conv))
    g_mix, w_in, q_gain, k_gain, w_s, b_s, conv_w, w_out, g_ffn, w_up, w_down = map(
        f, (g_mix, w_in, q_gain, k_gain, w_s, b_s, conv_w, w_out, g_ffn, w_up, w_down))
    if "nc" not in _NC_CACHE:
        _NC_CACHE["nc"] = build_program()
    nc = _NC_CACHE["nc"]
    w_in_p = np.ascontiguousarray(w_in[:, :, _win_perm()])
    gvec = np.ascontiguousarray(np.concatenate([g_mix.reshape(2, 8, 128), g_ffn.reshape(2, 8, 128)], 0).reshape(32, 128).T)
    wsT = np.ascontiguousarray(w_s.transpose(0, 3, 1, 2))
    cwv = np.ascontiguousarray(conv_w.reshape(2, 3, 2, 128).transpose(3, 0, 2, 1).reshape(128, 12))
    rope = _rope_tables()
    p2 = (2.0 ** -(np.arange(32, dtype=np.float64) + 1)).astype(np.float32).reshape(1, 32)
    shared = dict(w_in=w_in_p, w_out=w_out, w_up=w_up, w_down=w_down, gvec=gvec, wsT=wsT, bs=b_s, cwv=cwv, qg=q_gain, kg=k_gain,
                  rope=rope, p2=p2)
    in_maps = []
    for c in range(8):
        m = dict(shared)
        m["xp"] = x_prompt[c]
        m["xs"] = x_sample[c]
        m["ck"] = np.ascontiguousarray(cache_k[:, c].reshape(2, PAST, 256))
        m["cv"] = np.ascontiguousarray(cache_v[:, c].reshape(2, PAST, 256))
        m["cki"] = np.ascontiguousarray(cache_kidx[:, c])
        m["sconvT"] = np.ascontiguousarray(state_conv[:, c].reshape(2, 2, 2, 128).transpose(0, 3, 2, 1))
        in_maps.append(m)
    res = run_bass_kernel_spmd(nc, in_maps, core_ids=list(range(8)))
    R = res.results
    st_ = lambda k: np.stack([np.asarray(R[c][k]) for c in range(8)])
    yp = st_("yp"); ys = st_("ys")
    nkp = st_("nkp").transpose(1, 0, 2, 3).reshape(2, 8, SEQ, 4, 64)
    nvp = st_("nvp").transpose(1, 0, 2, 3).reshape(2, 8, SEQ, 4, 64)
    nkip = st_("nkip").transpose(1, 0, 2, 3)
    cvt = lambda a: a.transpose(1, 0, 4, 3, 2).reshape(2, 8, 2, 256)
    ncp = cvt(st_("ncp")); ncs = cvt(st_("ncs"))
    nks = st_("nks").transpose(1, 0, 2, 3).reshape(2, 8, TS, 4, 64)
    nvs = st_("nvs").transpose(1, 0, 2, 3).reshape(2, 8, TS, 4, 64)
    nkis = st_("nkis").transpose(1, 0, 2, 3)
    nsgv = st_("nsgv").transpose(1, 0, 2, 3)
    out = (yp, ys, nkp, nvp, nkip, ncp, nks, nvs, nkis, ncs, nsgv)
    return tuple(np.ascontiguousarray(o, dtype=np.float32) for o in out)
```

```python
import contextlib
import numpy as np
import concourse.bass as bass
import concourse.mybir as mybir
from concourse.bass_utils import run_bass_kernel_spmd

F32 = mybir.dt.float32
BF16 = mybir.dt.bfloat16
ALU = mybir.AluOpType
AF = mybir.ActivationFunctionType
AX = mybir.AxisListType

D = 1024
SEQ = 2048
NT = 17
TS = 32
PAST = 4096
DEPTH = 2
INW = 2600
NIT = 15
EPS = 1e-6
NEG = -1.0e30
TMW = 1576
GC = 0.7978845608028654


class Buf:
    __slots__ = ("name", "w", "r", "xr")

    def __init__(self, name="b", xr=False):
        self.name = name
        self.w = None
        self.r = []
        self.xr = xr


def seed(new, olds):
    lst = []
    for o in olds:
        if o.w is not None:
            lst.append(o.w)
        lst.extend(o.r)
    new.w = None
    new.r = lst


class Op:
    __slots__ = ("eng", "fn", "deps", "signal", "dma", "semi", "val", "lane")


class Sched:
    ENGS = ("pe", "act", "dve", "pool", "sp")
    LIMIT = 30000
    NLANES = 8

    def __init__(self):
        self.ops = {e: [] for e in self.ENGS}

    def add(self, eng, fn, reads=(), writes=(), dma=False):
        op = Op()
        op.eng = eng
        op.fn = fn
        op.signal = False
        op.dma = dma
        deps = {}
        for b in reads:
            if b.w is not None:
                deps[id(b.w)] = (b.w, True)
            if b.xr:
                for r in b.r:
                    if r.eng != eng and id(r) not in deps:
                        deps[id(r)] = (r, False)
        for b in writes:
            if b.w is not None and id(b.w) not in deps:
                deps[id(b.w)] = (b.w, False)
            for r in b.r:
                if id(r) not in deps:
                    deps[id(r)] = (r, False)
        op.deps = []
        for d, raw in deps.values():
            if d is op:
                continue
            if not d.dma and not dma and d.eng == eng:
                if eng == "pe":
                    continue
            if not d.dma:
                d.signal = True
            op.deps.append(d)
        for b in reads:
            b.r.append(op)
        for b in writes:
            b.w = op
            b.r = []
        self.ops[eng].append(op)
        return op

    def finalize(self, nc, stack):
        self.sems = {}
        self.final_dma = []
        keys = set()
        for e in self.ENGS:
            cnt = 0
            nd = 0
            lane_cnt = [0] * self.NLANES
            nl = 2 if e == "pool" else self.NLANES
            for op in self.ops[e]:
                if op.dma:
                    op.lane = nd % nl
                    nd += 1
                    lane_cnt[op.lane] += 1
                    op.val = 16 * lane_cnt[op.lane]
                    op.semi = ("dma", e, op.lane)
                    keys.add(op.semi)
                elif op.signal:
                    cnt += 1
                    op.semi = ("c", e, (cnt - 1) // self.LIMIT)
                    op.val = (cnt - 1) % self.LIMIT + 1
                    keys.add(op.semi)
            for ln in range(self.NLANES):
                if lane_cnt[ln]:
                    self.final_dma.append((("dma", e, ln), 16 * lane_cnt[ln]))
        for k in sorted(keys, key=str):
            self.sems[k] = stack.enter_context(nc.semaphore("s_" + "_".join(map(str, k))))

    def emit(self, block):
        def run(engname):
            def body(eng):
                seen = {}
                for op in self.ops[engname]:
                    waits = [(d.semi, d.val) for d in op.deps]
                    if op.dma and op.val > 16:
                        waits.append((op.semi, op.val - 16))
                    for k, v in waits:
                        if seen.get(k, 0) < v:
                            eng.wait_ge(self.sems[k], v)
                            seen[k] = v
                    ins = op.fn(eng)
                    if op.dma:
                        ins.then_inc(self.sems[op.semi], 16)
                    elif op.signal:
                        ins.then_inc(self.sems[op.semi], 1)
                if engname == "sp":
                    for k, v in self.final_dma:
                        if seen.get(k, 0) < v:
                            eng.wait_ge(self.sems[k], v)
            return body
        block.tensor(run("pe"))
        block.scalar(run("act"))
        block.vector(run("dve"))
        block.gpsimd(run("pool"))
        block.sync(run("sp"))


class _Stop(Exception):
    pass


DBG = {"tiles": NT, "stage": 99, "ffn": True, "depth": DEPTH, "sample": True, "tm": True, "fm": True}


def build_program(depth=DEPTH):
    depth = min(depth, DBG["depth"])
    nc = bass.Bass("TRN2", target_bir_lowering=False, dynamic_dma_scratch_size=6144)
    S = Sched()

    def din(name, shape):
        return nc.dram_tensor(name, list(shape), F32, kind="ExternalInput").ap()

    def dout(name, shape):
        return nc.dram_tensor(name, list(shape), F32, kind="ExternalOutput").ap()

    xp = din("xp", [SEQ, D]); xs = din("xs", [TS, D])
    ck = din("ck", [DEPTH, PAST, 256]); cv = din("cv", [DEPTH, PAST, 256]); cki = din("cki", [DEPTH, PAST, 32])
    sconvT = din("sconvT", [DEPTH, 128, 2, 2])
    w_in = din("w_in", [DEPTH, D, INW]); w_out = din("w_out", [DEPTH, D, D])
    w_up = din("w_up", [DEPTH, D, 4 * D]); w_down = din("w_down", [DEPTH, 4 * D, D])
    gvec = din("gvec", [128, 32]); wsT = din("wsT", [DEPTH, 128, 4, 128]); bs = din("bs", [DEPTH, 4, 128])
    cwv = din("cwv", [128, 12]); qg = din("qg", [DEPTH, 64]); kg = din("kg", [DEPTH, 64])
    rope = din("rope", [128, NT, 96]); p2 = din("p2", [1, 32])
    yp = dout("yp", [SEQ, D]); ys = dout("ys", [TS, D])
    nkp = dout("nkp", [DEPTH, SEQ, 256]); nvp = dout("nvp", [DEPTH, SEQ, 256]); nkip = dout("nkip", [DEPTH, SEQ, 32])
    ncp = dout("ncp", [DEPTH, 128, 2, 2])
    nks = dout("nks", [DEPTH, TS, 256]); nvs = dout("nvs", [DEPTH, TS, 256]); nkis = dout("nkis", [DEPTH, TS, 32])
    ncs = dout("ncs", [DEPTH, 128, 2, 2]); nsgv = dout("nsgv", [DEPTH, TS, 256])

    st = contextlib.ExitStack()
    cnt = [0]

    def sb(shape, dt=F32, name=None):
        cnt[0] += 1
        return st.enter_context(nc.sbuf_tensor(name or f"sb{cnt[0]}", list(shape), dt))

    def ps(shape, dt=F32):
        cnt[0] += 1
        return st.enter_context(nc.psum_tensor(f"ps{cnt[0]}", list(shape), dt))

    X = sb([128, NT, D], F32, "X")
    WA = sb([128, 8 * INW], BF16, "WA")
    WB = sb([128, 8 * D], BF16, "WB")
    AR = sb([128, 21504], BF16, "AR")
    SA = sb([128, 1024], F32, "SA"); SB_ = sb([128, 1024], F32, "SBs"); SC = sb([128, 1024], F32, "SC")
    hbf = sb([128, 1024], BF16, "hbf")
    hT = sb([128, 8, 128], BF16, "hT")
    P12 = sb([128, 2304], BF16, "P12")
    qkr = P12[:, 0:1536].bitcast(F32); qkbf = P12[:, 1536:2304]
    v32 = SB_[:, 768:1024]; vbf = sb([128, 256], BF16, "vbf"); sgv = SC[:, 768:1024]
    qki = sb([128, 288], F32, "qki"); qkir = sb([128, 288], F32, "qkir"); qibf = sb([128, 256], BF16, "qibf")
    ki4 = sb([128, 128], BF16, "ki4")
    w8 = sb([128, 8], F32, "w8")
    xpad = sb([128, 2, 130], F32, "xpad"); hin = sb([128, 2, 128], F32, "hin"); cacc = sb([128, 2, 128], F32, "cacc")
    ccT = [sb([128, 8, 128], BF16, f"concatT{i}") for i in range(2)]
    qTzs = [sb([128, 2, 2, 2, 128], BF16, f"qTz{i}") for i in range(2)]; qiT = sb([128, 2, 128], BF16, "qiT")
    rt = [sb([128, 512], F32, f"rt{i}") for i in range(2)]
    pT = [sb([128, 256], BF16, f"pT{i}") for i in range(2)]
    rec = sb([128, 256], F32, "rec")
    accs = SC[:, 0:256].rearrange("p (g c) -> p g c", g=4)
    actT0 = P12[:, 0:2048].rearrange("p (m s) -> p m s", m=4)
    actT = [actT0, actT0]
    ident = sb([128, 128], BF16, "ident")
    ropeB = [sb([128, 96], F32, f"ropeB{i}") for i in range(2)]
    gv = sb([128, 32], F32, "gv"); cw = sb([128, 12], F32, "cw")
    wsf = SC[:, 0:512]; wmT = sb([128, 4, 128], BF16, "wmT")
    bP = sb([128, 2, 128], F32, "bP"); gqk = sb([128, 128], F32, "gqk")
    p2b = sb([128, 32], F32, "p2b"); nh = sb([128, 12], F32, "nh")
    sm = sb([128, 64], F32, "sm")
    steps = sb([128, 32], F32, "steps"); nsteps = sb([128, 32], F32, "nsteps")
    KTn = sb([128, 2, 32], BF16, "KTn"); Vn = sb([128, 448], BF16, "Vn"); kiTn = sb([128, 32], BF16, "kiTn")
    kstage = SB_[:, :].bitcast(BF16).rearrange("p (k c) -> p k c", k=8)
    SAb = SA[:, :].bitcast(BF16)
    ki4s = SAb[:, 0:1024].rearrange("p (k c) -> p k c", k=8)
    kistage = SAb[:, 1024:1280].rearrange("p (k c) -> p k c", k=8)

    KT = AR[:, 0:4096].rearrange("p (g s) -> p g s", g=2)
    Vb = AR[:, 4096:11264].rearrange("p (k c) -> p k c", k=16)
    kiT = AR[:, 11264:13312]
    isc = AR[:, 13312:17408].bitcast(F32)
    maskbs = [AR[:, 17408:19456], AR[:, 19456:21504]]
    isc_s = AR[:, 0:8256].bitcast(F32)
    maskb_s = AR[:, 8256:12384]
    KT_s = AR[:, 12384:14432].rearrange("p (g s) -> p g s", g=2)
    Vb_s = AR[:, 14432:18016].rearrange("p (k c) -> p k c", k=8)
    kiT_s = AR[:, 18016:19040]
    hT2 = AR[:, 0:16640].rearrange("p (k s) -> p k s", k=8)

    win = WA[:, :].rearrange("p (k n) -> p k n", k=8)
    wout = WB[:, :].rearrange("p (k n) -> p k n", k=8)
    FSL = [(WA, 0), (WA, 8192), (WB, 0)]

    PB = [ps([128, 512]) for _ in range(6)]
    PT = ps([128, 1024], BF16)
    PB7 = ps([128, 512])
    bPB = [Buf(f"PB{i}", xr=True) for i in range(6)]
    bPT = Buf("PT", xr=True)
    bPB7 = Buf("PB7", xr=True)

    bX = [Buf(f"X{t}") for t in range(NT)]
    bWA, bWB = Buf("WA"), Buf("WB")
    bF = [Buf("FA"), Buf("FB"), Buf("FC")]
    bSA, bSB, bSC = Buf("SA"), Buf("SB"), Buf("SC")
    bhbf, bhT = Buf("hbf"), Buf("hT")
    bqkr, bqkbf, bv32, bvbf = Buf(), Buf(), Buf(), Buf()
    bsgv = bSC
    bqki, bqkir, bqibf, bki4, bw8 = Buf(), Buf(), Buf(), Buf(), Buf()
    bxpad, bhin, bcacc = Buf("xpad"), Buf(), Buf()
    bccs = [[Buf(f"cc{i}_{c}") for c in range(8)] for i in range(2)]
    bqTs = [Buf("qT0"), Buf("qT1")]
    bqiT = Buf("qiT")
    brt = [Buf(), Buf()]
    bpT = [Buf(), Buf()]
    brec = Buf()
    baccs = bSC
    _ba = Buf()
    bactT = [_ba, _ba]
    bident, bgv, bcw, bwmT, bbP, bgqk, bp2b, bnh = [Buf() for _ in range(8)]
    bwsf = bSC
    bropeB = [Buf(), Buf()]
    bss, brstd, bss12, br12, brmax, brmin, bw0, bmid, bcnt, btq, bthr, bsteps = [Buf() for _ in range(12)]
    bKT = [Buf(f"KT{t}") for t in range(16)]; bVb = [Buf(f"Vb{t}") for t in range(16)]; bkiT = [Buf(f"kiT{t}") for t in range(16)]
    bisc = Buf("isc"); bmaskbs = [Buf("maskb0"), Buf("maskb1")]
    bisc_s, bmaskb_s, bKT_s, bVb_s, bkiT_s = Buf(), Buf(), Buf(), Buf(), Buf()
    bKTn, bVn, bkiTn = Buf(), Buf(), Buf()
    bkstage, bkistage, bki4s = bSB, bSA, bSA
    bhT2 = [Buf(f"hT2_{t}") for t in range(NT)]
    PROMPT_AR = bKT + bVb + bkiT + [bisc] + bmaskbs
    SAMPLE_AR = [bisc_s, bmaskb_s, bKT_s, bVb_s, bkiT_s]

    ss = sm[:, 0:1]; msq = sm[:, 1:2]; rstd = sm[:, 2:3]
    ss12 = sm[:, 4:16]; ms12 = sm[:, 16:28]; r12 = sm[:, 28:40]
    rmax = sm[:, 40:41]; rmin = sm[:, 41:42]; w0 = sm[:, 42:43]; mid = sm[:, 43:44]
    cntc = sm[:, 44:45]; tq = sm[:, 45:46]; thr = sm[:, 46:47]

    add = S.add

    def nrows(t):
        return 128 if t < 16 else TS

    add("pool", lambda e: e.memset(ident[:], 0.0), writes=[bident])
    add("pool", lambda e: e.affine_select(out=ident[:], in_=ident[:], pattern=[[-1, 128]], compare_op=ALU.not_equal,
                                          fill=1.0, base=0, channel_multiplier=1), reads=[bident], writes=[bident])
    add("pool", lambda e: e.memset(nh[:], -0.5), writes=[bnh])
    for i_ in range(2):
        add("pool", (lambda i_: lambda e: e.memset(qTzs[i_][:].rearrange("p a b c d -> p (a b c d)"), 0.0))(i_), writes=[bqTs[i_]])
    add("sp", lambda e: e.dma_start(out=gv[:], in_=gvec), writes=[bgv], dma=True)
    add("sp", lambda e: e.dma_start(out=cw[:], in_=cwv), writes=[bcw], dma=True)
    add("sp", lambda e: e.dma_start(out=p2b[:], in_=p2.partition_broadcast(128)), writes=[bp2b], dma=True)
    for t in range(16):
        add("sp", (lambda t: lambda e: e.dma_start(out=X[:, t, :], in_=xp[t * 128:(t + 1) * 128, :]))(t), writes=[bX[t]], dma=True)
    add("sp", lambda e: e.dma_start(out=X[0:TS, 16, :], in_=xs), writes=[bX[16]], dma=True)

    def load_mix_weights(l):
        src = w_in[l].rearrange("(k p) n -> p k n", p=128)
        add("pool", lambda e: e.dma_start(out=win[:, :, 0:1300], in_=src[:, :, 0:1300]), writes=[bWA], dma=True)
        add("pool", lambda e: e.dma_start(out=win[:, :, 1300:2600], in_=src[:, :, 1300:2600]), writes=[bWA], dma=True)
        for k in range(8):
            add("dve", (lambda k: lambda e: e.tensor_scalar(out=win[:, k, :], in0=win[:, k, :], scalar1=gv[:, l * 8 + k:l * 8 + k + 1],
                                                            scalar2=None, op0=ALU.mult))(k), reads=[bWA, bgv], writes=[bWA])
        src2 = w_out[l].rearrange("(k p) n -> p k n", p=128)
        add("pool", lambda e: e.dma_start(out=wout[:, :, :], in_=src2), writes=[bWB], dma=True)

    def ffn_views(slot):
        tns, off = FSL[slot]
        wu = tns[:, off:off + 4096].rearrange("p (k n) -> p k n", k=8)
        wd = tns[:, off + 4096:off + 8192].rearrange("p (m n) -> p m n", m=4)
        return wu, wd

    def load_ffn_dma(l, f, slot):
        wu, wd = ffn_views(slot)
        srcu = w_up[l].rearrange("(k p) n -> p k n", p=128)[:, :, f * 512:(f + 1) * 512]
        srcd = w_down[l, f * 512:(f + 1) * 512, :].rearrange("(m p) n -> p m n", p=128)
        b = bF[slot]
        add("pool", lambda e: e.dma_start(out=wu, in_=srcu), writes=[b], dma=True)
        add("pool", lambda e: e.dma_start(out=wd, in_=srcd), writes=[b], dma=True)

    def fold_ffn(l, f, slot):
        wu, wd = ffn_views(slot)
        b = bF[slot]
        for k in range(8):
            add("dve", (lambda k: lambda e: e.tensor_scalar(out=wu[:, k, :], in0=wu[:, k, :],
                                                            scalar1=gv[:, 16 + l * 8 + k:16 + l * 8 + k + 1], scalar2=None,
                                                            op0=ALU.mult))(k), reads=[b, bgv], writes=[b])

    def load_ffn_chunk(l, f, slot):
        load_ffn_dma(l, f, slot)
        fold_ffn(l, f, slot)

    PREF = {}

    def load_layer_params(l):
        add("sp", lambda e: e.dma_start(out=wsf.rearrange("p (h i) -> p h i", h=4), in_=wsT[l]), writes=[bwsf], dma=True)
        add("dve", lambda e: e.tensor_copy(out=wmT[:].rearrange("p h i -> p (h i)"), in_=wsf), reads=[bwsf], writes=[bwmT])
        add("dve", lambda e: e.memset(wmT[64:128, :, 0:64], 0.0), writes=[bwmT])
        for c in range(2):
            for hh in range(2):
                add("sp", (lambda c, hh: lambda e: e.dma_start(out=bP[hh * 64:(hh + 1) * 64, c, :],
                                                               in_=bs[l, 2 * c + hh:2 * c + hh + 1, :].partition_broadcast(64)))(c, hh),
                    writes=[bbP], dma=True)
        add("sp", lambda e: e.dma_start(out=gqk[:, 0:64], in_=qg[l:l + 1, :].partition_broadcast(128)), writes=[bgqk], dma=True)
        add("sp", lambda e: e.dma_start(out=gqk[:, 64:128], in_=kg[l:l + 1, :].partition_broadcast(128)), writes=[bgqk], dma=True)

    def norm_to_hT(t, dstT, bdst, col0):
        nt = nrows(t)
        xt = X[0:nt, t, :]
        add("act", lambda e: e.activation(out=hbf[0:nt, :], in_=xt, func=AF.Square, accum_out=ss[0:nt, :]),
            reads=[bX[t]], writes=[bhbf, bss])
        add("dve", lambda e: e.tensor_scalar(out=msq[0:nt, :], in0=ss[0:nt, :], scalar1=1.0 / D, scalar2=EPS, op0=ALU.mult, op1=ALU.add),
            reads=[bss], writes=[brstd])
        add("pool", lambda e: e.tensor_tensor(out=rstd[0:nt, :], in0=msq[0:nt, :], in1=nh[0:nt, 0:1], op=ALU.pow),
            reads=[brstd, bnh], writes=[brstd])
        add("act", lambda e: e.activation(out=hbf[0:nt, :], in_=xt, func=AF.Copy, scale=rstd[0:nt, :]),
            reads=[bX[t], brstd], writes=[bhbf])

        def tr(e):
            ins = None
            for k in range(8):
                ins = e.transpose(PT[:, k * 128:k * 128 + nt], hbf[0:nt, k * 128:(k + 1) * 128], ident[0:nt, 0:nt])
            return ins
        add("pe", tr, reads=[bhbf, bident], writes=[bPT])
        add("dve", lambda e: e.tensor_copy(out=dstT[:, :, col0:col0 + nt],
                                           in_=PT[:, :].rearrange("p (k s) -> p k s", k=8)[:, :, 0:nt]),
            reads=[bPT], writes=[bdst])

    def rope_ops(src, dst, nt, nh_, half, cos, sin, bsrc, bdst, brope):
        s3 = src.rearrange("p (h d) -> p h d", h=nh_)
        d3 = dst.rearrange("p (h d) -> p h d", h=nh_)
        x1 = s3[:, :, 0:half]; x2 = s3[:, :, half:2 * half]
        cb = cos.unsqueeze(1).to_broadcast([nt, nh_, half])
        sbb = sin.unsqueeze(1).to_broadcast([nt, nh_, half])
        ta = SA[0:nt, 0:nh_ * half].rearrange("p (h d) -> p h d", h=nh_)
        tb = SA[0:nt, 512:512 + nh_ * half].rearrange("p (h d) -> p h d", h=nh_)
        add("dve", lambda e: e.tensor_tensor(out=ta, in0=x1, in1=cb, op=ALU.mult), reads=[bsrc, brope], writes=[bSA])
        add("dve", lambda e: e.tensor_tensor(out=tb, in0=x2, in1=sbb, op=ALU.mult), reads=[bsrc, brope], writes=[bSA])
        add("dve", lambda e: e.tensor_tensor(out=d3[:, :, 0:half], in0=ta, in1=tb, op=ALU.subtract), reads=[bSA], writes=[bdst])
        add("dve", lambda e: e.tensor_tensor(out=ta, in0=x2, in1=cb, op=ALU.mult), reads=[bsrc, brope], writes=[bSA])
        add("dve", lambda e: e.tensor_tensor(out=tb, in0=x1, in1=sbb, op=ALU.mult), reads=[bsrc, brope], writes=[bSA])
        add("dve", lambda e: e.tensor_tensor(out=d3[:, :, half:2 * half], in0=ta, in1=tb, op=ALU.add), reads=[bSA], writes=[bdst])

    def gelu_core(src, T, bsrc_list, bT):
        add("act", lambda e: e.activation(out=T, in_=src, func=AF.Square), reads=bsrc_list, writes=[bT])
        add("dve", lambda e: e.tensor_scalar(out=T, in0=T, scalar1=0.044715, scalar2=1.0, op0=ALU.mult, op1=ALU.add), reads=[bT], writes=[bT])
        add("dve", lambda e: e.tensor_tensor(out=T, in0=T, in1=src, op=ALU.mult), reads=[bT] + bsrc_list, writes=[bT])
        add("act", lambda e: e.activation(out=T, in_=T, func=AF.Tanh, scale=GC), reads=[bT], writes=[bT])
        add("dve", lambda e: e.scalar_tensor_tensor(out=T, in0=T, scalar=1.0, in1=src, op0=ALU.add, op1=ALU.mult),
            reads=[bT] + bsrc_list, writes=[bT])

    VOFF = [0, 128, 192, 320]
    O_ROWS = [64, 0, 64, 0]

    def attn_block(nt, g, tiles, KTsrc, Vsrc, mask_ap, bk_fn, first, last_norm, bmask, qTz, bqT, concatT, bcc):
        gp, gl = g // 2, g % 2
        scb = PB[4]
        bscb = bPB[4]
        PO = PB[5]
        n2 = 2 * nt
        ntiles = len(tiles)
        for i, (kt, ks, mc0) in enumerate(tiles):
            half = (i % 2) * 256
            pti = i % 2

            def sc_fn(e, kt=kt, ks=ks, mc0=mc0, half=half):
                e.matmul(scb[0:ks, half:half + n2].rearrange("p (j t) -> p j t", j=2),
                         lhsT=KTsrc[:, gp, kt * 128:kt * 128 + ks],
                         rhs=qTz[:, gl, gp, :, 0:nt], start=True, stop=False)
                ins = None
                for j in range(2):
                    ins = e.matmul(scb[0:ks, half + j * nt:half + (j + 1) * nt], lhsT=mask_ap[0:nt, mc0:mc0 + ks],
                                   rhs=ident[0:nt, 0:nt], start=False, stop=(j == 1))
                return ins
            add("pe", sc_fn, reads=[bqT, bmask, bident] + bk_fn(kt), writes=[bscb])
            add("act", (lambda ks, half, pti: lambda e: e.activation(out=pT[pti][0:ks, 0:n2], in_=scb[0:ks, half:half + n2],
                                                                      func=AF.Exp, scale=0.125))(ks, half, pti),
                reads=[bscb], writes=[bpT[pti]])
            add("pe", (lambda kt, ks, pti, i: lambda e: e.matmul(PO[:, 0:n2], lhsT=Vsrc[0:ks, kt, VOFF[g]:VOFF[g] + 128],
                                                                 rhs=pT[pti][0:ks, 0:n2], start=(i == 0), stop=(i == ntiles - 1)))(kt, ks, pti, i),
                reads=[bpT[pti]] + bk_fn(kt), writes=[bPB[5]])
            yield
        if last_norm and first:
            src_all = PO
            bsrc = bPB[5]
        else:
            if first:
                add("act", lambda e: e.activation(out=accs[:, g, 0:n2], in_=PO[:, 0:n2], func=AF.Copy), reads=[bPB[5]], writes=[baccs])
            else:
                add("dve", lambda e: e.tensor_tensor(out=accs[:, g, 0:n2], in0=accs[:, g, 0:n2], in1=PO[:, 0:n2], op=ALU.add),
                    reads=[bPB[5], baccs], writes=[baccs])
            src_all = accs[:, g, :]
            bsrc = baccs
        if last_norm:
            orow = O_ROWS[g]
            srow = 64 - orow
            add("dve", lambda e: e.reciprocal(out=rec[orow:orow + 64, 0:n2], in_=src_all[srow:srow + 64, 0:n2]), reads=[bsrc], writes=[brec])
            for j in range(2):
                add("dve", (lambda j: lambda e: e.tensor_tensor(out=concatT[64 * j:64 * j + 64, 4 + g, 0:nt],
                                                                in0=src_all[orow:orow + 64, j * nt:(j + 1) * nt],
                                                                in1=rec[orow:orow + 64, j * nt:(j + 1) * nt], op=ALU.mult))(j),
                    reads=[bsrc, brec], writes=[bcc[4 + g]])

    def indexer_chunk(nt, n0, w, kiTsrc, kcol0, isc_ap, bki_list, bisc_):
        for h in range(8):
            r, c = h % 4, h // 4
            add("pe", (lambda r, c: lambda e: e.matmul(PB[r][0:nt, 0:w], lhsT=qiT[32 * r:32 * r + 32, c, 0:nt],
                                                       rhs=kiTsrc[32 * r:32 * r + 32, kcol0:kcol0 + w], start=True, stop=True,
                                                       tile_position=(32 * r, 0)))(r, c),
                reads=[bqiT] + bki_list, writes=[bPB[r]])
            add("act", (lambda r, h: lambda e: e.activation(out=rt[h % 2][0:nt, 0:w], in_=PB[r][0:nt, 0:w], func=AF.Relu))(r, h),
                reads=[bPB[r]], writes=[brt[h % 2]])
            if h == 0:
                add("dve", lambda e: e.tensor_scalar(out=isc_ap[0:nt, n0:n0 + w], in0=rt[0][0:nt, 0:w], scalar1=w8[0:nt, 0:1],
                                                     scalar2=None, op0=ALU.mult), reads=[brt[0], bw8], writes=[bisc_])
            else:
                add("dve", (lambda h: lambda e: e.scalar_tensor_tensor(out=isc_ap[0:nt, n0:n0 + w], in0=rt[h % 2][0:nt, 0:w],
                                                                       scalar=w8[0:nt, h:h + 1], in1=isc_ap[0:nt, n0:n0 + w],
                                                                       op0=ALU.mult, op1=ALU.add))(h),
                    reads=[brt[h % 2], bw8, bisc_], writes=[bisc_])
            yield

    def topk_mask(nt, Stot, isc_ap, mask_ap, bisc_, bmask, inadm=None):
        iv = isc_ap[0:nt, 0:Stot]
        yield
        if Stot > 256:
            add("dve", lambda e: e.tensor_reduce(out=rmax[0:nt, :], in_=iv, axis=AX.X, op=ALU.max), reads=[bisc_], writes=[brmax])
            add("dve", lambda e: e.tensor_reduce(out=rmin[0:nt, :], in_=iv, axis=AX.X, op=ALU.min), reads=[bisc_], writes=[brmin])
        if inadm is not None:
            add("dve", lambda e: e.memset(isc_ap[0:64, inadm:inadm + 64], NEG), writes=[bisc_])
        if Stot <= 256:
            add("dve", lambda e: e.memset(thr[0:nt, :], 0.5 * NEG), writes=[bthr])
        else:
            add("dve", lambda e: e.tensor_tensor(out=w0[0:nt, :], in0=rmax[0:nt, :], in1=rmin[0:nt, :], op=ALU.subtract),
                reads=[brmax, brmin], writes=[bw0])
            add("dve", lambda e: e.tensor_scalar(out=steps[0:nt, :], in0=p2b[0:nt, :], scalar1=w0[0:nt, :], scalar2=None, op0=ALU.mult),
                reads=[bp2b, bw0], writes=[bsteps])
            add("dve", lambda e: e.tensor_scalar(out=nsteps[0:nt, :], in0=steps[0:nt, :], scalar1=-1.0, scalar2=None, op0=ALU.mult),
                reads=[bsteps], writes=[bsteps])
            add("dve", lambda e: e.tensor_scalar(out=mid[0:nt, :], in0=rmin[0:nt, :], scalar1=steps[0:nt, 0:1], scalar2=-1.0,
                                                 op0=ALU.add, op1=ALU.mult), reads=[brmin, bsteps], writes=[bmid])
            for n in range(NIT):
                add("act", lambda e: e.activation(out=mask_ap[0:nt, 0:Stot], in_=iv, func=AF.Sign, bias=mid[0:nt, :],
                                                  accum_out=cntc[0:nt, :]), reads=[bisc_, bmid], writes=[bmask, bcnt])
                add("act", lambda e: e.activation(out=tq[0:nt, :], in_=cntc[0:nt, :], func=AF.Sign, bias=float(Stot - 511)),
                    reads=[bcnt], writes=[btq])
                add("act", (lambda n: lambda e: e.activation(out=mid[0:nt, :], in_=tq[0:nt, :], func=AF.Identity,
                                                             scale=nsteps[0:nt, n + 1:n + 2], bias=mid[0:nt, :]))(n),
                    reads=[btq, bsteps, bmid], writes=[bmid])
                yield
            add("dve", lambda e: e.tensor_scalar(out=thr[0:nt, :], in0=mid[0:nt, :], scalar1=-1.0, scalar2=steps[0:nt, NIT:NIT + 1],
                                                 op0=ALU.mult, op1=ALU.subtract), reads=[bmid, bsteps], writes=[bthr])
        add("dve", lambda e: e.tensor_scalar(out=mask_ap[0:nt, 0:Stot], in0=iv, scalar1=thr[0:nt, :], scalar2=-30000.0,
                                             op0=ALU.is_lt, op1=ALU.mult), reads=[bisc_, bthr], writes=[bmask])

    def mixer_tile(l, t, part="AB"):
        nt = nrows(t)
        sample = (t == 16)
        par = t % 2
        qTz, bqT, concatT, bcc = qTzs[par], bqTs[par], ccT[par], bccs[par]
        maskb, bmaskb = maskbs[par], bmaskbs[par]
        if "A" in part:
            yield from _mixer_A(l, t, nt, sample, qTz, bqT, concatT, bcc, maskb, bmaskb)
        if "B" in part:
            yield from _mixer_B(l, t, nt, sample, qTz, bqT, concatT, bcc, maskb, bmaskb)

    def _mixer_A(l, t, nt, sample, qTz, bqT, concatT, bcc, maskb, bmaskb):
        rT = ropeB[t % 2]; brT = bropeB[t % 2]
        add("sp", lambda e: e.dma_start(out=rT[:, :], in_=rope[:, t, :]), writes=[brT], dma=True)
        norm_to_hT(t, hT, bhT, 0)
        if DBG["stage"] < 2:
            return
        tm = [(0, 512), (512, 512), (1024, 512), (1536, 40)]
        for n, (c0, w) in enumerate(tm):
            if not DBG["tm"]:
                break

            def f(e, c0=c0, w=w, n=n):
                ins = None
                for k in range(8):
                    ins = e.matmul(PB[n][0:nt, 0:w], lhsT=hT[:, k, 0:nt], rhs=win[:, k, c0:c0 + w], start=(k == 0), stop=(k == 7))
                return ins
            add("pe", f, reads=[bhT, bWA], writes=[bPB[n]])
        for fi in range(8):
            if not DBG["fm"]:
                break
            bank = PB[4] if fi < 4 else PB[5]
            bb = bPB[4] if fi < 4 else bPB[5]

            def f(e, fi=fi, bank=bank):
                ins = None
                for k in range(8):
                    ins = e.matmul(bank[:, (fi % 4) * 128:(fi % 4) * 128 + nt], lhsT=win[:, k, TMW + fi * 128:TMW + (fi + 1) * 128],
                                   rhs=hT[:, k, 0:nt], start=(k == 0), stop=(k == 7))
                return ins
            add("pe", f, reads=[bhT, bWA], writes=[bb])
        PF0 = PB[4][:, :].rearrange("p (f s) -> p f s", f=4)
        PF1 = PB[5][:, :].rearrange("p (f s) -> p f s", f=4)
        if DBG["stage"] < 3:
            return
        add("act", lambda e: e.activation(out=SB_[0:nt, 0:512], in_=PB[0][0:nt, 0:512], func=AF.Square), reads=[bPB[0]], writes=[bSB])
        add("act", lambda e: e.activation(out=SB_[0:nt, 512:768], in_=PB[1][0:nt, 0:256], func=AF.Square), reads=[bPB[1]], writes=[bSB])
        add("dve", lambda e: e.tensor_reduce(out=ss12[0:nt, :], in_=SB_[0:nt, 0:768].rearrange("p (h d) -> p h d", h=12), axis=AX.X, op=ALU.add),
            reads=[bSB], writes=[bss12])
        add("dve", lambda e: e.tensor_scalar(out=ms12[0:nt, :], in0=ss12[0:nt, :], scalar1=1.0 / 64, scalar2=EPS, op0=ALU.mult, op1=ALU.add),
            reads=[bss12], writes=[bss12])
        add("pool", lambda e: e.tensor_tensor(out=r12[0:nt, :], in0=ms12[0:nt, :], in1=nh[0:nt, :], op=ALU.pow), reads=[bss12, bnh], writes=[br12])
        add("dve", lambda e: e.tensor_tensor(out=SB_[0:nt, 0:512].rearrange("p (h d) -> p h d", h=8),
                                             in0=PB[0][0:nt, 0:512].rearrange("p (h d) -> p h d", h=8),
                                             in1=r12[0:nt, 0:8].unsqueeze(2).to_broadcast([nt, 8, 64]), op=ALU.mult),
            reads=[bPB[0], br12], writes=[bSB])
        add("dve", lambda e: e.tensor_tensor(out=SB_[0:nt, 512:768].rearrange("p (h d) -> p h d", h=4),
                                             in0=PB[1][0:nt, 0:256].rearrange("p (h d) -> p h d", h=4),
                                             in1=r12[0:nt, 8:12].unsqueeze(2).to_broadcast([nt, 4, 64]), op=ALU.mult),
            reads=[bPB[1], br12], writes=[bSB])
        add("dve", lambda e: e.tensor_tensor(out=SB_[0:nt, 0:512].rearrange("p (h d) -> p h d", h=8),
                                             in0=SB_[0:nt, 0:512].rearrange("p (h d) -> p h d", h=8),
                                             in1=gqk[0:nt, 0:64].unsqueeze(1).to_broadcast([nt, 8, 64]), op=ALU.mult),
            reads=[bSB, bgqk], writes=[bSB])
        add("dve", lambda e: e.tensor_tensor(out=SB_[0:nt, 512:768].rearrange("p (h d) -> p h d", h=4),
                                             in0=SB_[0:nt, 512:768].rearrange("p (h d) -> p h d", h=4),
                                             in1=gqk[0:nt, 64:128].unsqueeze(1).to_broadcast([nt, 4, 64]), op=ALU.mult),
            reads=[bSB, bgqk], writes=[bSB])
        if DBG["stage"] < 3.1:
            return
        rope_ops(SB_[0:nt, 0:768], qkr[0:nt, :], nt, 12, 32, rT[0:nt, 0:32], rT[0:nt, 32:64], bSB, bqkr, brT)
        if DBG["stage"] < 3.2:
            return
        add("act", lambda e: e.activation(out=v32[0:nt, :], in_=PB[1][0:nt, 256:512], func=AF.Copy), reads=[bPB[1]], writes=[bv32])
        if not sample:
            Vdst = Vb[0:nt, t, 64:448].rearrange("p (a b) -> p a b", a=2)[:, :, 0:128]
            bVdst = bVb[t]
        else:
            Vdst = Vn[0:nt, 64:448].rearrange("p (a b) -> p a b", a=2)[:, :, 0:128]
            bVdst = bVn
        add("dve", lambda e: e.tensor_copy(out=Vdst, in_=PB[1][0:nt, 256:512].rearrange("p (a b) -> p a b", a=2)), reads=[bPB[1]], writes=[bVdst])
        if DBG["stage"] < 3.3:
            return
        add("act", lambda e: e.activation(out=qki[0:nt, 0:256], in_=PB[2][0:nt, 256:512], func=AF.Copy), reads=[bPB[2]], writes=[bqki])
        add("act", lambda e: e.activation(out=qki[0:nt, 256:288], in_=PB[3][0:nt, 0:32], func=AF.Copy), reads=[bPB[3]], writes=[bqki])
        add("act", lambda e: e.activation(out=w8[0:nt, :], in_=PB[3][0:nt, 32:40], func=AF.Copy, scale=0.0625), reads=[bPB[3]], writes=[bw8])
        if DBG["stage"] < 3.4:
            return
        TV = SC[0:nt, 0:256]
        gelu_core(PB[2][0:nt, 0:256], TV, [bPB[2]], bSC)
        add("act", lambda e: e.activation(out=vbf[0:nt, :], in_=TV, func=AF.Copy, scale=0.5), reads=[bSC], writes=[bvbf])
        if sample:
            add("act", lambda e: e.activation(out=sgv[0:nt, :], in_=TV, func=AF.Copy, scale=0.5), reads=[bSC], writes=[bsgv])
            add("sp", lambda e: e.dma_start(out=nsgv[l], in_=sgv[0:nt, :]), reads=[bsgv], dma=True)
        if DBG["stage"] < 3.5:
            return
        rope_ops(qki[0:nt, :], qkir[0:nt, :], nt, 9, 16, rT[0:nt, 64:80], rT[0:nt, 80:96], bqki, bqkir, brT)
        if DBG["stage"] < 3.6:
            return
        if not sample:
            row0 = t * 128
            add("sp", lambda e: e.dma_start(out=nkp[l, row0:row0 + nt, :], in_=qkr[0:nt, 512:768]), reads=[bqkr], dma=True)
            add("sp", lambda e: e.dma_start(out=nvp[l, row0:row0 + nt, :], in_=v32[0:nt, :]), reads=[bv32], dma=True)
            add("sp", lambda e: e.dma_start(out=nkip[l, row0:row0 + nt, :], in_=qkir[0:nt, 256:288]), reads=[bqkir], dma=True)
        else:
            add("sp", lambda e: e.dma_start(out=nks[l], in_=qkr[0:nt, 512:768]), reads=[bqkr], dma=True)
            add("sp", lambda e: e.dma_start(out=nvs[l], in_=v32[0:nt, :]), reads=[bv32], dma=True)
            add("sp", lambda e: e.dma_start(out=nkis[l], in_=qkir[0:nt, 256:288]), reads=[bqkir], dma=True)
        if DBG["stage"] < 3.7:
            return
        add("act", lambda e: e.activation(out=qkbf[0:nt, :], in_=qkr[0:nt, :], func=AF.Copy), reads=[bqkr], writes=[bqkbf])

        def trq(e):
            ins = None
            for b_ in range(6):
                ins = e.transpose(PT[:, b_ * 128:b_ * 128 + nt], qkbf[0:nt, b_ * 128:(b_ + 1) * 128], ident[0:nt, 0:nt])
            return ins
        add("pe", trq, reads=[bqkbf, bident], writes=[bPT])
        PT3 = PT[:, :].rearrange("p (k s) -> p k s", k=8)
        for gl in range(2):
            if DBG.get("skip") == "qTz" or (DBG.get("skip") == "qTz1" and gl == 1):
                continue
            add("dve", (lambda gl: lambda e: e.tensor_copy(
                out=qTz[64 * gl:64 * gl + 64, gl, :, :, 0:nt],
                in_=PT3[64 * gl:64 * gl + 64, 0:4, 0:nt].rearrange("p (a b) s -> p a b s", a=2)))(gl), reads=[bPT], writes=[bqT])
        if DBG.get("skip") == "KT":
            return
        if not sample:
            add("dve", lambda e: e.tensor_copy(out=KT[:, :, t * 128:t * 128 + nt], in_=PT3[:, 4:6, 0:nt]), reads=[bPT], writes=[bKT[t]])
        else:
            add("dve", lambda e: e.tensor_copy(out=KTn[:, :, 0:nt], in_=PT3[:, 4:6, 0:nt]), reads=[bPT], writes=[bKTn])
        if DBG["stage"] < 3.8:
            return
        if DBG.get("skip") == "qibf":
            return
        add("act", lambda e: e.activation(out=qibf[0:nt, :], in_=qkir[0:nt, 0:256], func=AF.Copy), reads=[bqkir], writes=[bqibf])
        if DBG.get("skip") == "ki4":
            return
        add("dve", lambda e: e.tensor_copy(out=ki4[0:nt, :].rearrange("p (r d) -> p r d", r=4),
                                           in_=qkir[0:nt, 256:288].unsqueeze(1).to_broadcast([nt, 4, 32])), reads=[bqkir], writes=[bki4])

        def tri(e):
            e.transpose(PT[:, 0:nt], qibf[0:nt, 0:128], ident[0:nt, 0:nt])
            e.transpose(PT[:, 128:128 + nt], qibf[0:nt, 128:256], ident[0:nt, 0:nt])
            return e.transpose(PT[:, 256:256 + nt], ki4[0:nt, :], ident[0:nt, 0:nt])
        if DBG.get("skip") == "tri":
            return
        add("pe", tri, reads=[bqibf, bki4, bident], writes=[bPT])
        if DBG.get("skip") == "qiT":
            return
        add("dve", lambda e: e.tensor_copy(out=qiT[:, :, 0:nt], in_=PT3[:, 0:2, 0:nt]), reads=[bPT], writes=[bqiT])
        if DBG.get("skip") == "kiT":
            return
        if not sample:
            add("dve", lambda e: e.tensor_copy(out=kiT[:, t * 128:t * 128 + nt], in_=PT[:, 256:256 + nt]), reads=[bPT], writes=[bkiT[t]])
        else:
            add("dve", lambda e: e.tensor_copy(out=kiTn[:, 0:nt], in_=PT[:, 256:256 + nt]), reads=[bPT], writes=[bkiTn])

        if DBG["stage"] < 4:
            return
        TU = SC[:, 256:512].rearrange("p (c s) -> p c s", c=2)[:, :, 0:nt]
        gelu_core(PF0[:, 0:2, 0:nt], TU, [bPB[4]], bSC)
        for c in range(2):
            for hh in range(2):
                h = 2 * c + hh
                add("pe", (lambda c, h: lambda e: e.matmul(PB7[:, h * 128:h * 128 + nt], lhsT=vbf[0:nt, c * 128:(c + 1) * 128],
                                                           rhs=wmT[0:nt, h, 0:nt], start=True, stop=True))(c, h),
                    reads=[bvbf, bwmT], writes=[bPB7])
        for c in range(2):
            for hh in range(2):
                h = 2 * c + hh
                r0 = hh * 64
                tmpv = SC[r0:r0 + 64, 512 + h * 128:512 + h * 128 + nt]
                add("dve", (lambda h, r0, c, tmpv: lambda e: e.tensor_tensor(out=tmpv, in0=PB7[r0:r0 + 64, h * 128:h * 128 + nt],
                                                                             in1=bP[r0:r0 + 64, c, 0:nt], op=ALU.add))(h, r0, c, tmpv),
                    reads=[bPB7, bbP], writes=[bSC])
                add("dve", (lambda r0, c, tmpv: lambda e: e.scalar_tensor_tensor(out=concatT[r0:r0 + 64, c, 0:nt], in0=tmpv, scalar=0.5,
                                                                                 in1=SC[r0:r0 + 64, 256 + c * 128:256 + c * 128 + nt],
                                                                                 op0=ALU.mult, op1=ALU.mult))(r0, c, tmpv),
                    reads=[bSC], writes=[bcc[c]])
        if DBG["stage"] < 5:
            return
        if t == 0:
            add("dve", lambda e: e.memset(xpad[:, :, 0:2], 0.0), writes=[bxpad])
        elif sample:
            add("sp", lambda e: e.dma_start(out=xpad[:, :, 0:2], in_=sconvT[l]), writes=[bxpad], dma=True)
        add("act", lambda e: e.activation(out=hin[:, :, 0:nt], in_=PF1[:, 2:4, 0:nt], func=AF.Copy), reads=[bPB[5]], writes=[bhin])
        add("dve", lambda e: e.tensor_tensor(out=xpad[:, :, 2:2 + nt], in0=PF1[:, 0:2, 0:nt], in1=hin[:, :, 0:nt], op=ALU.mult),
            reads=[bPB[5], bhin], writes=[bxpad])
        for c in range(2):
            ci = l * 6 + c * 3
            add("dve", (lambda c, ci: lambda e: e.tensor_scalar(out=cacc[:, c, 0:nt], in0=xpad[:, c, 0:nt], scalar1=cw[:, ci:ci + 1],
                                                                scalar2=None, op0=ALU.mult))(c, ci), reads=[bxpad, bcw], writes=[bcacc])
            for tap in (1, 2):
                add("dve", (lambda c, ci, tap: lambda e: e.scalar_tensor_tensor(out=cacc[:, c, 0:nt], in0=xpad[:, c, tap:tap + nt],
                                                                                scalar=cw[:, ci + tap:ci + tap + 1], in1=cacc[:, c, 0:nt],
                                                                                op0=ALU.mult, op1=ALU.add))(c, ci, tap),
                    reads=[bxpad, bcw, bcacc], writes=[bcacc])
        add("dve", lambda e: e.tensor_tensor(out=concatT[:, 2:4, 0:nt], in0=PF0[:, 2:4, 0:nt], in1=cacc[:, :, 0:nt], op=ALU.mult),
            reads=[bPB[4], bcacc], writes=[bcc[2], bcc[3]])
        if t == 15:
            add("sp", lambda e: e.dma_start(out=ncp[l], in_=xpad[:, :, nt:nt + 2]), reads=[bxpad], dma=True)
        elif sample:
            add("sp", lambda e: e.dma_start(out=ncs[l], in_=xpad[:, :, nt:nt + 2]), reads=[bxpad], dma=True)
        if t < 15:
            add("act", lambda e: e.activation(out=xpad[:, :, 0:2], in_=xpad[:, :, nt:nt + 2], func=AF.Copy), reads=[bxpad], writes=[bxpad])

        if DBG["stage"] < 6:
            return
        if not sample:
            Stot = 128 * (t + 1)
            for n0 in range(0, Stot, 512):
                w_ = min(512, Stot - n0)
                yield from indexer_chunk(nt, n0, w_, kiT, n0, isc, [bkiT[i_] for i_ in range(n0 // 128, (n0 + w_) // 128)], bisc)
            yield from topk_mask(nt, Stot, isc, maskb, bisc, bmaskb, inadm=t * 128 + 64)
            return
        if True:
            for b_ in SAMPLE_AR:
                seed(b_, PROMPT_AR)
            for blk in range(4):
                s0 = blk * 1024
                add("pool", (lambda s0: lambda e: e.dma_start(out=kistage[:, :, :], in_=cki[l, s0:s0 + 1024, :].rearrange("(k p) d -> p k d", p=128)))(s0),
                    writes=[bkistage], dma=True)
                add("dve", lambda e: e.tensor_copy(out=ki4s[:, :, :].rearrange("p k (r d) -> p k r d", r=4),
                                                   in_=kistage[:, :, :].unsqueeze(2).to_broadcast([128, 8, 4, 32])), reads=[bkistage], writes=[bki4s])

                def trk(e):
                    ins = None
                    for k in range(8):
                        ins = e.transpose(PT[:, k * 128:(k + 1) * 128], ki4s[:, k, :], ident[:, :])
                    return ins
                add("pe", trk, reads=[bki4s, bident], writes=[bPT])
                add("act", lambda e: e.activation(out=kiT_s[:, :], in_=PT[:, :], func=AF.Copy), reads=[bPT], writes=[bkiT_s])
                for n0 in (0, 512):
                    yield from indexer_chunk(nt, s0 + n0, 512, kiT_s, n0, isc_s, [bkiT_s], bisc_s)
            yield from indexer_chunk(nt, PAST, TS, kiTn, 0, isc_s, [bkiTn], bisc_s)
            Stot = PAST + TS
            if DBG["ffn"]:
                seed(bF[0], [bWA]); seed(bF[1], [bWA])
                load_ffn_dma(l, 0, 0); load_ffn_dma(l, 1, 1)
                PREF[l] = True
            yield from topk_mask(nt, Stot, isc_s, maskb_s, bisc_s, bmaskb_s, inadm=None)
            for blk in range(4):
                s0 = blk * 1024
                add("pool", (lambda s0: lambda e: e.dma_start(out=kstage[:, :, :], in_=ck[l, s0:s0 + 1024, :].rearrange("(k p) d -> p k d", p=128)))(s0),
                    writes=[bkstage, bv32], dma=True)
                for a_ in range(2):
                    add("pool", (lambda s0, a_: lambda e: e.dma_start(
                        out=Vb_s[:, :, 64 + a_ * 192:64 + a_ * 192 + 128],
                        in_=cv[l, s0:s0 + 1024, a_ * 128:(a_ + 1) * 128].rearrange("(k p) d -> p k d", p=128)))(s0, a_),
                        writes=[bVb_s], dma=True)
                for half in range(2):
                    def trK(e, half=half):
                        ins = None
                        for k in range(4):
                            for gp in range(2):
                                ins = e.transpose(PT[:, (gp * 4 + k) * 128:(gp * 4 + k + 1) * 128],
                                                  kstage[:, half * 4 + k, gp * 128:(gp + 1) * 128], ident[:, :])
                        return ins
                    add("pe", trK, reads=[bkstage, bident], writes=[bPT])
                    add("act", (lambda half: lambda e: e.activation(out=KT_s[:, :, half * 512:(half + 1) * 512],
                                                                    in_=PT[:, :].rearrange("p (g s) -> p g s", g=2), func=AF.Copy))(half),
                        reads=[bPT], writes=[bKT_s])
                tiles = [(kt, 128, s0 + kt * 128) for kt in range(8)]
                for g in range(4):
                    yield from attn_block(nt, g, tiles, KT_s, Vb_s, maskb_s, (lambda kt: [bKT_s, bVb_s]), blk == 0, False, bmaskb_s, qTz, bqT, concatT, bcc)
            for g in range(4):
                yield from attn_block(nt, g, [(0, TS, PAST)], KTn, Vn[:, :].rearrange("p (k c) -> p k c", k=1), maskb_s, (lambda kt: [bKTn, bVn]), False, True, bmaskb_s,
                           qTz, bqT, concatT, bcc)

        _out_proj(t, nt, concatT, bcc)

    def _mixer_B(l, t, nt, sample, qTz, bqT, concatT, bcc, maskb, bmaskb):
        if sample:
            return
        tiles = [(kt, 128, kt * 128) for kt in range(t + 1)]
        for g in range(4):
            yield from attn_block(nt, g, tiles, KT, Vb, maskb, (lambda kt: [bKT[kt], bVb[kt]]), True, True, bmaskb, qTz, bqT, concatT, bcc)
        _out_proj(t, nt, concatT, bcc)

    def _out_proj(t, nt, concatT, bcc):
        if DBG["stage"] < 8:
            return
        for n in range(2):
            def f(e, n=n):
                ins = None
                for c in range(8):
                    ins = e.matmul(PB[n][0:nt, 0:512], lhsT=concatT[:, c, 0:nt], rhs=wout[:, c, n * 512:(n + 1) * 512],
                                   start=(c == 0), stop=(c == 7))
                return ins
            add("pe", f, reads=bcc + [bWB], writes=[bPB[n]])
            add("dve", (lambda n: lambda e: e.tensor_tensor(out=X[0:nt, t, n * 512:(n + 1) * 512], in0=X[0:nt, t, n * 512:(n + 1) * 512],
                                                            in1=PB[n][0:nt, 0:512], op=ALU.add))(n), reads=[bX[t], bPB[n]], writes=[bX[t]])

    def ffn_layer(l, last):
        seed(_ba, [bqkr, bqkbf])
        allh = Buf("hT2seed")
        seed(allh, PROMPT_AR + SAMPLE_AR)
        for t in range(NT):
            bhT2[t].w = None
            bhT2[t].r = list(allh.r)
            norm_to_hT(t, hT2, bhT2[t], t * 128)
        groups = [(0, 512, [0, 1, 2, 3]), (512, 512, [4, 5, 6, 7]), (1024, 512, [8, 9, 10, 11]), (1536, 512, [12, 13, 14, 15]), (2048, TS, [16])]
        for f in range(8):
            slot = f % 3
            wu, wd = ffn_views(slot)
            for gi, (tok0, gw, tl) in enumerate(groups):
                ab = (f * 5 + gi) % 2
                for m in range(4):
                    def fu(e, m=m, gw=gw, wu=wu, tok0=tok0):
                        ins = None
                        for k in range(8):
                            ins = e.matmul(PB[m][:, 0:gw], lhsT=wu[:, k, m * 128:(m + 1) * 128], rhs=hT2[:, k, tok0:tok0 + gw],
                                           start=(k == 0), stop=(k == 7))
                        return ins
                    add("pe", fu, reads=[bF[slot]] + [bhT2[t] for t in tl], writes=[bPB[m]])
                    add("act", (lambda m, gw: lambda e: e.activation(out=rt[m % 2][:, 0:gw], in_=PB[m][:, 0:gw], func=AF.Relu))(m, gw),
                        reads=[bPB[m]], writes=[brt[m % 2]])
                    add("act", (lambda m, gw, ab: lambda e: e.activation(out=actT[ab][:, m, 0:gw], in_=rt[m % 2][:, 0:gw], func=AF.Square))(m, gw, ab),
                        reads=[brt[m % 2]], writes=[bactT[ab]])
                for ti, t in enumerate(tl):
                    nt = nrows(t)
                    for n in range(2):
                        pd = PB[4 + n] if ti % 2 == 0 else (PB7 if n == 1 else PB[4 + n])
                        bpd = bPB[4 + n] if ti % 2 == 0 else (bPB7 if n == 1 else bPB[4 + n])

                        def fd(e, n=n, ti=ti, pd=pd, nt=nt, ab=ab, wd=wd):
                            ins = None
                            for m in range(4):
                                ins = e.matmul(pd[0:nt, 0:512], lhsT=actT[ab][:, m, ti * 128:ti * 128 + nt], rhs=wd[:, m, n * 512:(n + 1) * 512],
                                               start=(m == 0), stop=(m == 3))
                            return ins
                        add("pe", fd, reads=[bactT[ab], bF[slot]], writes=[bpd])
                        add("dve", (lambda n, t, pd, nt: lambda e: e.tensor_tensor(out=X[0:nt, t, n * 512:(n + 1) * 512],
                                                                               in0=X[0:nt, t, n * 512:(n + 1) * 512], in1=pd[0:nt, 0:512],
                                                                               op=ALU.add))(n, t, pd, nt), reads=[bX[t], bpd], writes=[bX[t]])
            if f + 3 < 8:
                load_ffn_chunk(l, f + 3, slot)

    for l in range(depth):
        if l == 0:
            load_mix_weights(0)
        load_layer_params(l)
        for b_ in PROMPT_AR:
            if l > 0:
                seed(b_, bhT2 + SAMPLE_AR)
        if l > 0:
            seed(bqkr, [_ba]); seed(bqkbf, [_ba])
        for c0 in (0, 192, 384):
            add("dve", (lambda c0: lambda e: e.memset(Vb[:, :, c0:c0 + 64], 1.0))(c0), writes=bVb)
            if l == 0:
                add("dve", (lambda c0: lambda e: e.memset(Vn[:, c0:c0 + 64], 1.0))(c0), writes=[bVn])
        npt = min(16, DBG["tiles"])

        def drive(gens):
            gens = list(gens)
            while gens:
                for g_ in list(gens):
                    try:
                        next(g_)
                    except StopIteration:
                        gens.remove(g_)
        if DBG.get("pipe", True):
            if npt > 0:
                drive([mixer_tile(l, 0, "A")])
            for t in range(npt):
                gl_ = []
                if t + 1 < npt:
                    gl_.append(mixer_tile(l, t + 1, "A"))
                gl_.append(mixer_tile(l, t, "B"))
                drive(gl_)
        else:
            for t in range(npt):
                drive([mixer_tile(l, t, "AB")])
        if DBG["sample"]:
            for c0 in (0, 192, 384):
                add("dve", (lambda c0: lambda e: e.memset(Vb_s[:, :, c0:c0 + 64], 1.0))(c0), writes=[bVb_s], reads=[])
            drive([mixer_tile(l, 16, "AB")])
        if PREF.get(l):
            seed(bF[2], [bWB])
            fold_ffn(l, 0, 0); fold_ffn(l, 1, 1)
            load_ffn_chunk(l, 2, 2)
        else:
            seed(bF[0], [bWA]); seed(bF[1], [bWA]); seed(bF[2], [bWB])
            for f in range(3):
                load_ffn_chunk(l, f, f)
        if DBG["ffn"]:
            ffn_layer(l, l == depth - 1)
        if l + 1 < depth:
            seed(bWA, [bF[0], bF[1]]); seed(bWB, [bF[2]])
            load_mix_weights(l + 1)
    for t in range(16):
        add("sp", (lambda t: lambda e: e.dma_start(out=yp[t * 128:(t + 1) * 128, :], in_=X[:, t, :]))(t), reads=[bX[t]], dma=True)
    add("sp", lambda e: e.dma_start(out=ys, in_=X[0:TS, 16, :]), reads=[bX[16]], dma=True)

    S.finalize(nc, st)
    with nc.Block() as block:
        S.emit(block)
    st.close()
    return nc


def _rope_tables():
    tab = np.zeros((128, NT, 96), np.float32)
    for t in range(NT):
        n = 128 if t < 16 else TS
        pos = (np.arange(n) + (t * 128 if t < 16 else PAST)).astype(np.float32)
        for half, off in ((32, 0), (16, 64)):
            inv = (np.float32(10000.0) ** (-np.arange(half, dtype=np.float32) / np.float32(half))).astype(np.float32)
            ang = (pos[:, None] * inv[None, :]).astype(np.float32)
            tab[:n, t, off:off + half] = np.cos(ang).astype(np.float32)
            tab[:n, t, off + half:off + 2 * half] = np.sin(ang).astype(np.float32)
    return tab


def _win_perm():
    qcols = []
    for gp in range(2):
        for j in range(2):
            for gl in range(2):
                h = 2 * (2 * gp + gl) + j
                qcols.extend(range(1280 + h * 64, 1280 + (h + 1) * 64))
    tmc = qcols + list(range(1792, 2048)) + list(range(2048, 2304)) + list(range(256, 512)) + list(range(2304, 2560)) \
        + list(range(2560, 2592)) + list(range(2592, 2600))
    fmc = list(range(0, 256)) + list(range(512, 1280))
    return np.array(tmc + fmc, dtype=np.int64)


_NC_CACHE = {}


def kernel(x_prompt, x_sample, cache_k, cache_v, cache_kidx, state_conv, g_mix, w_in, q_gain, k_gain, w_s, b_s, conv_w,
           w_out, g_ffn, w_up, w_down):
    f = lambda a: np.ascontiguousarray(np.asarray(a, dtype=np.float32))
    x_prompt, x_sample, cache_k, cache_v, cache_kidx, state_conv = map(f, (x_prompt, x_sample, cache_k, cache_v, cache_kidx, state_conv))
    g_mix, w_in, q_gain, k_gain, w_s, b_s, conv_w, w_out, g_ffn, w_up, w_down = map(
        f, (g_mix, w_in, q_gain, k_gain, w_s, b_s, conv_w, w_out, g_ffn, w_up, w_down))
    if "nc" not in _NC_CACHE:
        _NC_CACHE["nc"] = build_program()
    nc = _NC_CACHE["nc"]
    w_in_p = np.ascontiguousarray(w_in[:, :, _win_perm()])
    gvec = np.ascontiguousarray(np.concatenate([g_mix.reshape(2, 8, 128), g_ffn.reshape(2, 8, 128)], 0).reshape(32, 128).T)
    wsT = np.ascontiguousarray(w_s.transpose(0, 3, 1, 2))
    cwv = np.ascontiguousarray(conv_w.reshape(2, 3, 2, 128).transpose(3, 0, 2, 1).reshape(128, 12))
    rope = _rope_tables()
    p2 = (2.0 ** -(np.arange(32, dtype=np.float64) + 1)).astype(np.float32).reshape(1, 32)
    shared = dict(w_in=w_in_p, w_out=w_out, w_up=w_up, w_down=w_down, gvec=gvec, wsT=wsT, bs=b_s, cwv=cwv, qg=q_gain, kg=k_gain,
                  rope=rope, p2=p2)
    in_maps = []
    for c in range(8):
        m = dict(shared)
        m["xp"] = x_prompt[c]
        m["xs"] = x_sample[c]
        m["ck"] = np.ascontiguousarray(cache_k[:, c].reshape(2, PAST, 256))
        m["cv"] = np.ascontiguousarray(cache_v[:, c].reshape(2, PAST, 256))
        m["cki"] = np.ascontiguousarray(cache_kidx[:, c])
        m["sconvT"] = np.ascontiguousarray(state_conv[:, c].reshape(2, 2, 2, 128).transpose(0, 3, 2, 1))
        in_maps.append(m)
    res = run_bass_kernel_spmd(nc, in_maps, core_ids=list(range(8)))
    R = res.results
    st_ = lambda k: np.stack([np.asarray(R[c][k]) for c in range(8)])
    yp = st_("yp"); ys = st_("ys")
    nkp = st_("nkp").transpose(1, 0, 2, 3).reshape(2, 8, SEQ, 4, 64)
    nvp = st_("nvp").transpose(1, 0, 2, 3).reshape(2, 8, SEQ, 4, 64)
    nkip = st_("nkip").transpose(1, 0, 2, 3)
    cvt = lambda a: a.transpose(1, 0, 4, 3, 2).reshape(2, 8, 2, 256)
    ncp = cvt(st_("ncp")); ncs = cvt(st_("ncs"))
    nks = st_("nks").transpose(1, 0, 2, 3).reshape(2, 8, TS, 4, 64)
    nvs = st_("nvs").transpose(1, 0, 2, 3).reshape(2, 8, TS, 4, 64)
    nkis = st_("nkis").transpose(1, 0, 2, 3)
    nsgv = st_("nsgv").transpose(1, 0, 2, 3)
    out = (yp, ys, nkp, nvp, nkip, ncp, nks, nvs, nkis, ncs, nsgv)
    return tuple(np.ascontiguousarray(o, dtype=np.float32) for o in out)
```

```python
import contextlib
import numpy as np
import concourse.bass as bass
import concourse.mybir as mybir
from concourse.bass_utils import run_bass_kernel_spmd

F32 = mybir.dt.float32
BF16 = mybir.dt.bfloat16
ALU = mybir.AluOpType
AF = mybir.ActivationFunctionType
AX = mybir.AxisListType

D = 1024
SEQ = 2048
NT = 17
TS = 32
PAST = 4096
DEPTH = 2
INW = 2600
NIT = 14
EPS = 1e-6
NEG = -1.0e30
TMW = 1576
GC = 0.7978845608028654


class Buf:
    __slots__ = ("name", "w", "r", "xr")

    def __init__(self, name="b", xr=False):
        self.name = name
        self.w = None
        self.r = []
        self.xr = xr


def seed(new, olds):
    lst = []
    for o in olds:
        if o.w is not None:
            lst.append(o.w)
        lst.extend(o.r)
    new.w = None
    new.r = lst


class Op:
    __slots__ = ("eng", "fn", "deps", "signal", "dma", "semi", "val", "lane")


class Sched:
    ENGS = ("pe", "act", "dve", "pool", "sp")
    LIMIT = 30000
    NLANES = 8

    def __init__(self):
        self.ops = {e: [] for e in self.ENGS}

    def add(self, eng, fn, reads=(), writes=(), dma=False):
        op = Op()
        op.eng = eng
        op.fn = fn
        op.signal = False
        op.dma = dma
        deps = {}
        for b in reads:
            if b.w is not None:
                deps[id(b.w)] = (b.w, True)
            if b.xr:
                for r in b.r:
                    if r.eng != eng and id(r) not in deps:
                        deps[id(r)] = (r, False)
        for b in writes:
            if b.w is not None and id(b.w) not in deps:
                deps[id(b.w)] = (b.w, False)
            for r in b.r:
                if id(r) not in deps:
                    deps[id(r)] = (r, False)
        op.deps = []
        for d, raw in deps.values():
            if d is op:
                continue
            if not d.dma and not dma and d.eng == eng:
                if eng == "pe":
                    continue
            if not d.dma:
                d.signal = True
            op.deps.append(d)
        for b in reads:
            b.r.append(op)
        for b in writes:
            b.w = op
            b.r = []
        self.ops[eng].append(op)
        return op

    def finalize(self, nc, stack):
        self.sems = {}
        self.final_dma = []
        keys = set()
        for e in self.ENGS:
            cnt = 0
            nd = 0
            lane_cnt = [0] * self.NLANES
            nl = 2 if e == "pool" else self.NLANES
            for op in self.ops[e]:
                if op.dma:
                    op.lane = nd % nl
                    nd += 1
                    lane_cnt[op.lane] += 1
                    op.val = 16 * lane_cnt[op.lane]
                    op.semi = ("dma", e, op.lane)
                    keys.add(op.semi)
                elif op.signal:
                    cnt += 1
                    op.semi = ("c", e, (cnt - 1) // self.LIMIT)
                    op.val = (cnt - 1) % self.LIMIT + 1
                    keys.add(op.semi)
            for ln in range(self.NLANES):
                if lane_cnt[ln]:
                    self.final_dma.append((("dma", e, ln), 16 * lane_cnt[ln]))
        for k in sorted(keys, key=str):
            self.sems[k] = stack.enter_context(nc.semaphore("s_" + "_".join(map(str, k))))

    def emit(self, block):
        def run(engname):
            def body(eng):
                seen = {}
                for op in self.ops[engname]:
                    waits = [(d.semi, d.val) for d in op.deps]
                    if op.dma and op.val > 16:
                        waits.append((op.semi, op.val - 16))
                    for k, v in waits:
                        if seen.get(k, 0) < v:
                            eng.wait_ge(self.sems[k], v)
                            seen[k] = v
                    ins = op.fn(eng)
                    if op.dma:
                        ins.then_inc(self.sems[op.semi], 16)
                    elif op.signal:
                        ins.then_inc(self.sems[op.semi], 1)
                if engname == "sp":
                    for k, v in self.final_dma:
                        if seen.get(k, 0) < v:
                            eng.wait_ge(self.sems[k], v)
            return body
        block.tensor(run("pe"))
        block.scalar(run("act"))
        block.vector(run("dve"))
        block.gpsimd(run("pool"))
        block.sync(run("sp"))


class _Stop(Exception):
    pass


DBG = {"tiles": NT, "stage": 99, "ffn": True, "depth": DEPTH, "sample": True, "tm": True, "fm": True}


def build_program(depth=DEPTH):
    depth = min(depth, DBG["depth"])
    nc = bass.Bass("TRN2", target_bir_lowering=False, dynamic_dma_scratch_size=6144)
    S = Sched()

    def din(name, shape):
        return nc.dram_tensor(name, list(shape), F32, kind="ExternalInput").ap()

    def dout(name, shape):
        return nc.dram_tensor(name, list(shape), F32, kind="ExternalOutput").ap()

    xp = din("xp", [SEQ, D]); xs = din("xs", [TS, D])
    ck = din("ck", [DEPTH, PAST, 256]); cv = din("cv", [DEPTH, PAST, 256]); cki = din("cki", [DEPTH, PAST, 32])
    sconvT = din("sconvT", [DEPTH, 128, 2, 2])
    w_in = din("w_in", [DEPTH, D, INW]); w_out = din("w_out", [DEPTH, D, D])
    w_up = din("w_up", [DEPTH, D, 4 * D]); w_down = din("w_down", [DEPTH, 4 * D, D])
    gvec = din("gvec", [128, 32]); wsT = din("wsT", [DEPTH, 128, 4, 128]); bs = din("bs", [DEPTH, 4, 128])
    cwv = din("cwv", [128, 12]); qg = din("qg", [DEPTH, 64]); kg = din("kg", [DEPTH, 64])
    rope = din("rope", [128, NT, 96]); p2 = din("p2", [1, 32])
    yp = dout("yp", [SEQ, D]); ys = dout("ys", [TS, D])
    nkp = dout("nkp", [DEPTH, SEQ, 256]); nvp = dout("nvp", [DEPTH, SEQ, 256]); nkip = dout("nkip", [DEPTH, SEQ, 32])
    ncp = dout("ncp", [DEPTH, 128, 2, 2])
    nks = dout("nks", [DEPTH, TS, 256]); nvs = dout("nvs", [DEPTH, TS, 256]); nkis = dout("nkis", [DEPTH, TS, 32])
    ncs = dout("ncs", [DEPTH, 128, 2, 2]); nsgv = dout("nsgv", [DEPTH, TS, 256])

    st = contextlib.ExitStack()
    cnt = [0]

    def sb(shape, dt=F32, name=None):
        cnt[0] += 1
        return st.enter_context(nc.sbuf_tensor(name or f"sb{cnt[0]}", list(shape), dt))

    def ps(shape, dt=F32):
        cnt[0] += 1
        return st.enter_context(nc.psum_tensor(f"ps{cnt[0]}", list(shape), dt))

    X = sb([128, NT, D], F32, "X")
    WA = sb([128, 8 * INW], BF16, "WA")
    WB = sb([128, 8 * D], BF16, "WB")
    AR = sb([128, 21504], BF16, "AR")
    SA = sb([128, 1024], F32, "SA"); SB_ = sb([128, 1024], F32, "SBs"); SC = sb([128, 1024], F32, "SC")
    hbf = sb([128, 1024], BF16, "hbf")
    hT = sb([128, 8, 128], BF16, "hT")
    P12 = sb([128, 2304], BF16, "P12")
    qkr = P12[:, 0:1536].bitcast(F32); qkbf = P12[:, 1536:2304]
    v32 = SB_[:, 768:1024]; vbf = sb([128, 256], BF16, "vbf"); sgv = SC[:, 768:1024]
    qki = sb([128, 288], F32, "qki"); qkir = sb([128, 288], F32, "qkir"); qibf = sb([128, 256], BF16, "qibf")
    ki4 = sb([128, 128], BF16, "ki4")
    w8 = sb([128, 8], F32, "w8")
    xpad = sb([128, 2, 130], F32, "xpad"); hin = sb([128, 2, 128], F32, "hin"); cacc = sb([128, 2, 128], F32, "cacc")
    ccT = [sb([128, 8, 128], BF16, f"concatT{i}") for i in range(2)]
    qTzs = [sb([128, 2, 2, 2, 128], BF16, f"qTz{i}") for i in range(2)]; qiT = sb([128, 2, 128], BF16, "qiT")
    rt = [sb([128, 512], F32, f"rt{i}") for i in range(2)]
    pT = [sb([128, 256], BF16, f"pT{i}") for i in range(2)]
    rec = sb([128, 256], F32, "rec")
    accs = SC[:, 0:256].rearrange("p (g c) -> p g c", g=4)
    actT0 = P12[:, 0:2048].rearrange("p (m s) -> p m s", m=4)
    actT = [actT0, actT0]
    ident = sb([128, 128], BF16, "ident")
    ropeB = [sb([128, 96], F32, f"ropeB{i}") for i in range(2)]
    gv = sb([128, 32], F32, "gv"); cw = sb([128, 12], F32, "cw")
    wsf = SC[:, 0:512]; wmT = sb([128, 4, 128], BF16, "wmT")
    bP = sb([128, 2, 128], F32, "bP"); gqk = sb([128, 128], F32, "gqk")
    p2b = sb([128, 32], F32, "p2b"); nh = sb([128, 12], F32, "nh")
    sm = sb([128, 64], F32, "sm")
    steps = sb([128, 32], F32, "steps"); nsteps = sb([128, 32], F32, "nsteps")
    KTn = sb([128, 2, 32], BF16, "KTn"); Vn = sb([128, 448], BF16, "Vn"); kiTn = sb([128, 32], BF16, "kiTn")
    kstage = SB_[:, :].bitcast(BF16).rearrange("p (k c) -> p k c", k=8)
    SAb = SA[:, :].bitcast(BF16)
    ki4s = SAb[:, 0:1024].rearrange("p (k c) -> p k c", k=8)
    kistage = SAb[:, 1024:1280].rearrange("p (k c) -> p k c", k=8)

    KT = AR[:, 0:4096].rearrange("p (g s) -> p g s", g=2)
    Vb = AR[:, 4096:11264].rearrange("p (k c) -> p k c", k=16)
    kiT = AR[:, 11264:13312]
    isc = AR[:, 13312:17408].bitcast(F32)
    maskbs = [AR[:, 17408:19456], AR[:, 19456:21504]]
    isc_s = AR[:, 0:8256].bitcast(F32)
    maskb_s = AR[:, 8256:12384]
    KT_s = AR[:, 12384:14432].rearrange("p (g s) -> p g s", g=2)
    Vb_s = AR[:, 14432:18016].rearrange("p (k c) -> p k c", k=8)
    kiT_s = AR[:, 18016:19040]
    hT2 = AR[:, 0:16640].rearrange("p (k s) -> p k s", k=8)

    win = WA[:, :].rearrange("p (k n) -> p k n", k=8)
    wout = WB[:, :].rearrange("p (k n) -> p k n", k=8)
    FSL = [(WA, 0), (WA, 8192), (WB, 0)]

    PB = [ps([128, 512]) for _ in range(6)]
    PT = ps([128, 1024], BF16)
    PB7 = ps([128, 512])
    bPB = [Buf(f"PB{i}", xr=True) for i in range(6)]
    bPT = Buf("PT", xr=True)
    bPB7 = Buf("PB7", xr=True)

    bX = [Buf(f"X{t}") for t in range(NT)]
    bWA, bWB = Buf("WA"), Buf("WB")
    bF = [Buf("FA"), Buf("FB"), Buf("FC")]
    bSA, bSB, bSC = Buf("SA"), Buf("SB"), Buf("SC")
    bhbf, bhT = Buf("hbf"), Buf("hT")
    bqkr, bqkbf, bv32, bvbf = Buf(), Buf(), Buf(), Buf()
    bsgv = bSC
    bqki, bqkir, bqibf, bki4, bw8 = Buf(), Buf(), Buf(), Buf(), Buf()
    bxpad, bhin, bcacc = Buf("xpad"), Buf(), Buf()
    bccs = [[Buf(f"cc{i}_{c}") for c in range(8)] for i in range(2)]
    bqTs = [Buf("qT0"), Buf("qT1")]
    bqiT = Buf("qiT")
    brt = [Buf(), Buf()]
    bpT = [Buf(), Buf()]
    brec = Buf()
    baccs = bSC
    _ba = Buf()
    bactT = [_ba, _ba]
    bident, bgv, bcw, bwmT, bbP, bgqk, bp2b, bnh = [Buf() for _ in range(8)]
    bwsf = bSC
    bropeB = [Buf(), Buf()]
    bss, brstd, bss12, br12, brmax, brmin, bw0, bmid, bcnt, btq, bthr, bsteps = [Buf() for _ in range(12)]
    bKT = [Buf(f"KT{t}") for t in range(16)]; bVb = [Buf(f"Vb{t}") for t in range(16)]; bkiT = [Buf(f"kiT{t}") for t in range(16)]
    bisc = Buf("isc"); bmaskbs = [Buf("maskb0"), Buf("maskb1")]
    bisc_s, bmaskb_s, bKT_s, bVb_s, bkiT_s = Buf(), Buf(), Buf(), Buf(), Buf()
    bKTn, bVn, bkiTn = Buf(), Buf(), Buf()
    bkstage, bkistage, bki4s = bSB, bSA, bSA
    bhT2 = [Buf(f"hT2_{t}") for t in range(NT)]
    PROMPT_AR = bKT + bVb + bkiT + [bisc] + bmaskbs
    SAMPLE_AR = [bisc_s, bmaskb_s, bKT_s, bVb_s, bkiT_s]

    ss = sm[:, 0:1]; msq = sm[:, 1:2]; rstd = sm[:, 2:3]
    ss12 = sm[:, 4:16]; ms12 = sm[:, 16:28]; r12 = sm[:, 28:40]
    rmax = sm[:, 40:41]; rmin = sm[:, 41:42]; w0 = sm[:, 42:43]; mid = sm[:, 43:44]
    cntc = sm[:, 44:45]; tq = sm[:, 45:46]; thr = sm[:, 46:47]

    add = S.add

    def nrows(t):
        return 128 if t < 16 else TS

    add("pool", lambda e: e.memset(ident[:], 0.0), writes=[bident])
    add("pool", lambda e: e.affine_select(out=ident[:], in_=ident[:], pattern=[[-1, 128]], compare_op=ALU.not_equal,
                                          fill=1.0, base=0, channel_multiplier=1), reads=[bident], writes=[bident])
    add("pool", lambda e: e.memset(nh[:], -0.5), writes=[bnh])
    for i_ in range(2):
        add("pool", (lambda i_: lambda e: e.memset(qTzs[i_][:].rearrange("p a b c d -> p (a b c d)"), 0.0))(i_), writes=[bqTs[i_]])
    add("sp", lambda e: e.dma_start(out=gv[:], in_=gvec), writes=[bgv], dma=True)
    add("sp", lambda e: e.dma_start(out=cw[:], in_=cwv), writes=[bcw], dma=True)
    add("sp", lambda e: e.dma_start(out=p2b[:], in_=p2.partition_broadcast(128)), writes=[bp2b], dma=True)
    for t in range(16):
        add("sp", (lambda t: lambda e: e.dma_start(out=X[:, t, :], in_=xp[t * 128:(t + 1) * 128, :]))(t), writes=[bX[t]], dma=True)
    add("sp", lambda e: e.dma_start(out=X[0:TS, 16, :], in_=xs), writes=[bX[16]], dma=True)

    def load_mix_weights(l):
        src = w_in[l].rearrange("(k p) n -> p k n", p=128)
        add("pool", lambda e: e.dma_start(out=win[:, :, 0:1300], in_=src[:, :, 0:1300]), writes=[bWA], dma=True)
        add("pool", lambda e: e.dma_start(out=win[:, :, 1300:2600], in_=src[:, :, 1300:2600]), writes=[bWA], dma=True)
        for k in range(8):
            add("dve", (lambda k: lambda e: e.tensor_scalar(out=win[:, k, :], in0=win[:, k, :], scalar1=gv[:, l * 8 + k:l * 8 + k + 1],
                                                            scalar2=None, op0=ALU.mult))(k), reads=[bWA, bgv], writes=[bWA])
        src2 = w_out[l].rearrange("(k p) n -> p k n", p=128)
        add("pool", lambda e: e.dma_start(out=wout[:, :, :], in_=src2), writes=[bWB], dma=True)

    def ffn_views(slot):
        tns, off = FSL[slot]
        wu = tns[:, off:off + 4096].rearrange("p (k n) -> p k n", k=8)
        wd = tns[:, off + 4096:off + 8192].rearrange("p (m n) -> p m n", m=4)
        return wu, wd

    def load_ffn_dma(l, f, slot):
        wu, wd = ffn_views(slot)
        srcu = w_up[l].rearrange("(k p) n -> p k n", p=128)[:, :, f * 512:(f + 1) * 512]
        srcd = w_down[l, f * 512:(f + 1) * 512, :].rearrange("(m p) n -> p m n", p=128)
        b = bF[slot]
        add("pool", lambda e: e.dma_start(out=wu, in_=srcu), writes=[b], dma=True)
        add("pool", lambda e: e.dma_start(out=wd, in_=srcd), writes=[b], dma=True)

    def fold_ffn(l, f, slot):
        wu, wd = ffn_views(slot)
        b = bF[slot]
        for k in range(8):
            add("dve", (lambda k: lambda e: e.tensor_scalar(out=wu[:, k, :], in0=wu[:, k, :],
                                                            scalar1=gv[:, 16 + l * 8 + k:16 + l * 8 + k + 1], scalar2=None,
                                                            op0=ALU.mult))(k), reads=[b, bgv], writes=[b])

    def load_ffn_chunk(l, f, slot):
        load_ffn_dma(l, f, slot)
        fold_ffn(l, f, slot)

    PREF = {}

    def load_layer_params(l):
        add("sp", lambda e: e.dma_start(out=wsf.rearrange("p (h i) -> p h i", h=4), in_=wsT[l]), writes=[bwsf], dma=True)
        add("dve", lambda e: e.tensor_copy(out=wmT[:].rearrange("p h i -> p (h i)"), in_=wsf), reads=[bwsf], writes=[bwmT])
        add("dve", lambda e: e.memset(wmT[64:128, :, 0:64], 0.0), writes=[bwmT])
        for c in range(2):
            for hh in range(2):
                add("sp", (lambda c, hh: lambda e: e.dma_start(out=bP[hh * 64:(hh + 1) * 64, c, :],
                                                               in_=bs[l, 2 * c + hh:2 * c + hh + 1, :].partition_broadcast(64)))(c, hh),
                    writes=[bbP], dma=True)
        add("sp", lambda e: e.dma_start(out=gqk[:, 0:64], in_=qg[l:l + 1, :].partition_broadcast(128)), writes=[bgqk], dma=True)
        add("sp", lambda e: e.dma_start(out=gqk[:, 64:128], in_=kg[l:l + 1, :].partition_broadcast(128)), writes=[bgqk], dma=True)

    def norm_to_hT(t, dstT, bdst, col0):
        nt = nrows(t)
        xt = X[0:nt, t, :]
        add("act", lambda e: e.activation(out=hbf[0:nt, :], in_=xt, func=AF.Square, accum_out=ss[0:nt, :]),
            reads=[bX[t]], writes=[bhbf, bss])
        add("dve", lambda e: e.tensor_scalar(out=msq[0:nt, :], in0=ss[0:nt, :], scalar1=1.0 / D, scalar2=EPS, op0=ALU.mult, op1=ALU.add),
            reads=[bss], writes=[brstd])
        add("pool", lambda e: e.tensor_tensor(out=rstd[0:nt, :], in0=msq[0:nt, :], in1=nh[0:nt, 0:1], op=ALU.pow),
            reads=[brstd, bnh], writes=[brstd])
        add("act", lambda e: e.activation(out=hbf[0:nt, :], in_=xt, func=AF.Copy, scale=rstd[0:nt, :]),
            reads=[bX[t], brstd], writes=[bhbf])

        def tr(e):
            ins = None
            for k in range(8):
                ins = e.transpose(PT[:, k * 128:k * 128 + nt], hbf[0:nt, k * 128:(k + 1) * 128], ident[0:nt, 0:nt])
            return ins
        add("pe", tr, reads=[bhbf, bident], writes=[bPT])
        add("dve", lambda e: e.tensor_copy(out=dstT[:, :, col0:col0 + nt],
                                           in_=PT[:, :].rearrange("p (k s) -> p k s", k=8)[:, :, 0:nt]),
            reads=[bPT], writes=[bdst])

    def rope_ops(src, dst, nt, nh_, half, cos, sin, bsrc, bdst, brope):
        s3 = src.rearrange("p (h d) -> p h d", h=nh_)
        d3 = dst.rearrange("p (h d) -> p h d", h=nh_)
        x1 = s3[:, :, 0:half]; x2 = s3[:, :, half:2 * half]
        cb = cos.unsqueeze(1).to_broadcast([nt, nh_, half])
        sbb = sin.unsqueeze(1).to_broadcast([nt, nh_, half])
        ta = SA[0:nt, 0:nh_ * half].rearrange("p (h d) -> p h d", h=nh_)
        tb = SA[0:nt, 512:512 + nh_ * half].rearrange("p (h d) -> p h d", h=nh_)
        add("dve", lambda e: e.tensor_tensor(out=ta, in0=x1, in1=cb, op=ALU.mult), reads=[bsrc, brope], writes=[bSA])
        add("dve", lambda e: e.tensor_tensor(out=tb, in0=x2, in1=sbb, op=ALU.mult), reads=[bsrc, brope], writes=[bSA])
        add("dve", lambda e: e.tensor_tensor(out=d3[:, :, 0:half], in0=ta, in1=tb, op=ALU.subtract), reads=[bSA], writes=[bdst])
        add("dve", lambda e: e.tensor_tensor(out=ta, in0=x2, in1=cb, op=ALU.mult), reads=[bsrc, brope], writes=[bSA])
        add("dve", lambda e: e.tensor_tensor(out=tb, in0=x1, in1=sbb, op=ALU.mult), reads=[bsrc, brope], writes=[bSA])
        add("dve", lambda e: e.tensor_tensor(out=d3[:, :, half:2 * half], in0=ta, in1=tb, op=ALU.add), reads=[bSA], writes=[bdst])

    def gelu_core(src, T, bsrc_list, bT):
        add("act", lambda e: e.activation(out=T, in_=src, func=AF.Square), reads=bsrc_list, writes=[bT])
        add("dve", lambda e: e.tensor_scalar(out=T, in0=T, scalar1=0.044715, scalar2=1.0, op0=ALU.mult, op1=ALU.add), reads=[bT], writes=[bT])
        add("dve", lambda e: e.tensor_tensor(out=T, in0=T, in1=src, op=ALU.mult), reads=[bT] + bsrc_list, writes=[bT])
        add("act", lambda e: e.activation(out=T, in_=T, func=AF.Tanh, scale=GC), reads=[bT], writes=[bT])
        add("dve", lambda e: e.scalar_tensor_tensor(out=T, in0=T, scalar=1.0, in1=src, op0=ALU.add, op1=ALU.mult),
            reads=[bT] + bsrc_list, writes=[bT])

    VOFF = [0, 128, 192, 320]
    O_ROWS = [64, 0, 64, 0]

    def attn_block(nt, g, tiles, KTsrc, Vsrc, mask_ap, bk_fn, first, last_norm, bmask, qTz, bqT, concatT, bcc):
        gp, gl = g // 2, g % 2
        scb = PB[4]
        bscb = bPB[4]
        PO = PB[5]
        n2 = 2 * nt
        ntiles = len(tiles)
        for i, (kt, ks, mc0) in enumerate(tiles):
            half = (i % 2) * 256
            pti = i % 2

            def sc_fn(e, kt=kt, ks=ks, mc0=mc0, half=half):
                e.matmul(scb[0:ks, half:half + n2].rearrange("p (j t) -> p j t", j=2),
                         lhsT=KTsrc[:, gp, kt * 128:kt * 128 + ks],
                         rhs=qTz[:, gl, gp, :, 0:nt], start=True, stop=False)
                ins = None
                for j in range(2):
                    ins = e.matmul(scb[0:ks, half + j * nt:half + (j + 1) * nt], lhsT=mask_ap[0:nt, mc0:mc0 + ks],
                                   rhs=ident[0:nt, 0:nt], start=False, stop=(j == 1))
                return ins
            add("pe", sc_fn, reads=[bqT, bmask, bident] + bk_fn(kt), writes=[bscb])
            add("act", (lambda ks, half, pti: lambda e: e.activation(out=pT[pti][0:ks, 0:n2], in_=scb[0:ks, half:half + n2],
                                                                      func=AF.Exp, scale=0.125))(ks, half, pti),
                reads=[bscb], writes=[bpT[pti]])
            add("pe", (lambda kt, ks, pti, i: lambda e: e.matmul(PO[:, 0:n2], lhsT=Vsrc[0:ks, kt, VOFF[g]:VOFF[g] + 128],
                                                                 rhs=pT[pti][0:ks, 0:n2], start=(i == 0), stop=(i == ntiles - 1)))(kt, ks, pti, i),
                reads=[bpT[pti]] + bk_fn(kt), writes=[bPB[5]])
            yield
        if last_norm and first:
            src_all = PO
            bsrc = bPB[5]
        else:
            if first:
                add("act", lambda e: e.activation(out=accs[:, g, 0:n2], in_=PO[:, 0:n2], func=AF.Copy), reads=[bPB[5]], writes=[baccs])
            else:
                add("dve", lambda e: e.tensor_tensor(out=accs[:, g, 0:n2], in0=accs[:, g, 0:n2], in1=PO[:, 0:n2], op=ALU.add),
                    reads=[bPB[5], baccs], writes=[baccs])
            src_all = accs[:, g, :]
            bsrc = baccs
        if last_norm:
            orow = O_ROWS[g]
            srow = 64 - orow
            add("dve", lambda e: e.reciprocal(out=rec[orow:orow + 64, 0:n2], in_=src_all[srow:srow + 64, 0:n2]), reads=[bsrc], writes=[brec])
            for j in range(2):
                add("dve", (lambda j: lambda e: e.tensor_tensor(out=concatT[64 * j:64 * j + 64, 4 + g, 0:nt],
                                                                in0=src_all[orow:orow + 64, j * nt:(j + 1) * nt],
                                                                in1=rec[orow:orow + 64, j * nt:(j + 1) * nt], op=ALU.mult))(j),
                    reads=[bsrc, brec], writes=[bcc[4 + g]])

    def indexer_chunk(nt, n0, w, kiTsrc, kcol0, isc_ap, bki_list, bisc_):
        for h in range(8):
            r, c = h % 4, h // 4
            add("pe", (lambda r, c: lambda e: e.matmul(PB[r][0:nt, 0:w], lhsT=qiT[32 * r:32 * r + 32, c, 0:nt],
                                                       rhs=kiTsrc[32 * r:32 * r + 32, kcol0:kcol0 + w], start=True, stop=True,
                                                       tile_position=(32 * r, 0)))(r, c),
                reads=[bqiT] + bki_list, writes=[bPB[r]])
            add("act", (lambda r, h: lambda e: e.activation(out=rt[h % 2][0:nt, 0:w], in_=PB[r][0:nt, 0:w], func=AF.Relu))(r, h),
                reads=[bPB[r]], writes=[brt[h % 2]])
            if h == 0:
                add("dve", lambda e: e.tensor_scalar(out=isc_ap[0:nt, n0:n0 + w], in0=rt[0][0:nt, 0:w], scalar1=w8[0:nt, 0:1],
                                                     scalar2=None, op0=ALU.mult), reads=[brt[0], bw8], writes=[bisc_])
            else:
                add("dve", (lambda h: lambda e: e.scalar_tensor_tensor(out=isc_ap[0:nt, n0:n0 + w], in0=rt[h % 2][0:nt, 0:w],
                                                                       scalar=w8[0:nt, h:h + 1], in1=isc_ap[0:nt, n0:n0 + w],
                                                                       op0=ALU.mult, op1=ALU.add))(h),
                    reads=[brt[h % 2], bw8, bisc_], writes=[bisc_])
            yield

    def topk_mask(nt, Stot, isc_ap, mask_ap, bisc_, bmask, inadm=None):
        iv = isc_ap[0:nt, 0:Stot]
        yield
        if Stot > 256:
            add("dve", lambda e: e.tensor_reduce(out=rmax[0:nt, :], in_=iv, axis=AX.X, op=ALU.max), reads=[bisc_], writes=[brmax])
            add("dve", lambda e: e.tensor_reduce(out=rmin[0:nt, :], in_=iv, axis=AX.X, op=ALU.min), reads=[bisc_], writes=[brmin])
        if inadm is not None:
            add("dve", lambda e: e.memset(isc_ap[0:64, inadm:inadm + 64], NEG), writes=[bisc_])
        if Stot <= 256:
            add("dve", lambda e: e.memset(thr[0:nt, :], 0.5 * NEG), writes=[bthr])
        else:
            add("dve", lambda e: e.tensor_tensor(out=w0[0:nt, :], in0=rmax[0:nt, :], in1=rmin[0:nt, :], op=ALU.subtract),
                reads=[brmax, brmin], writes=[bw0])
            add("dve", lambda e: e.tensor_scalar(out=steps[0:nt, :], in0=p2b[0:nt, :], scalar1=w0[0:nt, :], scalar2=None, op0=ALU.mult),
                reads=[bp2b, bw0], writes=[bsteps])
            add("dve", lambda e: e.tensor_scalar(out=nsteps[0:nt, :], in0=steps[0:nt, :], scalar1=-1.0, scalar2=None, op0=ALU.mult),
                reads=[bsteps], writes=[bsteps])
            add("dve", lambda e: e.tensor_scalar(out=mid[0:nt, :], in0=rmin[0:nt, :], scalar1=steps[0:nt, 0:1], scalar2=-1.0,
                                                 op0=ALU.add, op1=ALU.mult), reads=[brmin, bsteps], writes=[bmid])
            for n in range(NIT):
                add("act", lambda e: e.activation(out=mask_ap[0:nt, 0:Stot], in_=iv, func=AF.Sign, bias=mid[0:nt, :],
                                                  accum_out=cntc[0:nt, :]), reads=[bisc_, bmid], writes=[bmask, bcnt])
                add("act", lambda e: e.activation(out=tq[0:nt, :], in_=cntc[0:nt, :], func=AF.Sign, bias=float(Stot - 511)),
                    reads=[bcnt], writes=[btq])
                add("act", (lambda n: lambda e: e.activation(out=mid[0:nt, :], in_=tq[0:nt, :], func=AF.Identity,
                                                             scale=nsteps[0:nt, n + 1:n + 2], bias=mid[0:nt, :]))(n),
                    reads=[btq, bsteps, bmid], writes=[bmid])
                yield
            add("dve", lambda e: e.tensor_scalar(out=thr[0:nt, :], in0=mid[0:nt, :], scalar1=-1.0, scalar2=steps[0:nt, NIT:NIT + 1],
                                                 op0=ALU.mult, op1=ALU.subtract), reads=[bmid, bsteps], writes=[bthr])
        add("dve", lambda e: e.tensor_scalar(out=mask_ap[0:nt, 0:Stot], in0=iv, scalar1=thr[0:nt, :], scalar2=-30000.0,
                                             op0=ALU.is_lt, op1=ALU.mult), reads=[bisc_, bthr], writes=[bmask])

    def mixer_tile(l, t, part="AB"):
        nt = nrows(t)
        sample = (t == 16)
        par = t % 2
        qTz, bqT, concatT, bcc = qTzs[par], bqTs[par], ccT[par], bccs[par]
        maskb, bmaskb = maskbs[par], bmaskbs[par]
        if "A" in part:
            yield from _mixer_A(l, t, nt, sample, qTz, bqT, concatT, bcc, maskb, bmaskb)
        if "B" in part:
            yield from _mixer_B(l, t, nt, sample, qTz, bqT, concatT, bcc, maskb, bmaskb)

    def _mixer_A(l, t, nt, sample, qTz, bqT, concatT, bcc, maskb, bmaskb):
        rT = ropeB[t % 2]; brT = bropeB[t % 2]
        add("sp", lambda e: e.dma_start(out=rT[:, :], in_=rope[:, t, :]), writes=[brT], dma=True)
        norm_to_hT(t, hT, bhT, 0)
        if DBG["stage"] < 2:
            return
        tm = [(0, 512), (512, 512), (1024, 512), (1536, 40)]
        for n, (c0, w) in enumerate(tm):
            if not DBG["tm"]:
                break

            def f(e, c0=c0, w=w, n=n):
                ins = None
                for k in range(8):
                    ins = e.matmul(PB[n][0:nt, 0:w], lhsT=hT[:, k, 0:nt], rhs=win[:, k, c0:c0 + w], start=(k == 0), stop=(k == 7))
                return ins
            add("pe", f, reads=[bhT, bWA], writes=[bPB[n]])
        for fi in range(8):
            if not DBG["fm"]:
                break
            bank = PB[4] if fi < 4 else PB[5]
            bb = bPB[4] if fi < 4 else bPB[5]

            def f(e, fi=fi, bank=bank):
                ins = None
                for k in range(8):
                    ins = e.matmul(bank[:, (fi % 4) * 128:(fi % 4) * 128 + nt], lhsT=win[:, k, TMW + fi * 128:TMW + (fi + 1) * 128],
                                   rhs=hT[:, k, 0:nt], start=(k == 0), stop=(k == 7))
                return ins
            add("pe", f, reads=[bhT, bWA], writes=[bb])
        PF0 = PB[4][:, :].rearrange("p (f s) -> p f s", f=4)
        PF1 = PB[5][:, :].rearrange("p (f s) -> p f s", f=4)
        if DBG["stage"] < 3:
            return
        add("act", lambda e: e.activation(out=SB_[0:nt, 0:512], in_=PB[0][0:nt, 0:512], func=AF.Square), reads=[bPB[0]], writes=[bSB])
        add("act", lambda e: e.activation(out=SB_[0:nt, 512:768], in_=PB[1][0:nt, 0:256], func=AF.Square), reads=[bPB[1]], writes=[bSB])
        add("dve", lambda e: e.tensor_reduce(out=ss12[0:nt, :], in_=SB_[0:nt, 0:768].rearrange("p (h d) -> p h d", h=12), axis=AX.X, op=ALU.add),
            reads=[bSB], writes=[bss12])
        add("dve", lambda e: e.tensor_scalar(out=ms12[0:nt, :], in0=ss12[0:nt, :], scalar1=1.0 / 64, scalar2=EPS, op0=ALU.mult, op1=ALU.add),
            reads=[bss12], writes=[bss12])
        add("pool", lambda e: e.tensor_tensor(out=r12[0:nt, :], in0=ms12[0:nt, :], in1=nh[0:nt, :], op=ALU.pow), reads=[bss12, bnh], writes=[br12])
        add("dve", lambda e: e.tensor_tensor(out=SB_[0:nt, 0:512].rearrange("p (h d) -> p h d", h=8),
                                             in0=PB[0][0:nt, 0:512].rearrange("p (h d) -> p h d", h=8),
                                             in1=r12[0:nt, 0:8].unsqueeze(2).to_broadcast([nt, 8, 64]), op=ALU.mult),
            reads=[bPB[0], br12], writes=[bSB])
        add("dve", lambda e: e.tensor_tensor(out=SB_[0:nt, 512:768].rearrange("p (h d) -> p h d", h=4),
                                             in0=PB[1][0:nt, 0:256].rearrange("p (h d) -> p h d", h=4),
                                             in1=r12[0:nt, 8:12].unsqueeze(2).to_broadcast([nt, 4, 64]), op=ALU.mult),
            reads=[bPB[1], br12], writes=[bSB])
        add("dve", lambda e: e.tensor_tensor(out=SB_[0:nt, 0:512].rearrange("p (h d) -> p h d", h=8),
                                             in0=SB_[0:nt, 0:512].rearrange("p (h d) -> p h d", h=8),
                                             in1=gqk[0:nt, 0:64].unsqueeze(1).to_broadcast([nt, 8, 64]), op=ALU.mult),
            reads=[bSB, bgqk], writes=[bSB])
        add("dve", lambda e: e.tensor_tensor(out=SB_[0:nt, 512:768].rearrange("p (h d) -> p h d", h=4),
                                             in0=SB_[0:nt, 512:768].rearrange("p (h d) -> p h d", h=4),
                                             in1=gqk[0:nt, 64:128].unsqueeze(1).to_broadcast([nt, 4, 64]), op=ALU.mult),
            reads=[bSB, bgqk], writes=[bSB])
        if DBG["stage"] < 3.1:
            return
        rope_ops(SB_[0:nt, 0:768], qkr[0:nt, :], nt, 12, 32, rT[0:nt, 0:32], rT[0:nt, 32:64], bSB, bqkr, brT)
        if DBG["stage"] < 3.2:
            return
        add("act", lambda e: e.activation(out=v32[0:nt, :], in_=PB[1][0:nt, 256:512], func=AF.Copy), reads=[bPB[1]], writes=[bv32])
        if not sample:
            Vdst = Vb[0:nt, t, 64:448].rearrange("p (a b) -> p a b", a=2)[:, :, 0:128]
            bVdst = bVb[t]
        else:
            Vdst = Vn[0:nt, 64:448].rearrange("p (a b) -> p a b", a=2)[:, :, 0:128]
            bVdst = bVn
        add("dve", lambda e: e.tensor_copy(out=Vdst, in_=PB[1][0:nt, 256:512].rearrange("p (a b) -> p a b", a=2)), reads=[bPB[1]], writes=[bVdst])
        if DBG["stage"] < 3.3:
            return
        add("act", lambda e: e.activation(out=qki[0:nt, 0:256], in_=PB[2][0:nt, 256:512], func=AF.Copy), reads=[bPB[2]], writes=[bqki])
        add("act", lambda e: e.activation(out=qki[0:nt, 256:288], in_=PB[3][0:nt, 0:32], func=AF.Copy), reads=[bPB[3]], writes=[bqki])
        add("act", lambda e: e.activation(out=w8[0:nt, :], in_=PB[3][0:nt, 32:40], func=AF.Copy, scale=0.0625), reads=[bPB[3]], writes=[bw8])
        if DBG["stage"] < 3.4:
            return
        TV = SC[0:nt, 0:256]
        gelu_core(PB[2][0:nt, 0:256], TV, [bPB[2]], bSC)
        add("act", lambda e: e.activation(out=vbf[0:nt, :], in_=TV, func=AF.Copy, scale=0.5), reads=[bSC], writes=[bvbf])
        if sample:
            add("act", lambda e: e.activation(out=sgv[0:nt, :], in_=TV, func=AF.Copy, scale=0.5), reads=[bSC], writes=[bsgv])
            add("sp", lambda e: e.dma_start(out=nsgv[l], in_=sgv[0:nt, :]), reads=[bsgv], dma=True)
        if DBG["stage"] < 3.5:
            return
        rope_ops(qki[0:nt, :], qkir[0:nt, :], nt, 9, 16, rT[0:nt, 64:80], rT[0:nt, 80:96], bqki, bqkir, brT)
        if DBG["stage"] < 3.6:
            return
        if not sample:
            row0 = t * 128
            add("sp", lambda e: e.dma_start(out=nkp[l, row0:row0 + nt, :], in_=qkr[0:nt, 512:768]), reads=[bqkr], dma=True)
            add("sp", lambda e: e.dma_start(out=nvp[l, row0:row0 + nt, :], in_=v32[0:nt, :]), reads=[bv32], dma=True)
            add("sp", lambda e: e.dma_start(out=nkip[l, row0:row0 + nt, :], in_=qkir[0:nt, 256:288]), reads=[bqkir], dma=True)
        else:
            add("sp", lambda e: e.dma_start(out=nks[l], in_=qkr[0:nt, 512:768]), reads=[bqkr], dma=True)
            add("sp", lambda e: e.dma_start(out=nvs[l], in_=v32[0:nt, :]), reads=[bv32], dma=True)
            add("sp", lambda e: e.dma_start(out=nkis[l], in_=qkir[0:nt, 256:288]), reads=[bqkir], dma=True)
        if DBG["stage"] < 3.7:
            return
        add("act", lambda e: e.activation(out=qkbf[0:nt, :], in_=qkr[0:nt, :], func=AF.Copy), reads=[bqkr], writes=[bqkbf])

        def trq(e):
            ins = None
            for b_ in range(6):
                ins = e.transpose(PT[:, b_ * 128:b_ * 128 + nt], qkbf[0:nt, b_ * 128:(b_ + 1) * 128], ident[0:nt, 0:nt])
            return ins
        add("pe", trq, reads=[bqkbf, bident], writes=[bPT])
        PT3 = PT[:, :].rearrange("p (k s) -> p k s", k=8)
        for gl in range(2):
            if DBG.get("skip") == "qTz" or (DBG.get("skip") == "qTz1" and gl == 1):
                continue
            add("dve", (lambda gl: lambda e: e.tensor_copy(
                out=qTz[64 * gl:64 * gl + 64, gl, :, :, 0:nt],
                in_=PT3[64 * gl:64 * gl + 64, 0:4, 0:nt].rearrange("p (a b) s -> p a b s", a=2)))(gl), reads=[bPT], writes=[bqT])
        if DBG.get("skip") == "KT":
            return
        if not sample:
            add("dve", lambda e: e.tensor_copy(out=KT[:, :, t * 128:t * 128 + nt], in_=PT3[:, 4:6, 0:nt]), reads=[bPT], writes=[bKT[t]])
        else:
            add("dve", lambda e: e.tensor_copy(out=KTn[:, :, 0:nt], in_=PT3[:, 4:6, 0:nt]), reads=[bPT], writes=[bKTn])
        if DBG["stage"] < 3.8:
            return
        if DBG.get("skip") == "qibf":
            return
        add("act", lambda e: e.activation(out=qibf[0:nt, :], in_=qkir[0:nt, 0:256], func=AF.Copy), reads=[bqkir], writes=[bqibf])
        if DBG.get("skip") == "ki4":
            return
        add("dve", lambda e: e.tensor_copy(out=ki4[0:nt, :].rearrange("p (r d) -> p r d", r=4),
                                           in_=qkir[0:nt, 256:288].unsqueeze(1).to_broadcast([nt, 4, 32])), reads=[bqkir], writes=[bki4])

        def tri(e):
            e.transpose(PT[:, 0:nt], qibf[0:nt, 0:128], ident[0:nt, 0:nt])
            e.transpose(PT[:, 128:128 + nt], qibf[0:nt, 128:256], ident[0:nt, 0:nt])
            return e.transpose(PT[:, 256:256 + nt], ki4[0:nt, :], ident[0:nt, 0:nt])
        if DBG.get("skip") == "tri":
            return
        add("pe", tri, reads=[bqibf, bki4, bident], writes=[bPT])
        if DBG.get("skip") == "qiT":
            return
        add("dve", lambda e: e.tensor_copy(out=qiT[:, :, 0:nt], in_=PT3[:, 0:2, 0:nt]), reads=[bPT], writes=[bqiT])
        if DBG.get("skip") == "kiT":
            return
        if not sample:
            add("dve", lambda e: e.tensor_copy(out=kiT[:, t * 128:t * 128 + nt], in_=PT[:, 256:256 + nt]), reads=[bPT], writes=[bkiT[t]])
        else:
            add("dve", lambda e: e.tensor_copy(out=kiTn[:, 0:nt], in_=PT[:, 256:256 + nt]), reads=[bPT], writes=[bkiTn])

        if DBG["stage"] < 4:
            return
        TU = SC[:, 256:512].rearrange("p (c s) -> p c s", c=2)[:, :, 0:nt]
        gelu_core(PF0[:, 0:2, 0:nt], TU, [bPB[4]], bSC)
        for c in range(2):
            for hh in range(2):
                h = 2 * c + hh
                add("pe", (lambda c, h: lambda e: e.matmul(PB7[:, h * 128:h * 128 + nt], lhsT=vbf[0:nt, c * 128:(c + 1) * 128],
                                                           rhs=wmT[0:nt, h, 0:nt], start=True, stop=True))(c, h),
                    reads=[bvbf, bwmT], writes=[bPB7])
        for c in range(2):
            for hh in range(2):
                h = 2 * c + hh
                r0 = hh * 64
                tmpv = SC[r0:r0 + 64, 512 + h * 128:512 + h * 128 + nt]
                add("dve", (lambda h, r0, c, tmpv: lambda e: e.tensor_tensor(out=tmpv, in0=PB7[r0:r0 + 64, h * 128:h * 128 + nt],
                                                                             in1=bP[r0:r0 + 64, c, 0:nt], op=ALU.add))(h, r0, c, tmpv),
                    reads=[bPB7, bbP], writes=[bSC])
                add("dve", (lambda r0, c, tmpv: lambda e: e.scalar_tensor_tensor(out=concatT[r0:r0 + 64, c, 0:nt], in0=tmpv, scalar=0.5,
                                                                                 in1=SC[r0:r0 + 64, 256 + c * 128:256 + c * 128 + nt],
                                                                                 op0=ALU.mult, op1=ALU.mult))(r0, c, tmpv),
                    reads=[bSC], writes=[bcc[c]])
        if DBG["stage"] < 5:
            return
        if t == 0:
            add("dve", lambda e: e.memset(xpad[:, :, 0:2], 0.0), writes=[bxpad])
        elif sample:
            add("sp", lambda e: e.dma_start(out=xpad[:, :, 0:2], in_=sconvT[l]), writes=[bxpad], dma=True)
        add("act", lambda e: e.activation(out=hin[:, :, 0:nt], in_=PF1[:, 2:4, 0:nt], func=AF.Copy), reads=[bPB[5]], writes=[bhin])
        add("dve", lambda e: e.tensor_tensor(out=xpad[:, :, 2:2 + nt], in0=PF1[:, 0:2, 0:nt], in1=hin[:, :, 0:nt], op=ALU.mult),
            reads=[bPB[5], bhin], writes=[bxpad])
        for c in range(2):
            ci = l * 6 + c * 3
            add("dve", (lambda c, ci: lambda e: e.tensor_scalar(out=cacc[:, c, 0:nt], in0=xpad[:, c, 0:nt], scalar1=cw[:, ci:ci + 1],
                                                                scalar2=None, op0=ALU.mult))(c, ci), reads=[bxpad, bcw], writes=[bcacc])
            for tap in (1, 2):
                add("dve", (lambda c, ci, tap: lambda e: e.scalar_tensor_tensor(out=cacc[:, c, 0:nt], in0=xpad[:, c, tap:tap + nt],
                                                                                scalar=cw[:, ci + tap:ci + tap + 1], in1=cacc[:, c, 0:nt],
                                                                                op0=ALU.mult, op1=ALU.add))(c, ci, tap),
                    reads=[bxpad, bcw, bcacc], writes=[bcacc])
        add("dve", lambda e: e.tensor_tensor(out=concatT[:, 2:4, 0:nt], in0=PF0[:, 2:4, 0:nt], in1=cacc[:, :, 0:nt], op=ALU.mult),
            reads=[bPB[4], bcacc], writes=[bcc[2], bcc[3]])
        if t == 15:
            add("sp", lambda e: e.dma_start(out=ncp[l], in_=xpad[:, :, nt:nt + 2]), reads=[bxpad], dma=True)
        elif sample:
            add("sp", lambda e: e.dma_start(out=ncs[l], in_=xpad[:, :, nt:nt + 2]), reads=[bxpad], dma=True)
        if t < 15:
            add("act", lambda e: e.activation(out=xpad[:, :, 0:2], in_=xpad[:, :, nt:nt + 2], func=AF.Copy), reads=[bxpad], writes=[bxpad])

        if DBG["stage"] < 6:
            return
        if not sample:
            Stot = 128 * (t + 1)
            for n0 in range(0, Stot, 512):
                w_ = min(512, Stot - n0)
                yield from indexer_chunk(nt, n0, w_, kiT, n0, isc, [bkiT[i_] for i_ in range(n0 // 128, (n0 + w_) // 128)], bisc)
            yield from topk_mask(nt, Stot, isc, maskb, bisc, bmaskb, inadm=t * 128 + 64)
            return
        if True:
            for b_ in SAMPLE_AR:
                seed(b_, PROMPT_AR)
            for blk in range(4):
                s0 = blk * 1024
                add("pool", (lambda s0: lambda e: e.dma_start(out=kistage[:, :, :], in_=cki[l, s0:s0 + 1024, :].rearrange("(k p) d -> p k d", p=128)))(s0),
                    writes=[bkistage], dma=True)
                add("dve", lambda e: e.tensor_copy(out=ki4s[:, :, :].rearrange("p k (r d) -> p k r d", r=4),
                                                   in_=kistage[:, :, :].unsqueeze(2).to_broadcast([128, 8, 4, 32])), reads=[bkistage], writes=[bki4s])

                def trk(e):
                    ins = None
                    for k in range(8):
                        ins = e.transpose(PT[:, k * 128:(k + 1) * 128], ki4s[:, k, :], ident[:, :])
                    return ins
                add("pe", trk, reads=[bki4s, bident], writes=[bPT])
                add("act", lambda e: e.activation(out=kiT_s[:, :], in_=PT[:, :], func=AF.Copy), reads=[bPT], writes=[bkiT_s])
                for n0 in (0, 512):
                    yield from indexer_chunk(nt, s0 + n0, 512, kiT_s, n0, isc_s, [bkiT_s], bisc_s)
            yield from indexer_chunk(nt, PAST, TS, kiTn, 0, isc_s, [bkiTn], bisc_s)
            Stot = PAST + TS
            if DBG["ffn"]:
                seed(bF[0], [bWA]); seed(bF[1], [bWA])
                load_ffn_dma(l, 0, 0); load_ffn_dma(l, 1, 1)
                PREF[l] = True
            yield from topk_mask(nt, Stot, isc_s, maskb_s, bisc_s, bmaskb_s, inadm=None)
            for blk in range(4):
                s0 = blk * 1024
                add("pool", (lambda s0: lambda e: e.dma_start(out=kstage[:, :, :], in_=ck[l, s0:s0 + 1024, :].rearrange("(k p) d -> p k d", p=128)))(s0),
                    writes=[bkstage, bv32], dma=True)
                for a_ in range(2):
                    add("pool", (lambda s0, a_: lambda e: e.dma_start(
                        out=Vb_s[:, :, 64 + a_ * 192:64 + a_ * 192 + 128],
                        in_=cv[l, s0:s0 + 1024, a_ * 128:(a_ + 1) * 128].rearrange("(k p) d -> p k d", p=128)))(s0, a_),
                        writes=[bVb_s], dma=True)
                for half in range(2):
                    def trK(e, half=half):
                        ins = None
                        for k in range(4):
                            for gp in range(2):
                                ins = e.transpose(PT[:, (gp * 4 + k) * 128:(gp * 4 + k + 1) * 128],
                                                  kstage[:, half * 4 + k, gp * 128:(gp + 1) * 128], ident[:, :])
                        return ins
                    add("pe", trK, reads=[bkstage, bident], writes=[bPT])
                    add("act", (lambda half: lambda e: e.activation(out=KT_s[:, :, half * 512:(half + 1) * 512],
                                                                    in_=PT[:, :].rearrange("p (g s) -> p g s", g=2), func=AF.Copy))(half),
                        reads=[bPT], writes=[bKT_s])
                tiles = [(kt, 128, s0 + kt * 128) for kt in range(8)]
                for g in range(4):
                    yield from attn_block(nt, g, tiles, KT_s, Vb_s, maskb_s, (lambda kt: [bKT_s, bVb_s]), blk == 0, False, bmaskb_s, qTz, bqT, concatT, bcc)
            for g in range(4):
                yield from attn_block(nt, g, [(0, TS, PAST)], KTn, Vn[:, :].rearrange("p (k c) -> p k c", k=1), maskb_s, (lambda kt: [bKTn, bVn]), False, True, bmaskb_s,
                           qTz, bqT, concatT, bcc)

        _out_proj(t, nt, concatT, bcc)

    def _mixer_B(l, t, nt, sample, qTz, bqT, concatT, bcc, maskb, bmaskb):
        if sample:
            return
        tiles = [(kt, 128, kt * 128) for kt in range(t + 1)]
        for g in range(4):
            yield from attn_block(nt, g, tiles, KT, Vb, maskb, (lambda kt: [bKT[kt], bVb[kt]]), True, True, bmaskb, qTz, bqT, concatT, bcc)
        _out_proj(t, nt, concatT, bcc)

    def _out_proj(t, nt, concatT, bcc):
        if DBG["stage"] < 8:
            return
        for n in range(2):
            def f(e, n=n):
                ins = None
                for c in range(8):
                    ins = e.matmul(PB[n][0:nt, 0:512], lhsT=concatT[:, c, 0:nt], rhs=wout[:, c, n * 512:(n + 1) * 512],
                                   start=(c == 0), stop=(c == 7))
                return ins
            add("pe", f, reads=bcc + [bWB], writes=[bPB[n]])
            add("dve", (lambda n: lambda e: e.tensor_tensor(out=X[0:nt, t, n * 512:(n + 1) * 512], in0=X[0:nt, t, n * 512:(n + 1) * 512],
                                                            in1=PB[n][0:nt, 0:512], op=ALU.add))(n), reads=[bX[t], bPB[n]], writes=[bX[t]])

    def ffn_layer(l, last):
        seed(_ba, [bqkr, bqkbf])
        allh = Buf("hT2seed")
        seed(allh, PROMPT_AR + SAMPLE_AR)
        for t in range(NT):
            bhT2[t].w = None
            bhT2[t].r = list(allh.r)
            norm_to_hT(t, hT2, bhT2[t], t * 128)
        groups = [(0, 512, [0, 1, 2, 3]), (512, 512, [4, 5, 6, 7]), (1024, 512, [8, 9, 10, 11]), (1536, 512, [12, 13, 14, 15]), (2048, TS, [16])]
        for f in range(8):
            slot = f % 3
            wu, wd = ffn_views(slot)
            for gi, (tok0, gw, tl) in enumerate(groups):
                ab = (f * 5 + gi) % 2
                for m in range(4):
                    def fu(e, m=m, gw=gw, wu=wu, tok0=tok0):
                        ins = None
                        for k in range(8):
                            ins = e.matmul(PB[m][:, 0:gw], lhsT=wu[:, k, m * 128:(m + 1) * 128], rhs=hT2[:, k, tok0:tok0 + gw],
                                           start=(k == 0), stop=(k == 7))
                        return ins
                    add("pe", fu, reads=[bF[slot]] + [bhT2[t] for t in tl], writes=[bPB[m]])
                    add("act", (lambda m, gw: lambda e: e.activation(out=rt[m % 2][:, 0:gw], in_=PB[m][:, 0:gw], func=AF.Relu))(m, gw),
                        reads=[bPB[m]], writes=[brt[m % 2]])
                    add("act", (lambda m, gw, ab: lambda e: e.activation(out=actT[ab][:, m, 0:gw], in_=rt[m % 2][:, 0:gw], func=AF.Square))(m, gw, ab),
                        reads=[brt[m % 2]], writes=[bactT[ab]])
                for ti, t in enumerate(tl):
                    nt = nrows(t)
                    for n in range(2):
                        pd = PB[4 + n] if ti % 2 == 0 else (PB7 if n == 1 else PB[4 + n])
                        bpd = bPB[4 + n] if ti % 2 == 0 else (bPB7 if n == 1 else bPB[4 + n])

                        def fd(e, n=n, ti=ti, pd=pd, nt=nt, ab=ab, wd=wd):
                            ins = None
                            for m in range(4):
                                ins = e.matmul(pd[0:nt, 0:512], lhsT=actT[ab][:, m, ti * 128:ti * 128 + nt], rhs=wd[:, m, n * 512:(n + 1) * 512],
                                               start=(m == 0), stop=(m == 3))
                            return ins
                        add("pe", fd, reads=[bactT[ab], bF[slot]], writes=[bpd])
                        add("dve", (lambda n, t, pd, nt: lambda e: e.tensor_tensor(out=X[0:nt, t, n * 512:(n + 1) * 512],
                                                                               in0=X[0:nt, t, n * 512:(n + 1) * 512], in1=pd[0:nt, 0:512],
                                                                               op=ALU.add))(n, t, pd, nt), reads=[bX[t], bpd], writes=[bX[t]])
            if f + 3 < 8:
                load_ffn_chunk(l, f + 3, slot)

    for l in range(depth):
        if l == 0:
            load_mix_weights(0)
        load_layer_params(l)
        for b_ in PROMPT_AR:
            if l > 0:
                seed(b_, bhT2 + SAMPLE_AR)
        if l > 0:
            seed(bqkr, [_ba]); seed(bqkbf, [_ba])
        for c0 in (0, 192, 384):
            add("dve", (lambda c0: lambda e: e.memset(Vb[:, :, c0:c0 + 64], 1.0))(c0), writes=bVb)
            if l == 0:
                add("dve", (lambda c0: lambda e: e.memset(Vn[:, c0:c0 + 64], 1.0))(c0), writes=[bVn])
        npt = min(16, DBG["tiles"])

        def drive(gens):
            gens = list(gens)
            while gens:
                for g_ in list(gens):
                    try:
                        next(g_)
                    except StopIteration:
                        gens.remove(g_)
        if DBG.get("pipe", True):
            if npt > 0:
                drive([mixer_tile(l, 0, "A")])
            for t in range(npt):
                gl_ = []
                if t + 1 < npt:
                    gl_.append(mixer_tile(l, t + 1, "A"))
                gl_.append(mixer_tile(l, t, "B"))
                drive(gl_)
        else:
            for t in range(npt):
                drive([mixer_tile(l, t, "AB")])
        if DBG["sample"]:
            for c0 in (0, 192, 384):
                add("dve", (lambda c0: lambda e: e.memset(Vb_s[:, :, c0:c0 + 64], 1.0))(c0), writes=[bVb_s], reads=[])
            drive([mixer_tile(l, 16, "AB")])
        if PREF.get(l):
            seed(bF[2], [bWB])
            fold_ffn(l, 0, 0); fold_ffn(l, 1, 1)
            load_ffn_chunk(l, 2, 2)
        else:
            seed(bF[0], [bWA]); seed(bF[1], [bWA]); seed(bF[2], [bWB])
            for f in range(3):
                load_ffn_chunk(l, f, f)
        if DBG["ffn"]:
            ffn_layer(l, l == depth - 1)
        if l + 1 < depth:
            seed(bWA, [bF[0], bF[1]]); seed(bWB, [bF[2]])
            load_mix_weights(l + 1)
    for t in range(16):
        add("sp", (lambda t: lambda e: e.dma_start(out=yp[t * 128:(t + 1) * 128, :], in_=X[:, t, :]))(t), reads=[bX[t]], dma=True)
    add("sp", lambda e: e.dma_start(out=ys, in_=X[0:TS, 16, :]), reads=[bX[16]], dma=True)

    S.finalize(nc, st)
    with nc.Block() as block:
        S.emit(block)
    st.close()
    return nc


def _rope_tables():
    tab = np.zeros((128, NT, 96), np.float32)
    for t in range(NT):
        n = 128 if t < 16 else TS
        pos = (np.arange(n) + (t * 128 if t < 16 else PAST)).astype(np.float32)
        for half, off in ((32, 0), (16, 64)):
            inv = (np.float32(10000.0) ** (-np.arange(half, dtype=np.float32) / np.float32(half))).astype(np.float32)
            ang = (pos[:, None] * inv[None, :]).astype(np.float32)
            tab[:n, t, off:off + half] = np.cos(ang).astype(np.float32)
            tab[:n, t, off + half:off + 2 * half] = np.sin(ang).astype(np.float32)
    return tab


def _win_perm():
    qcols = []
    for gp in range(2):
        for j in range(2):
            for gl in range(2):
                h = 2 * (2 * gp + gl) + j
                qcols.extend(range(1280 + h * 64, 1280 + (h + 1) * 64))
    tmc = qcols + list(range(1792, 2048)) + list(range(2048, 2304)) + list(range(256, 512)) + list(range(2304, 2560)) \
        + list(range(2560, 2592)) + list(range(2592, 2600))
    fmc = list(range(0, 256)) + list(range(512, 1280))
    return np.array(tmc + fmc, dtype=np.int64)


_NC_CACHE = {}


def kernel(x_prompt, x_sample, cache_k, cache_v, cache_kidx, state_conv, g_mix, w_in, q_gain, k_gain, w_s, b_s, conv_w,
           w_out, g_ffn, w_up, w_down):
    f = lambda a: np.ascontiguousarray(np.asarray(a, dtype=np.float32))
    x_prompt, x_sample, cache_k, cache_v, cache_kidx, state_conv = map(f, (x_prompt, x_sample, cache_k, cache_v, cache_kidx, state_conv))
    g_mix, w_in, q_gain, k_gain, w_s, b_s, conv_w, w_out, g_ffn, w_up, w_down = map(
        f, (g_mix, w_in, q_gain, k_gain, w_s, b_s, conv_w, w_out, g_ffn, w_up, w_down))
    if "nc" not in _NC_CACHE:
        _NC_CACHE["nc"] = build_program()
    nc = _NC_CACHE["nc"]
    w_in_p = np.ascontiguousarray(w_in[:, :, _win_perm()])
    gvec = np.ascontiguousarray(np.concatenate([g_mix.reshape(2, 8, 128), g_ffn.reshape(2, 8, 128)], 0).reshape(32, 128).T)
    wsT = np.ascontiguousarray(w_s.transpose(0, 3, 1, 2))
    cwv = np.ascontiguousarray(conv_w.reshape(2, 3, 2, 128).transpose(3, 0, 2, 1).reshape(128, 12))
    rope = _rope_tables()
    p2 = (2.0 ** -(np.arange(32, dtype=np.float64) + 1)).astype(np.float32).reshape(1, 32)
    shared = dict(w_in=w_in_p, w_out=w_out, w_up=w_up, w_down=w_down, gvec=gvec, wsT=wsT, bs=b_s, cwv=cwv, qg=q_gain, kg=k_gain,
                  rope=rope, p2=p2)
    in_maps = []
    for c in range(8):
        m = dict(shared)
        m["xp"] = x_prompt[c]
        m["xs"] = x_sample[c]
        m["ck"] = np.ascontiguousarray(cache_k[:, c].reshape(2, PAST, 256))
        m["cv"] = np.ascontiguousarray(cache_v[:, c].reshape(2, PAST, 256))
        m["cki"] = np.ascontiguousarray(cache_kidx[:, c])
        m["sconvT"] = np.ascontiguousarray(state_conv[:, c].reshape(2, 2, 2, 128).transpose(0, 3, 2, 1))
        in_maps.append(m)
    res = run_bass_kernel_spmd(nc, in_maps, core_ids=list(range(8)))
    R = res.results
    st_ = lambda k: np.stack([np.asarray(R[c][k]) for c in range(8)])
    yp = st_("yp"); ys = st_("ys")
    nkp = st_("nkp").transpose(1, 0, 2, 3).reshape(2, 8, SEQ, 4, 64)
    nvp = st_("nvp").transpose(1, 0, 2, 3).reshape(2, 8, SEQ, 4, 64)
    nkip = st_("nkip").transpose(1, 0, 2, 3)
    cvt = lambda a: a.transpose(1, 0, 4, 3, 2).reshape(2, 8, 2, 256)
    ncp = cvt(st_("ncp")); ncs = cvt(st_("ncs"))
    nks = st_("nks").transpose(1, 0, 2, 3).reshape(2, 8, TS, 4, 64)
    nvs = st_("nvs").transpose(1, 0, 2, 3).reshape(2, 8, TS, 4, 64)
    nkis = st_("nkis").transpose(1, 0, 2, 3)
    nsgv = st_("nsgv").transpose(1, 0, 2, 3)
    out = (yp, ys, nkp, nvp, nkip, ncp, nks, nvs, nkis, ncs, nsgv)
    return tuple(np.ascontiguousarray(o, dtype=np.float32) for o in out)
```

```python
import contextlib
import numpy as np
import concourse.bass as bass
import concourse.mybir as mybir
from concourse.bass_utils import run_bass_kernel_spmd

F32 = mybir.dt.float32
BF16 = mybir.dt.bfloat16
ALU = mybir.AluOpType
AF = mybir.ActivationFunctionType
AX = mybir.AxisListType

D = 1024
SEQ = 2048
NT = 17
TS = 32
PAST = 4096
DEPTH = 2
INW = 2600
NIT = 14
EPS = 1e-6
NEG = -1.0e30
TMW = 1576
GC = 0.7978845608028654


class Buf:
    __slots__ = ("name", "w", "r", "xr")

    def __init__(self, name="b", xr=False):
        self.name = name
        self.w = None
        self.r = []
        self.xr = xr


def seed(new, olds):
    lst = []
    for o in olds:
        if o.w is not None:
            lst.append(o.w)
        lst.extend(o.r)
    new.w = None
    new.r = lst


class Op:
    __slots__ = ("eng", "fn", "deps", "signal", "dma", "semi", "val", "lane")


class Sched:
    ENGS = ("pe", "act", "dve", "pool", "sp")
    LIMIT = 30000
    NLANES = 8

    def __init__(self):
        self.ops = {e: [] for e in self.ENGS}

    def add(self, eng, fn, reads=(), writes=(), dma=False):
        op = Op()
        op.eng = eng
        op.fn = fn
        op.signal = False
        op.dma = dma
        deps = {}
        for b in reads:
            if b.w is not None:
                deps[id(b.w)] = (b.w, True)
            if b.xr:
                for r in b.r:
                    if r.eng != eng and id(r) not in deps:
                        deps[id(r)] = (r, False)
        for b in writes:
            if b.w is not None and id(b.w) not in deps:
                deps[id(b.w)] = (b.w, False)
            for r in b.r:
                if id(r) not in deps:
                    deps[id(r)] = (r, False)
        op.deps = []
        for d, raw in deps.values():
            if d is op:
                continue
            if not d.dma and not dma and d.eng == eng:
                if eng == "pe":
                    continue
            if not d.dma:
                d.signal = True
            op.deps.append(d)
        for b in reads:
            b.r.append(op)
        for b in writes:
            b.w = op
            b.r = []
        self.ops[eng].append(op)
        return op

    def finalize(self, nc, stack):
        self.sems = {}
        self.final_dma = []
        keys = set()
        for e in self.ENGS:
            cnt = 0
            nd = 0
            lane_cnt = [0] * self.NLANES
            nl = 2 if e == "pool" else self.NLANES
            for op in self.ops[e]:
                if op.dma:
                    op.lane = nd % nl
                    nd += 1
                    lane_cnt[op.lane] += 1
                    op.val = 16 * lane_cnt[op.lane]
                    op.semi = ("dma", e, op.lane)
                    keys.add(op.semi)
                elif op.signal:
                    cnt += 1
                    op.semi = ("c", e, (cnt - 1) // self.LIMIT)
                    op.val = (cnt - 1) % self.LIMIT + 1
                    keys.add(op.semi)
            for ln in range(self.NLANES):
                if lane_cnt[ln]:
                    self.final_dma.append((("dma", e, ln), 16 * lane_cnt[ln]))
        for k in sorted(keys, key=str):
            self.sems[k] = stack.enter_context(nc.semaphore("s_" + "_".join(map(str, k))))

    def emit(self, block):
        def run(engname):
            def body(eng):
                seen = {}
                for op in self.ops[engname]:
                    waits = [(d.semi, d.val) for d in op.deps]
                    if op.dma and op.val > 16:
                        waits.append((op.semi, op.val - 16))
                    for k, v in waits:
                        if seen.get(k, 0) < v:
                            eng.wait_ge(self.sems[k], v)
                            seen[k] = v
                    ins = op.fn(eng)
                    if op.dma:
                        ins.then_inc(self.sems[op.semi], 16)
                    elif op.signal:
                        ins.then_inc(self.sems[op.semi], 1)
                if engname == "sp":
                    for k, v in self.final_dma:
                        if seen.get(k, 0) < v:
                            eng.wait_ge(self.sems[k], v)
            return body
        block.tensor(run("pe"))
        block.scalar(run("act"))
        block.vector(run("dve"))
        block.gpsimd(run("pool"))
        block.sync(run("sp"))


class _Stop(Exception):
    pass


DBG = {"tiles": NT, "stage": 99, "ffn": True, "depth": DEPTH, "sample": True, "tm": True, "fm": True}


def build_program(depth=DEPTH):
    depth = min(depth, DBG["depth"])
    nc = bass.Bass("TRN2", target_bir_lowering=False, dynamic_dma_scratch_size=6144)
    S = Sched()

    def din(name, shape):
        return nc.dram_tensor(name, list(shape), F32, kind="ExternalInput").ap()

    def dout(name, shape):
        return nc.dram_tensor(name, list(shape), F32, kind="ExternalOutput").ap()

    xp = din("xp", [SEQ, D]); xs = din("xs", [TS, D])
    ck = din("ck", [DEPTH, PAST, 256]); cv = din("cv", [DEPTH, PAST, 256]); cki = din("cki", [DEPTH, PAST, 32])
    sconvT = din("sconvT", [DEPTH, 128, 2, 2])
    w_in = din("w_in", [DEPTH, D, INW]); w_out = din("w_out", [DEPTH, D, D])
    w_up = din("w_up", [DEPTH, D, 4 * D]); w_down = din("w_down", [DEPTH, 4 * D, D])
    gvec = din("gvec", [128, 32]); wsT = din("wsT", [DEPTH, 128, 4, 128]); bs = din("bs", [DEPTH, 4, 128])
    cwv = din("cwv", [128, 12]); qg = din("qg", [DEPTH, 64]); kg = din("kg", [DEPTH, 64])
    rope = din("rope", [128, NT, 96]); p2 = din("p2", [1, 32])
    yp = dout("yp", [SEQ, D]); ys = dout("ys", [TS, D])
    nkp = dout("nkp", [DEPTH, SEQ, 256]); nvp = dout("nvp", [DEPTH, SEQ, 256]); nkip = dout("nkip", [DEPTH, SEQ, 32])
    ncp = dout("ncp", [DEPTH, 128, 2, 2])
    nks = dout("nks", [DEPTH, TS, 256]); nvs = dout("nvs", [DEPTH, TS, 256]); nkis = dout("nkis", [DEPTH, TS, 32])
    ncs = dout("ncs", [DEPTH, 128, 2, 2]); nsgv = dout("nsgv", [DEPTH, TS, 256])

    st = contextlib.ExitStack()
    cnt = [0]

    def sb(shape, dt=F32, name=None):
        cnt[0] += 1
        return st.enter_context(nc.sbuf_tensor(name or f"sb{cnt[0]}", list(shape), dt))

    def ps(shape, dt=F32):
        cnt[0] += 1
        return st.enter_context(nc.psum_tensor(f"ps{cnt[0]}", list(shape), dt))

    X = sb([128, NT, D], F32, "X")
    WA = sb([128, 8 * INW], BF16, "WA")
    WB = sb([128, 8 * D], BF16, "WB")
    AR = sb([128, 21504], BF16, "AR")
    SA = sb([128, 1024], F32, "SA"); SB_ = sb([128, 1024], F32, "SBs"); SC = sb([128, 1024], F32, "SC")
    hbf = sb([128, 1024], BF16, "hbf")
    hT = sb([128, 8, 128], BF16, "hT")
    P12 = sb([128, 2304], BF16, "P12")
    qkr = P12[:, 0:1536].bitcast(F32); qkbf = P12[:, 1536:2304]
    v32 = SB_[:, 768:1024]; vbf = sb([128, 256], BF16, "vbf"); sgv = SC[:, 768:1024]
    qki = sb([128, 288], F32, "qki"); qkir = sb([128, 288], F32, "qkir"); qibf = sb([128, 256], BF16, "qibf")
    ki4 = sb([128, 128], BF16, "ki4")
    w8 = sb([128, 8], F32, "w8")
    xpad = sb([128, 2, 130], F32, "xpad"); hin = sb([128, 2, 128], F32, "hin"); cacc = sb([128, 2, 128], F32, "cacc")
    ccT = [sb([128, 8, 128], BF16, f"concatT{i}") for i in range(2)]
    qTzs = [sb([128, 2, 2, 2, 128], BF16, f"qTz{i}") for i in range(2)]; qiT = sb([128, 2, 128], BF16, "qiT")
    rt = [sb([128, 512], F32, f"rt{i}") for i in range(2)]
    pT = [sb([128, 256], BF16, f"pT{i}") for i in range(2)]
    rec = sb([128, 256], F32, "rec")
    accs = SC[:, 0:256].rearrange("p (g c) -> p g c", g=4)
    actT0 = P12[:, 0:2048].rearrange("p (m s) -> p m s", m=4)
    actT = [actT0, actT0]
    ident = sb([128, 128], BF16, "ident")
    ropeB = [sb([128, 96], F32, f"ropeB{i}") for i in range(2)]
    gv = sb([128, 32], F32, "gv"); cw = sb([128, 12], F32, "cw")
    wsf = SC[:, 0:512]; wmT = sb([128, 4, 128], BF16, "wmT")
    bP = sb([128, 2, 128], F32, "bP"); gqk = sb([128, 128], F32, "gqk")
    p2b = sb([128, 32], F32, "p2b"); nh = sb([128, 12], F32, "nh")
    sm = sb([128, 64], F32, "sm")
    steps = sb([128, 32], F32, "steps"); nsteps = sb([128, 32], F32, "nsteps")
    KTn = sb([128, 2, 32], BF16, "KTn"); Vn = sb([128, 448], BF16, "Vn"); kiTn = sb([128, 32], BF16, "kiTn")
    kstage = SB_[:, :].bitcast(BF16).rearrange("p (k c) -> p k c", k=8)
    SAb = SA[:, :].bitcast(BF16)
    ki4s = SAb[:, 0:1024].rearrange("p (k c) -> p k c", k=8)
    kistage = SAb[:, 1024:1280].rearrange("p (k c) -> p k c", k=8)

    KT = AR[:, 0:4096].rearrange("p (g s) -> p g s", g=2)
    Vb = AR[:, 4096:11264].rearrange("p (k c) -> p k c", k=16)
    kiT = AR[:, 11264:13312]
    isc = AR[:, 13312:17408].bitcast(F32)
    maskbs = [AR[:, 17408:19456], AR[:, 19456:21504]]
    isc_s = AR[:, 0:8256].bitcast(F32)
    maskb_s = AR[:, 8256:12384]
    KT_s = AR[:, 12384:14432].rearrange("p (g s) -> p g s", g=2)
    Vb_s = AR[:, 14432:18016].rearrange("p (k c) -> p k c", k=8)
    kiT_s = AR[:, 18016:19040]
    hT2 = AR[:, 0:16640].rearrange("p (k s) -> p k s", k=8)

    win = WA[:, :].rearrange("p (k n) -> p k n", k=8)
    wout = WB[:, :].rearrange("p (k n) -> p k n", k=8)
    FSL = [(WA, 0), (WA, 8192), (WB, 0)]

    PB = [ps([128, 512]) for _ in range(6)]
    PT = ps([128, 1024], BF16)
    PB7 = ps([128, 512])
    bPB = [Buf(f"PB{i}", xr=True) for i in range(6)]
    bPT = Buf("PT", xr=True)
    bPB7 = Buf("PB7", xr=True)

    bX = [Buf(f"X{t}") for t in range(NT)]
    bWA, bWB = Buf("WA"), Buf("WB")
    bF = [Buf("FA"), Buf("FB"), Buf("FC")]
    bSA, bSB, bSC = Buf("SA"), Buf("SB"), Buf("SC")
    bhbf, bhT = Buf("hbf"), Buf("hT")
    bqkr, bqkbf, bv32, bvbf = Buf(), Buf(), Buf(), Buf()
    bsgv = bSC
    bqki, bqkir, bqibf, bki4, bw8 = Buf(), Buf(), Buf(), Buf(), Buf()
    bxpad, bhin, bcacc = Buf("xpad"), Buf(), Buf()
    bccs = [[Buf(f"cc{i}_{c}") for c in range(8)] for i in range(2)]
    bqTs = [Buf("qT0"), Buf("qT1")]
    bqiT = Buf("qiT")
    brt = [Buf(), Buf()]
    bpT = [Buf(), Buf()]
    brec = Buf()
    baccs = bSC
    _ba = Buf()
    bactT = [_ba, _ba]
    bident, bgv, bcw, bwmT, bbP, bgqk, bp2b, bnh = [Buf() for _ in range(8)]
    bwsf = bSC
    bropeB = [Buf(), Buf()]
    bss, brstd, bss12, br12, brmax, brmin, bw0, bmid, bcnt, btq, bthr, bsteps = [Buf() for _ in range(12)]
    bKT = [Buf(f"KT{t}") for t in range(16)]; bVb = [Buf(f"Vb{t}") for t in range(16)]; bkiT = [Buf(f"kiT{t}") for t in range(16)]
    bisc = Buf("isc"); bmaskbs = [Buf("maskb0"), Buf("maskb1")]
    bisc_s, bmaskb_s, bKT_s, bVb_s, bkiT_s = Buf(), Buf(), Buf(), Buf(), Buf()
    bKTn, bVn, bkiTn = Buf(), Buf(), Buf()
    bkstage, bkistage, bki4s = bSB, bSA, bSA
    bhT2 = [Buf(f"hT2_{t}") for t in range(NT)]
    PROMPT_AR = bKT + bVb + bkiT + [bisc] + bmaskbs
    SAMPLE_AR = [bisc_s, bmaskb_s, bKT_s, bVb_s, bkiT_s]

    ss = sm[:, 0:1]; msq = sm[:, 1:2]; rstd = sm[:, 2:3]
    ss12 = sm[:, 4:16]; ms12 = sm[:, 16:28]; r12 = sm[:, 28:40]
    rmax = sm[:, 40:41]; rmin = sm[:, 41:42]; w0 = sm[:, 42:43]; mid = sm[:, 43:44]
    cntc = sm[:, 44:45]; tq = sm[:, 45:46]; thr = sm[:, 46:47]

    add = S.add

    def nrows(t):
        return 128 if t < 16 else TS

    add("pool", lambda e: e.memset(ident[:], 0.0), writes=[bident])
    add("pool", lambda e: e.affine_select(out=ident[:], in_=ident[:], pattern=[[-1, 128]], compare_op=ALU.not_equal,
                                          fill=1.0, base=0, channel_multiplier=1), reads=[bident], writes=[bident])
    add("pool", lambda e: e.memset(nh[:], -0.5), writes=[bnh])
    for i_ in range(2):
        add("pool", (lambda i_: lambda e: e.memset(qTzs[i_][:].rearrange("p a b c d -> p (a b c d)"), 0.0))(i_), writes=[bqTs[i_]])
    add("sp", lambda e: e.dma_start(out=gv[:], in_=gvec), writes=[bgv], dma=True)
    add("sp", lambda e: e.dma_start(out=cw[:], in_=cwv), writes=[bcw], dma=True)
    add("sp", lambda e: e.dma_start(out=p2b[:], in_=p2.partition_broadcast(128)), writes=[bp2b], dma=True)
    for t in range(16):
        add("sp", (lambda t: lambda e: e.dma_start(out=X[:, t, :], in_=xp[t * 128:(t + 1) * 128, :]))(t), writes=[bX[t]], dma=True)
    add("sp", lambda e: e.dma_start(out=X[0:TS, 16, :], in_=xs), writes=[bX[16]], dma=True)

    def load_mix_weights(l):
        src = w_in[l].rearrange("(k p) n -> p k n", p=128)
        add("pool", lambda e: e.dma_start(out=win[:, :, 0:1300], in_=src[:, :, 0:1300]), writes=[bWA], dma=True)
        add("pool", lambda e: e.dma_start(out=win[:, :, 1300:2600], in_=src[:, :, 1300:2600]), writes=[bWA], dma=True)
        for k in range(8):
            add("dve", (lambda k: lambda e: e.tensor_scalar(out=win[:, k, :], in0=win[:, k, :], scalar1=gv[:, l * 8 + k:l * 8 + k + 1],
                                                            scalar2=None, op0=ALU.mult))(k), reads=[bWA, bgv], writes=[bWA])
        src2 = w_out[l].rearrange("(k p) n -> p k n", p=128)
        add("pool", lambda e: e.dma_start(out=wout[:, :, :], in_=src2), writes=[bWB], dma=True)

    def ffn_views(slot):
        tns, off = FSL[slot]
        wu = tns[:, off:off + 4096].rearrange("p (k n) -> p k n", k=8)
        wd = tns[:, off + 4096:off + 8192].rearrange("p (m n) -> p m n", m=4)
        return wu, wd

    def load_ffn_dma(l, f, slot):
        wu, wd = ffn_views(slot)
        srcu = w_up[l].rearrange("(k p) n -> p k n", p=128)[:, :, f * 512:(f + 1) * 512]
        srcd = w_down[l, f * 512:(f + 1) * 512, :].rearrange("(m p) n -> p m n", p=128)
        b = bF[slot]
        add("pool", lambda e: e.dma_start(out=wu, in_=srcu), writes=[b], dma=True)
        add("pool", lambda e: e.dma_start(out=wd, in_=srcd), writes=[b], dma=True)

    def fold_ffn(l, f, slot):
        wu, wd = ffn_views(slot)
        b = bF[slot]
        for k in range(8):
            add("dve", (lambda k: lambda e: e.tensor_scalar(out=wu[:, k, :], in0=wu[:, k, :],
                                                            scalar1=gv[:, 16 + l * 8 + k:16 + l * 8 + k + 1], scalar2=None,
                                                            op0=ALU.mult))(k), reads=[b, bgv], writes=[b])

    def load_ffn_chunk(l, f, slot):
        load_ffn_dma(l, f, slot)
        fold_ffn(l, f, slot)

    PREF = {}

    def load_layer_params(l):
        add("sp", lambda e: e.dma_start(out=wsf.rearrange("p (h i) -> p h i", h=4), in_=wsT[l]), writes=[bwsf], dma=True)
        add("dve", lambda e: e.tensor_copy(out=wmT[:].rearrange("p h i -> p (h i)"), in_=wsf), reads=[bwsf], writes=[bwmT])
        add("dve", lambda e: e.memset(wmT[64:128, :, 0:64], 0.0), writes=[bwmT])
        for c in range(2):
            for hh in range(2):
                add("sp", (lambda c, hh: lambda e: e.dma_start(out=bP[hh * 64:(hh + 1) * 64, c, :],
                                                               in_=bs[l, 2 * c + hh:2 * c + hh + 1, :].partition_broadcast(64)))(c, hh),
                    writes=[bbP], dma=True)
        add("sp", lambda e: e.dma_start(out=gqk[:, 0:64], in_=qg[l:l + 1, :].partition_broadcast(128)), writes=[bgqk], dma=True)
        add("sp", lambda e: e.dma_start(out=gqk[:, 64:128], in_=kg[l:l + 1, :].partition_broadcast(128)), writes=[bgqk], dma=True)

    def norm_to_hT(t, dstT, bdst, col0):
        nt = nrows(t)
        xt = X[0:nt, t, :]
        add("act", lambda e: e.activation(out=hbf[0:nt, :], in_=xt, func=AF.Square, accum_out=ss[0:nt, :]),
            reads=[bX[t]], writes=[bhbf, bss])
        add("dve", lambda e: e.tensor_scalar(out=msq[0:nt, :], in0=ss[0:nt, :], scalar1=1.0 / D, scalar2=EPS, op0=ALU.mult, op1=ALU.add),
            reads=[bss], writes=[brstd])
        add("pool", lambda e: e.tensor_tensor(out=rstd[0:nt, :], in0=msq[0:nt, :], in1=nh[0:nt, 0:1], op=ALU.pow),
            reads=[brstd, bnh], writes=[brstd])
        add("act", lambda e: e.activation(out=hbf[0:nt, :], in_=xt, func=AF.Copy, scale=rstd[0:nt, :]),
            reads=[bX[t], brstd], writes=[bhbf])

        def tr(e):
            ins = None
            for k in range(8):
                ins = e.transpose(PT[:, k * 128:k * 128 + nt], hbf[0:nt, k * 128:(k + 1) * 128], ident[0:nt, 0:nt])
            return ins
        add("pe", tr, reads=[bhbf, bident], writes=[bPT])
        add("dve", lambda e: e.tensor_copy(out=dstT[:, :, col0:col0 + nt],
                                           in_=PT[:, :].rearrange("p (k s) -> p k s", k=8)[:, :, 0:nt]),
            reads=[bPT], writes=[bdst])

    def rope_ops(src, dst, nt, nh_, half, cos, sin, bsrc, bdst, brope):
        s3 = src.rearrange("p (h d) -> p h d", h=nh_)
        d3 = dst.rearrange("p (h d) -> p h d", h=nh_)
        x1 = s3[:, :, 0:half]; x2 = s3[:, :, half:2 * half]
        cb = cos.unsqueeze(1).to_broadcast([nt, nh_, half])
        sbb = sin.unsqueeze(1).to_broadcast([nt, nh_, half])
        ta = SA[0:nt, 0:nh_ * half].rearrange("p (h d) -> p h d", h=nh_)
        tb = SA[0:nt, 512:512 + nh_ * half].rearrange("p (h d) -> p h d", h=nh_)
        add("dve", lambda e: e.tensor_tensor(out=ta, in0=x1, in1=cb, op=ALU.mult), reads=[bsrc, brope], writes=[bSA])
        add("dve", lambda e: e.tensor_tensor(out=tb, in0=x2, in1=sbb, op=ALU.mult), reads=[bsrc, brope], writes=[bSA])
        add("dve", lambda e: e.tensor_tensor(out=d3[:, :, 0:half], in0=ta, in1=tb, op=ALU.subtract), reads=[bSA], writes=[bdst])
        add("dve", lambda e: e.tensor_tensor(out=ta, in0=x2, in1=cb, op=ALU.mult), reads=[bsrc, brope], writes=[bSA])
        add("dve", lambda e: e.tensor_tensor(out=tb, in0=x1, in1=sbb, op=ALU.mult), reads=[bsrc, brope], writes=[bSA])
        add("dve", lambda e: e.tensor_tensor(out=d3[:, :, half:2 * half], in0=ta, in1=tb, op=ALU.add), reads=[bSA], writes=[bdst])

    def gelu_core(src, T, bsrc_list, bT):
        add("act", lambda e: e.activation(out=T, in_=src, func=AF.Square), reads=bsrc_list, writes=[bT])
        add("dve", lambda e: e.tensor_scalar(out=T, in0=T, scalar1=0.044715, scalar2=1.0, op0=ALU.mult, op1=ALU.add), reads=[bT], writes=[bT])
        add("dve", lambda e: e.tensor_tensor(out=T, in0=T, in1=src, op=ALU.mult), reads=[bT] + bsrc_list, writes=[bT])
        add("act", lambda e: e.activation(out=T, in_=T, func=AF.Tanh, scale=GC), reads=[bT], writes=[bT])
        add("dve", lambda e: e.scalar_tensor_tensor(out=T, in0=T, scalar=1.0, in1=src, op0=ALU.add, op1=ALU.mult),
            reads=[bT] + bsrc_list, writes=[bT])

    VOFF = [0, 128, 192, 320]
    O_ROWS = [64, 0, 64, 0]

    def attn_block(nt, g, tiles, KTsrc, Vsrc, mask_ap, bk_fn, first, last_norm, bmask, qTz, bqT, concatT, bcc):
        gp, gl = g // 2, g % 2
        scb = PB[4]
        bscb = bPB[4]
        PO = PB[5]
        n2 = 2 * nt
        ntiles = len(tiles)
        for i, (kt, ks, mc0) in enumerate(tiles):
            half = (i % 2) * 256
            pti = i % 2

            def sc_fn(e, kt=kt, ks=ks, mc0=mc0, half=half):
                e.matmul(scb[0:ks, half:half + n2].rearrange("p (j t) -> p j t", j=2),
                         lhsT=KTsrc[:, gp, kt * 128:kt * 128 + ks],
                         rhs=qTz[:, gl, gp, :, 0:nt], start=True, stop=False)
                ins = None
                for j in range(2):
                    ins = e.matmul(scb[0:ks, half + j * nt:half + (j + 1) * nt], lhsT=mask_ap[0:nt, mc0:mc0 + ks],
                                   rhs=ident[0:nt, 0:nt], start=False, stop=(j == 1))
                return ins
            add("pe", sc_fn, reads=[bqT, bmask, bident] + bk_fn(kt), writes=[bscb])
            add("act", (lambda ks, half, pti: lambda e: e.activation(out=pT[pti][0:ks, 0:n2], in_=scb[0:ks, half:half + n2],
                                                                      func=AF.Exp, scale=0.125))(ks, half, pti),
                reads=[bscb], writes=[bpT[pti]])
            add("pe", (lambda kt, ks, pti, i: lambda e: e.matmul(PO[:, 0:n2], lhsT=Vsrc[0:ks, kt, VOFF[g]:VOFF[g] + 128],
                                                                 rhs=pT[pti][0:ks, 0:n2], start=(i == 0), stop=(i == ntiles - 1)))(kt, ks, pti, i),
                reads=[bpT[pti]] + bk_fn(kt), writes=[bPB[5]])
            yield
        if last_norm and first:
            src_all = PO
            bsrc = bPB[5]
        else:
            if first:
                add("act", lambda e: e.activation(out=accs[:, g, 0:n2], in_=PO[:, 0:n2], func=AF.Copy), reads=[bPB[5]], writes=[baccs])
            else:
                add("dve", lambda e: e.tensor_tensor(out=accs[:, g, 0:n2], in0=accs[:, g, 0:n2], in1=PO[:, 0:n2], op=ALU.add),
                    reads=[bPB[5], baccs], writes=[baccs])
            src_all = accs[:, g, :]
            bsrc = baccs
        if last_norm:
            orow = O_ROWS[g]
            srow = 64 - orow
            add("dve", lambda e: e.reciprocal(out=rec[orow:orow + 64, 0:n2], in_=src_all[srow:srow + 64, 0:n2]), reads=[bsrc], writes=[brec])
            for j in range(2):
                add("dve", (lambda j: lambda e: e.tensor_tensor(out=concatT[64 * j:64 * j + 64, 4 + g, 0:nt],
                                                                in0=src_all[orow:orow + 64, j * nt:(j + 1) * nt],
                                                                in1=rec[orow:orow + 64, j * nt:(j + 1) * nt], op=ALU.mult))(j),
                    reads=[bsrc, brec], writes=[bcc[4 + g]])

    def indexer_chunk(nt, n0, w, kiTsrc, kcol0, isc_ap, bki_list, bisc_):
        for h in range(8):
            r, c = h % 4, h // 4
            add("pe", (lambda r, c: lambda e: e.matmul(PB[r][0:nt, 0:w], lhsT=qiT[32 * r:32 * r + 32, c, 0:nt],
                                                       rhs=kiTsrc[32 * r:32 * r + 32, kcol0:kcol0 + w], start=True, stop=True,
                                                       tile_position=(32 * r, 0)))(r, c),
                reads=[bqiT] + bki_list, writes=[bPB[r]])
            add("act", (lambda r, h: lambda e: e.activation(out=rt[h % 2][0:nt, 0:w], in_=PB[r][0:nt, 0:w], func=AF.Relu))(r, h),
                reads=[bPB[r]], writes=[brt[h % 2]])
            if h == 0:
                add("dve", lambda e: e.tensor_scalar(out=isc_ap[0:nt, n0:n0 + w], in0=rt[0][0:nt, 0:w], scalar1=w8[0:nt, 0:1],
                                                     scalar2=None, op0=ALU.mult), reads=[brt[0], bw8], writes=[bisc_])
            else:
                add("dve", (lambda h: lambda e: e.scalar_tensor_tensor(out=isc_ap[0:nt, n0:n0 + w], in0=rt[h % 2][0:nt, 0:w],
                                                                       scalar=w8[0:nt, h:h + 1], in1=isc_ap[0:nt, n0:n0 + w],
                                                                       op0=ALU.mult, op1=ALU.add))(h),
                    reads=[brt[h % 2], bw8, bisc_], writes=[bisc_])
            yield

    def topk_mask(nt, Stot, isc_ap, mask_ap, bisc_, bmask, inadm=None):
        iv = isc_ap[0:nt, 0:Stot]
        yield
        if Stot > 256:
            add("dve", lambda e: e.tensor_reduce(out=rmax[0:nt, :], in_=iv, axis=AX.X, op=ALU.max), reads=[bisc_], writes=[brmax])
            add("dve", lambda e: e.tensor_reduce(out=rmin[0:nt, :], in_=isc_ap[0:nt, 0:256], axis=AX.X, op=ALU.min), reads=[bisc_], writes=[brmin])
        if inadm is not None:
            add("dve", lambda e: e.memset(isc_ap[0:64, inadm:inadm + 64], NEG), writes=[bisc_])
        if Stot <= 256:
            add("dve", lambda e: e.memset(thr[0:nt, :], 0.5 * NEG), writes=[bthr])
        else:
            add("dve", lambda e: e.tensor_tensor(out=w0[0:nt, :], in0=rmax[0:nt, :], in1=rmin[0:nt, :], op=ALU.subtract),
                reads=[brmax, brmin], writes=[bw0])
            add("dve", lambda e: e.tensor_scalar(out=steps[0:nt, :], in0=p2b[0:nt, :], scalar1=w0[0:nt, :], scalar2=None, op0=ALU.mult),
                reads=[bp2b, bw0], writes=[bsteps])
            add("dve", lambda e: e.tensor_scalar(out=nsteps[0:nt, :], in0=steps[0:nt, :], scalar1=-1.0, scalar2=None, op0=ALU.mult),
                reads=[bsteps], writes=[bsteps])
            add("dve", lambda e: e.tensor_scalar(out=mid[0:nt, :], in0=rmin[0:nt, :], scalar1=steps[0:nt, 0:1], scalar2=-1.0,
                                                 op0=ALU.add, op1=ALU.mult), reads=[brmin, bsteps], writes=[bmid])
            for n in range(NIT):
                add("act", lambda e: e.activation(out=mask_ap[0:nt, 0:Stot], in_=iv, func=AF.Sign, bias=mid[0:nt, :],
                                                  accum_out=cntc[0:nt, :]), reads=[bisc_, bmid], writes=[bmask, bcnt])
                add("act", lambda e: e.activation(out=tq[0:nt, :], in_=cntc[0:nt, :], func=AF.Sign, bias=float(Stot - 511)),
                    reads=[bcnt], writes=[btq])
                add("act", (lambda n: lambda e: e.activation(out=mid[0:nt, :], in_=tq[0:nt, :], func=AF.Identity,
                                                             scale=nsteps[0:nt, n + 1:n + 2], bias=mid[0:nt, :]))(n),
                    reads=[btq, bsteps, bmid], writes=[bmid])
                yield
            add("dve", lambda e: e.tensor_scalar(out=thr[0:nt, :], in0=mid[0:nt, :], scalar1=-1.0, scalar2=steps[0:nt, NIT:NIT + 1],
                                                 op0=ALU.mult, op1=ALU.subtract), reads=[bmid, bsteps], writes=[bthr])
        add("dve", lambda e: e.tensor_scalar(out=mask_ap[0:nt, 0:Stot], in0=iv, scalar1=thr[0:nt, :], scalar2=-30000.0,
                                             op0=ALU.is_lt, op1=ALU.mult), reads=[bisc_, bthr], writes=[bmask])

    def mixer_tile(l, t, part="AB"):
        nt = nrows(t)
        sample = (t == 16)
        par = t % 2
        qTz, bqT, concatT, bcc = qTzs[par], bqTs[par], ccT[par], bccs[par]
        maskb, bmaskb = maskbs[par], bmaskbs[par]
        if "A" in part:
            yield from _mixer_A(l, t, nt, sample, qTz, bqT, concatT, bcc, maskb, bmaskb)
        if "B" in part:
            yield from _mixer_B(l, t, nt, sample, qTz, bqT, concatT, bcc, maskb, bmaskb)

    def _mixer_A(l, t, nt, sample, qTz, bqT, concatT, bcc, maskb, bmaskb):
        rT = ropeB[t % 2]; brT = bropeB[t % 2]
        add("sp", lambda e: e.dma_start(out=rT[:, :], in_=rope[:, t, :]), writes=[brT], dma=True)
        norm_to_hT(t, hT, bhT, 0)
        if DBG["stage"] < 2:
            return
        tm = [(0, 512), (512, 512), (1024, 512), (1536, 40)]
        for n, (c0, w) in enumerate(tm):
            if not DBG["tm"]:
                break

            def f(e, c0=c0, w=w, n=n):
                ins = None
                for k in range(8):
                    ins = e.matmul(PB[n][0:nt, 0:w], lhsT=hT[:, k, 0:nt], rhs=win[:, k, c0:c0 + w], start=(k == 0), stop=(k == 7))
                return ins
            add("pe", f, reads=[bhT, bWA], writes=[bPB[n]])
        for fi in range(8):
            if not DBG["fm"]:
                break
            bank = PB[4] if fi < 4 else PB[5]
            bb = bPB[4] if fi < 4 else bPB[5]

            def f(e, fi=fi, bank=bank):
                ins = None
                for k in range(8):
                    ins = e.matmul(bank[:, (fi % 4) * 128:(fi % 4) * 128 + nt], lhsT=win[:, k, TMW + fi * 128:TMW + (fi + 1) * 128],
                                   rhs=hT[:, k, 0:nt], start=(k == 0), stop=(k == 7))
                return ins
            add("pe", f, reads=[bhT, bWA], writes=[bb])
        PF0 = PB[4][:, :].rearrange("p (f s) -> p f s", f=4)
        PF1 = PB[5][:, :].rearrange("p (f s) -> p f s", f=4)
        if DBG["stage"] < 3:
            return
        add("act", lambda e: e.activation(out=SB_[0:nt, 0:512], in_=PB[0][0:nt, 0:512], func=AF.Square), reads=[bPB[0]], writes=[bSB])
        add("act", lambda e: e.activation(out=SB_[0:nt, 512:768], in_=PB[1][0:nt, 0:256], func=AF.Square), reads=[bPB[1]], writes=[bSB])
        add("dve", lambda e: e.tensor_reduce(out=ss12[0:nt, :], in_=SB_[0:nt, 0:768].rearrange("p (h d) -> p h d", h=12), axis=AX.X, op=ALU.add),
            reads=[bSB], writes=[bss12])
        add("dve", lambda e: e.tensor_scalar(out=ms12[0:nt, :], in0=ss12[0:nt, :], scalar1=1.0 / 64, scalar2=EPS, op0=ALU.mult, op1=ALU.add),
            reads=[bss12], writes=[bss12])
        add("pool", lambda e: e.tensor_tensor(out=r12[0:nt, :], in0=ms12[0:nt, :], in1=nh[0:nt, :], op=ALU.pow), reads=[bss12, bnh], writes=[br12])
        add("dve", lambda e: e.tensor_tensor(out=SB_[0:nt, 0:512].rearrange("p (h d) -> p h d", h=8),
                                             in0=PB[0][0:nt, 0:512].rearrange("p (h d) -> p h d", h=8),
                                             in1=r12[0:nt, 0:8].unsqueeze(2).to_broadcast([nt, 8, 64]), op=ALU.mult),
            reads=[bPB[0], br12], writes=[bSB])
        add("dve", lambda e: e.tensor_tensor(out=SB_[0:nt, 512:768].rearrange("p (h d) -> p h d", h=4),
                                             in0=PB[1][0:nt, 0:256].rearrange("p (h d) -> p h d", h=4),
                                             in1=r12[0:nt, 8:12].unsqueeze(2).to_broadcast([nt, 4, 64]), op=ALU.mult),
            reads=[bPB[1], br12], writes=[bSB])
        add("dve", lambda e: e.tensor_tensor(out=SB_[0:nt, 0:512].rearrange("p (h d) -> p h d", h=8),
                                             in0=SB_[0:nt, 0:512].rearrange("p (h d) -> p h d", h=8),
                                             in1=gqk[0:nt, 0:64].unsqueeze(1).to_broadcast([nt, 8, 64]), op=ALU.mult),
            reads=[bSB, bgqk], writes=[bSB])
        add("dve", lambda e: e.tensor_tensor(out=SB_[0:nt, 512:768].rearrange("p (h d) -> p h d", h=4),
                                             in0=SB_[0:nt, 512:768].rearrange("p (h d) -> p h d", h=4),
                                             in1=gqk[0:nt, 64:128].unsqueeze(1).to_broadcast([nt, 4, 64]), op=ALU.mult),
            reads=[bSB, bgqk], writes=[bSB])
        if DBG["stage"] < 3.1:
            return
        rope_ops(SB_[0:nt, 0:768], qkr[0:nt, :], nt, 12, 32, rT[0:nt, 0:32], rT[0:nt, 32:64], bSB, bqkr, brT)
        if DBG["stage"] < 3.2:
            return
        add("act", lambda e: e.activation(out=v32[0:nt, :], in_=PB[1][0:nt, 256:512], func=AF.Copy), reads=[bPB[1]], writes=[bv32])
        if not sample:
            Vdst = Vb[0:nt, t, 64:448].rearrange("p (a b) -> p a b", a=2)[:, :, 0:128]
            bVdst = bVb[t]
        else:
            Vdst = Vn[0:nt, 64:448].rearrange("p (a b) -> p a b", a=2)[:, :, 0:128]
            bVdst = bVn
        add("dve", lambda e: e.tensor_copy(out=Vdst, in_=PB[1][0:nt, 256:512].rearrange("p (a b) -> p a b", a=2)), reads=[bPB[1]], writes=[bVdst])
        if DBG["stage"] < 3.3:
            return
        add("act", lambda e: e.activation(out=qki[0:nt, 0:256], in_=PB[2][0:nt, 256:512], func=AF.Copy), reads=[bPB[2]], writes=[bqki])
        add("act", lambda e: e.activation(out=qki[0:nt, 256:288], in_=PB[3][0:nt, 0:32], func=AF.Copy), reads=[bPB[3]], writes=[bqki])
        add("act", lambda e: e.activation(out=w8[0:nt, :], in_=PB[3][0:nt, 32:40], func=AF.Copy, scale=0.0625), reads=[bPB[3]], writes=[bw8])
        if DBG["stage"] < 3.4:
            return
        TV = SC[0:nt, 0:256]
        gelu_core(PB[2][0:nt, 0:256], TV, [bPB[2]], bSC)
        add("act", lambda e: e.activation(out=vbf[0:nt, :], in_=TV, func=AF.Copy, scale=0.5), reads=[bSC], writes=[bvbf])
        if sample:
            add("act", lambda e: e.activation(out=sgv[0:nt, :], in_=TV, func=AF.Copy, scale=0.5), reads=[bSC], writes=[bsgv])
            add("sp", lambda e: e.dma_start(out=nsgv[l], in_=sgv[0:nt, :]), reads=[bsgv], dma=True)
        if DBG["stage"] < 3.5:
            return
        rope_ops(qki[0:nt, :], qkir[0:nt, :], nt, 9, 16, rT[0:nt, 64:80], rT[0:nt, 80:96], bqki, bqkir, brT)
        if DBG["stage"] < 3.6:
            return
        if not sample:
            row0 = t * 128
            add("sp", lambda e: e.dma_start(out=nkp[l, row0:row0 + nt, :], in_=qkr[0:nt, 512:768]), reads=[bqkr], dma=True)
            add("sp", lambda e: e.dma_start(out=nvp[l, row0:row0 + nt, :], in_=v32[0:nt, :]), reads=[bv32], dma=True)
            add("sp", lambda e: e.dma_start(out=nkip[l, row0:row0 + nt, :], in_=qkir[0:nt, 256:288]), reads=[bqkir], dma=True)
        else:
            add("sp", lambda e: e.dma_start(out=nks[l], in_=qkr[0:nt, 512:768]), reads=[bqkr], dma=True)
            add("sp", lambda e: e.dma_start(out=nvs[l], in_=v32[0:nt, :]), reads=[bv32], dma=True)
            add("sp", lambda e: e.dma_start(out=nkis[l], in_=qkir[0:nt, 256:288]), reads=[bqkir], dma=True)
        if DBG["stage"] < 3.7:
            return
        add("act", lambda e: e.activation(out=qkbf[0:nt, :], in_=qkr[0:nt, :], func=AF.Copy), reads=[bqkr], writes=[bqkbf])

        def trq(e):
            ins = None
            for b_ in range(6):
                ins = e.transpose(PT[:, b_ * 128:b_ * 128 + nt], qkbf[0:nt, b_ * 128:(b_ + 1) * 128], ident[0:nt, 0:nt])
            return ins
        add("pe", trq, reads=[bqkbf, bident], writes=[bPT])
        PT3 = PT[:, :].rearrange("p (k s) -> p k s", k=8)
        for gl in range(2):
            if DBG.get("skip") == "qTz" or (DBG.get("skip") == "qTz1" and gl == 1):
                continue
            add("dve", (lambda gl: lambda e: e.tensor_copy(
                out=qTz[64 * gl:64 * gl + 64, gl, :, :, 0:nt],
                in_=PT3[64 * gl:64 * gl + 64, 0:4, 0:nt].rearrange("p (a b) s -> p a b s", a=2)))(gl), reads=[bPT], writes=[bqT])
        if DBG.get("skip") == "KT":
            return
        if not sample:
            add("dve", lambda e: e.tensor_copy(out=KT[:, :, t * 128:t * 128 + nt], in_=PT3[:, 4:6, 0:nt]), reads=[bPT], writes=[bKT[t]])
        else:
            add("dve", lambda e: e.tensor_copy(out=KTn[:, :, 0:nt], in_=PT3[:, 4:6, 0:nt]), reads=[bPT], writes=[bKTn])
        if DBG["stage"] < 3.8:
            return
        if DBG.get("skip") == "qibf":
            return
        add("act", lambda e: e.activation(out=qibf[0:nt, :], in_=qkir[0:nt, 0:256], func=AF.Copy), reads=[bqkir], writes=[bqibf])
        if DBG.get("skip") == "ki4":
            return
        add("dve", lambda e: e.tensor_copy(out=ki4[0:nt, :].rearrange("p (r d) -> p r d", r=4),
                                           in_=qkir[0:nt, 256:288].unsqueeze(1).to_broadcast([nt, 4, 32])), reads=[bqkir], writes=[bki4])

        def tri(e):
            e.transpose(PT[:, 0:nt], qibf[0:nt, 0:128], ident[0:nt, 0:nt])
            e.transpose(PT[:, 128:128 + nt], qibf[0:nt, 128:256], ident[0:nt, 0:nt])
            return e.transpose(PT[:, 256:256 + nt], ki4[0:nt, :], ident[0:nt, 0:nt])
        if DBG.get("skip") == "tri":
            return
        add("pe", tri, reads=[bqibf, bki4, bident], writes=[bPT])
        if DBG.get("skip") == "qiT":
            return
        add("dve", lambda e: e.tensor_copy(out=qiT[:, :, 0:nt], in_=PT3[:, 0:2, 0:nt]), reads=[bPT], writes=[bqiT])
        if DBG.get("skip") == "kiT":
            return
        if not sample:
            add("dve", lambda e: e.tensor_copy(out=kiT[:, t * 128:t * 128 + nt], in_=PT[:, 256:256 + nt]), reads=[bPT], writes=[bkiT[t]])
        else:
            add("dve", lambda e: e.tensor_copy(out=kiTn[:, 0:nt], in_=PT[:, 256:256 + nt]), reads=[bPT], writes=[bkiTn])

        if DBG["stage"] < 4:
            return
        TU = SC[:, 256:512].rearrange("p (c s) -> p c s", c=2)[:, :, 0:nt]
        gelu_core(PF0[:, 0:2, 0:nt], TU, [bPB[4]], bSC)
        for c in range(2):
            for hh in range(2):
                h = 2 * c + hh
                add("pe", (lambda c, h: lambda e: e.matmul(PB7[:, h * 128:h * 128 + nt], lhsT=vbf[0:nt, c * 128:(c + 1) * 128],
                                                           rhs=wmT[0:nt, h, 0:nt], start=True, stop=True))(c, h),
                    reads=[bvbf, bwmT], writes=[bPB7])
        for c in range(2):
            for hh in range(2):
                h = 2 * c + hh
                r0 = hh * 64
                tmpv = SC[r0:r0 + 64, 512 + h * 128:512 + h * 128 + nt]
                add("dve", (lambda h, r0, c, tmpv: lambda e: e.tensor_tensor(out=tmpv, in0=PB7[r0:r0 + 64, h * 128:h * 128 + nt],
                                                                             in1=bP[r0:r0 + 64, c, 0:nt], op=ALU.add))(h, r0, c, tmpv),
                    reads=[bPB7, bbP], writes=[bSC])
                add("dve", (lambda r0, c, tmpv: lambda e: e.scalar_tensor_tensor(out=concatT[r0:r0 + 64, c, 0:nt], in0=tmpv, scalar=0.5,
                                                                                 in1=SC[r0:r0 + 64, 256 + c * 128:256 + c * 128 + nt],
                                                                                 op0=ALU.mult, op1=ALU.mult))(r0, c, tmpv),
                    reads=[bSC], writes=[bcc[c]])
        if DBG["stage"] < 5:
            return
        if t == 0:
            add("dve", lambda e: e.memset(xpad[:, :, 0:2], 0.0), writes=[bxpad])
        elif sample:
            add("sp", lambda e: e.dma_start(out=xpad[:, :, 0:2], in_=sconvT[l]), writes=[bxpad], dma=True)
        add("act", lambda e: e.activation(out=hin[:, :, 0:nt], in_=PF1[:, 2:4, 0:nt], func=AF.Copy), reads=[bPB[5]], writes=[bhin])
        add("dve", lambda e: e.tensor_tensor(out=xpad[:, :, 2:2 + nt], in0=PF1[:, 0:2, 0:nt], in1=hin[:, :, 0:nt], op=ALU.mult),
            reads=[bPB[5], bhin], writes=[bxpad])
        for c in range(2):
            ci = l * 6 + c * 3
            add("dve", (lambda c, ci: lambda e: e.tensor_scalar(out=cacc[:, c, 0:nt], in0=xpad[:, c, 0:nt], scalar1=cw[:, ci:ci + 1],
                                                                scalar2=None, op0=ALU.mult))(c, ci), reads=[bxpad, bcw], writes=[bcacc])
            for tap in (1, 2):
                add("dve", (lambda c, ci, tap: lambda e: e.scalar_tensor_tensor(out=cacc[:, c, 0:nt], in0=xpad[:, c, tap:tap + nt],
                                                                                scalar=cw[:, ci + tap:ci + tap + 1], in1=cacc[:, c, 0:nt],
                                                                                op0=ALU.mult, op1=ALU.add))(c, ci, tap),
                    reads=[bxpad, bcw, bcacc], writes=[bcacc])
        add("dve", lambda e: e.tensor_tensor(out=concatT[:, 2:4, 0:nt], in0=PF0[:, 2:4, 0:nt], in1=cacc[:, :, 0:nt], op=ALU.mult),
            reads=[bPB[4], bcacc], writes=[bcc[2], bcc[3]])
        if t == 15:
            add("sp", lambda e: e.dma_start(out=ncp[l], in_=xpad[:, :, nt:nt + 2]), reads=[bxpad], dma=True)
        elif sample:
            add("sp", lambda e: e.dma_start(out=ncs[l], in_=xpad[:, :, nt:nt + 2]), reads=[bxpad], dma=True)
        if t < 15:
            add("act", lambda e: e.activation(out=xpad[:, :, 0:2], in_=xpad[:, :, nt:nt + 2], func=AF.Copy), reads=[bxpad], writes=[bxpad])

        if DBG["stage"] < 6:
            return
        if not sample:
            Stot = 128 * (t + 1)
            for n0 in range(0, Stot, 512):
                w_ = min(512, Stot - n0)
                yield from indexer_chunk(nt, n0, w_, kiT, n0, isc, [bkiT[i_] for i_ in range(n0 // 128, (n0 + w_) // 128)], bisc)
            yield from topk_mask(nt, Stot, isc, maskb, bisc, bmaskb, inadm=t * 128 + 64)
            return
        if True:
            for b_ in SAMPLE_AR:
                seed(b_, PROMPT_AR)
            for blk in range(4):
                s0 = blk * 1024
                add("pool", (lambda s0: lambda e: e.dma_start(out=kistage[:, :, :], in_=cki[l, s0:s0 + 1024, :].rearrange("(k p) d -> p k d", p=128)))(s0),
                    writes=[bkistage], dma=True)
                add("dve", lambda e: e.tensor_copy(out=ki4s[:, :, :].rearrange("p k (r d) -> p k r d", r=4),
                                                   in_=kistage[:, :, :].unsqueeze(2).to_broadcast([128, 8, 4, 32])), reads=[bkistage], writes=[bki4s])

                def trk(e):
                    ins = None
                    for k in range(8):
                        ins = e.transpose(PT[:, k * 128:(k + 1) * 128], ki4s[:, k, :], ident[:, :])
                    return ins
                add("pe", trk, reads=[bki4s, bident], writes=[bPT])
                add("act", lambda e: e.activation(out=kiT_s[:, :], in_=PT[:, :], func=AF.Copy), reads=[bPT], writes=[bkiT_s])
                for n0 in (0, 512):
                    yield from indexer_chunk(nt, s0 + n0, 512, kiT_s, n0, isc_s, [bkiT_s], bisc_s)
            yield from indexer_chunk(nt, PAST, TS, kiTn, 0, isc_s, [bkiTn], bisc_s)
            Stot = PAST + TS
            if DBG["ffn"]:
                seed(bF[0], [bWA]); seed(bF[1], [bWA])
                load_ffn_dma(l, 0, 0); load_ffn_dma(l, 1, 1)
                PREF[l] = True
            yield from topk_mask(nt, Stot, isc_s, maskb_s, bisc_s, bmaskb_s, inadm=None)
            for blk in range(4):
                s0 = blk * 1024
                add("pool", (lambda s0: lambda e: e.dma_start(out=kstage[:, :, :], in_=ck[l, s0:s0 + 1024, :].rearrange("(k p) d -> p k d", p=128)))(s0),
                    writes=[bkstage, bv32], dma=True)
                for a_ in range(2):
                    add("pool", (lambda s0, a_: lambda e: e.dma_start(
                        out=Vb_s[:, :, 64 + a_ * 192:64 + a_ * 192 + 128],
                        in_=cv[l, s0:s0 + 1024, a_ * 128:(a_ + 1) * 128].rearrange("(k p) d -> p k d", p=128)))(s0, a_),
                        writes=[bVb_s], dma=True)
                for half in range(2):
                    def trK(e, half=half):
                        ins = None
                        for k in range(4):
                            for gp in range(2):
                                ins = e.transpose(PT[:, (gp * 4 + k) * 128:(gp * 4 + k + 1) * 128],
                                                  kstage[:, half * 4 + k, gp * 128:(gp + 1) * 128], ident[:, :])
                        return ins
                    add("pe", trK, reads=[bkstage, bident], writes=[bPT])
                    add("act", (lambda half: lambda e: e.activation(out=KT_s[:, :, half * 512:(half + 1) * 512],
                                                                    in_=PT[:, :].rearrange("p (g s) -> p g s", g=2), func=AF.Copy))(half),
                        reads=[bPT], writes=[bKT_s])
                tiles = [(kt, 128, s0 + kt * 128) for kt in range(8)]
                for g in range(4):
                    yield from attn_block(nt, g, tiles, KT_s, Vb_s, maskb_s, (lambda kt: [bKT_s, bVb_s]), blk == 0, False, bmaskb_s, qTz, bqT, concatT, bcc)
            for g in range(4):
                yield from attn_block(nt, g, [(0, TS, PAST)], KTn, Vn[:, :].rearrange("p (k c) -> p k c", k=1), maskb_s, (lambda kt: [bKTn, bVn]), False, True, bmaskb_s,
                           qTz, bqT, concatT, bcc)

        _out_proj(t, nt, concatT, bcc)

    def _mixer_B(l, t, nt, sample, qTz, bqT, concatT, bcc, maskb, bmaskb):
        if sample:
            return
        tiles = [(kt, 128, kt * 128) for kt in range(t + 1)]
        for g in range(4):
            yield from attn_block(nt, g, tiles, KT, Vb, maskb, (lambda kt: [bKT[kt], bVb[kt]]), True, True, bmaskb, qTz, bqT, concatT, bcc)
        _out_proj(t, nt, concatT, bcc)

    def _out_proj(t, nt, concatT, bcc):
        if DBG["stage"] < 8:
            return
        for n in range(2):
            def f(e, n=n):
                ins = None
                for c in range(8):
                    ins = e.matmul(PB[n][0:nt, 0:512], lhsT=concatT[:, c, 0:nt], rhs=wout[:, c, n * 512:(n + 1) * 512],
                                   start=(c == 0), stop=(c == 7))
                return ins
            add("pe", f, reads=bcc + [bWB], writes=[bPB[n]])
            add("dve", (lambda n: lambda e: e.tensor_tensor(out=X[0:nt, t, n * 512:(n + 1) * 512], in0=X[0:nt, t, n * 512:(n + 1) * 512],
                                                            in1=PB[n][0:nt, 0:512], op=ALU.add))(n), reads=[bX[t], bPB[n]], writes=[bX[t]])

    def ffn_layer(l, last):
        seed(_ba, [bqkr, bqkbf])
        allh = Buf("hT2seed")
        seed(allh, PROMPT_AR + SAMPLE_AR)
        for t in range(NT):
            bhT2[t].w = None
            bhT2[t].r = list(allh.r)
            norm_to_hT(t, hT2, bhT2[t], t * 128)
        groups = [(0, 512, [0, 1, 2, 3]), (512, 512, [4, 5, 6, 7]), (1024, 512, [8, 9, 10, 11]), (1536, 512, [12, 13, 14, 15]), (2048, TS, [16])]
        for f in range(8):
            slot = f % 3
            wu, wd = ffn_views(slot)
            for gi, (tok0, gw, tl) in enumerate(groups):
                ab = (f * 5 + gi) % 2
                for m in range(4):
                    def fu(e, m=m, gw=gw, wu=wu, tok0=tok0):
                        ins = None
                        for k in range(8):
                            ins = e.matmul(PB[m][:, 0:gw], lhsT=wu[:, k, m * 128:(m + 1) * 128], rhs=hT2[:, k, tok0:tok0 + gw],
                                           start=(k == 0), stop=(k == 7))
                        return ins
                    add("pe", fu, reads=[bF[slot]] + [bhT2[t] for t in tl], writes=[bPB[m]])
                    add("act", (lambda m, gw: lambda e: e.activation(out=rt[m % 2][:, 0:gw], in_=PB[m][:, 0:gw], func=AF.Relu))(m, gw),
                        reads=[bPB[m]], writes=[brt[m % 2]])
                    add("act", (lambda m, gw, ab: lambda e: e.activation(out=actT[ab][:, m, 0:gw], in_=rt[m % 2][:, 0:gw], func=AF.Square))(m, gw, ab),
                        reads=[brt[m % 2]], writes=[bactT[ab]])
                for ti, t in enumerate(tl):
                    nt = nrows(t)
                    for n in range(2):
                        pd = PB[4 + n] if ti % 2 == 0 else (PB7 if n == 1 else PB[4 + n])
                        bpd = bPB[4 + n] if ti % 2 == 0 else (bPB7 if n == 1 else bPB[4 + n])

                        def fd(e, n=n, ti=ti, pd=pd, nt=nt, ab=ab, wd=wd):
                            ins = None
                            for m in range(4):
                                ins = e.matmul(pd[0:nt, 0:512], lhsT=actT[ab][:, m, ti * 128:ti * 128 + nt], rhs=wd[:, m, n * 512:(n + 1) * 512],
                                               start=(m == 0), stop=(m == 3))
                            return ins
                        add("pe", fd, reads=[bactT[ab], bF[slot]], writes=[bpd])
                        add("dve", (lambda n, t, pd, nt: lambda e: e.tensor_tensor(out=X[0:nt, t, n * 512:(n + 1) * 512],
                                                                               in0=X[0:nt, t, n * 512:(n + 1) * 512], in1=pd[0:nt, 0:512],
                                                                               op=ALU.add))(n, t, pd, nt), reads=[bX[t], bpd], writes=[bX[t]])
            if f + 3 < 8:
                load_ffn_chunk(l, f + 3, slot)

    for l in range(depth):
        if l == 0:
            load_mix_weights(0)
        load_layer_params(l)
        for b_ in PROMPT_AR:
            if l > 0:
                seed(b_, bhT2 + SAMPLE_AR)
        if l > 0:
            seed(bqkr, [_ba]); seed(bqkbf, [_ba])
        for c0 in (0, 192, 384):
            add("dve", (lambda c0: lambda e: e.memset(Vb[:, :, c0:c0 + 64], 1.0))(c0), writes=bVb)
            if l == 0:
                add("dve", (lambda c0: lambda e: e.memset(Vn[:, c0:c0 + 64], 1.0))(c0), writes=[bVn])
        npt = min(16, DBG["tiles"])

        def drive(gens):
            gens = list(gens)
            while gens:
                for g_ in list(gens):
                    try:
                        next(g_)
                    except StopIteration:
                        gens.remove(g_)
        if DBG.get("pipe", True):
            if npt > 0:
                drive([mixer_tile(l, 0, "A")])
            for t in range(npt):
                gl_ = []
                if t + 1 < npt:
                    gl_.append(mixer_tile(l, t + 1, "A"))
                gl_.append(mixer_tile(l, t, "B"))
                drive(gl_)
        else:
            for t in range(npt):
                drive([mixer_tile(l, t, "AB")])
        if DBG["sample"]:
            for c0 in (0, 192, 384):
                add("dve", (lambda c0: lambda e: e.memset(Vb_s[:, :, c0:c0 + 64], 1.0))(c0), writes=[bVb_s], reads=[])
            drive([mixer_tile(l, 16, "AB")])
        if PREF.get(l):
            seed(bF[2], [bWB])
            fold_ffn(l, 0, 0); fold_ffn(l, 1, 1)
            load_ffn_chunk(l, 2, 2)
        else:
            seed(bF[0], [bWA]); seed(bF[1], [bWA]); seed(bF[2], [bWB])
            for f in range(3):
                load_ffn_chunk(l, f, f)
        if DBG["ffn"]:
            ffn_layer(l, l == depth - 1)
        if l + 1 < depth:
            seed(bWA, [bF[0], bF[1]]); seed(bWB, [bF[2]])
            load_mix_weights(l + 1)
    for t in range(16):
        add("sp", (lambda t: lambda e: e.dma_start(out=yp[t * 128:(t + 1) * 128, :], in_=X[:, t, :]))(t), reads=[bX[t]], dma=True)
    add("sp", lambda e: e.dma_start(out=ys, in_=X[0:TS, 16, :]), reads=[bX[16]], dma=True)

    S.finalize(nc, st)
    with nc.Block() as block:
        S.emit(block)
    st.close()
    return nc


def _rope_tables():
    tab = np.zeros((128, NT, 96), np.float32)
    for t in range(NT):
        n = 128 if t < 16 else TS
        pos = (np.arange(n) + (t * 128 if t < 16 else PAST)).astype(np.float32)
        for half, off in ((32, 0), (16, 64)):
            inv = (np.float32(10000.0) ** (-np.arange(half, dtype=np.float32) / np.float32(half))).astype(np.float32)
            ang = (pos[:, None] * inv[None, :]).astype(np.float32)
            tab[:n, t, off:off + half] = np.cos(ang).astype(np.float32)
            tab[:n, t, off + half:off + 2 * half] = np.sin(ang).astype(np.float32)
    return tab


def _win_perm():
    qcols = []
    for gp in range(2):
        for j in range(2):
            for gl in range(2):
                h = 2 * (2 * gp + gl) + j
                qcols.extend(range(1280 + h * 64, 1280 + (h + 1) * 64))
    tmc = qcols + list(range(1792, 2048)) + list(range(2048, 2304)) + list(range(256, 512)) + list(range(2304, 2560)) \
        + list(range(2560, 2592)) + list(range(2592, 2600))
    fmc = list(range(0, 256)) + list(range(512, 1280))
    return np.array(tmc + fmc, dtype=np.int64)


_NC_CACHE = {}


def kernel(x_prompt, x_sample, cache_k, cache_v, cache_kidx, state_conv, g_mix, w_in, q_gain, k_gain, w_s, b_s, conv_w,
           w_out, g_ffn, w_up, w_down):
    f = lambda a: np.ascontiguousarray(np.asarray(a, dtype=np.float32))
    x_prompt, x_sample, cache_k, cache_v, cache_kidx, state_conv = map(f, (x_prompt, x_sample, cache_k, cache_v, cache_kidx, state_conv))
    g_mix, w_in, q_gain, k_gain, w_s, b_s, conv_w, w_out, g_ffn, w_up, w_down = map(
        f, (g_mix, w_in, q_gain, k_gain, w_s, b_s, conv_w, w_out, g_ffn, w_up, w_down))
    if "nc" not in _NC_CACHE:
        _NC_CACHE["nc"] = build_program()
    nc = _NC_CACHE["nc"]
    w_in_p = np.ascontiguousarray(w_in[:, :, _win_perm()])
    gvec = np.ascontiguousarray(np.concatenate([g_mix.reshape(2, 8, 128), g_ffn.reshape(2, 8, 128)], 0).reshape(32, 128).T)
    wsT = np.ascontiguousarray(w_s.transpose(0, 3, 1, 2))
    cwv = np.ascontiguousarray(conv_w.reshape(2, 3, 2, 128).transpose(3, 0, 2, 1).reshape(128, 12))
    rope = _rope_tables()
    p2 = (2.0 ** -(np.arange(32, dtype=np.float64) + 1)).astype(np.float32).reshape(1, 32)
    shared = dict(w_in=w_in_p, w_out=w_out, w_up=w_up, w_down=w_down, gvec=gvec, wsT=wsT, bs=b_s, cwv=cwv, qg=q_gain, kg=k_gain,
                  rope=rope, p2=p2)
    in_maps = []
    for c in range(8):
        m = dict(shared)
        m["xp"] = x_prompt[c]
        m["xs"] = x_sample[c]
        m["ck"] = np.ascontiguousarray(cache_k[:, c].reshape(2, PAST, 256))
        m["cv"] = np.ascontiguousarray(cache_v[:, c].reshape(2, PAST, 256))
        m["cki"] = np.ascontiguousarray(cache_kidx[:, c])
        m["sconvT"] = np.ascontiguousarray(state_conv[:, c].reshape(2, 2, 2, 128).transpose(0, 3, 2, 1))
        in_maps.append(m)
    res = run_bass_kernel_spmd(nc, in_maps, core_ids=list(range(8)))
    R = res.results
    st_ = lambda k: np.stack([np.asarray(R[c][k]) for c in range(8)])
    yp = st_("yp"); ys = st_("ys")
    nkp = st_("nkp").transpose(1, 0, 2, 3).reshape(2, 8, SEQ, 4, 64)
    nvp = st_("nvp").transpose(1, 0, 2, 3).reshape(2, 8, SEQ, 4, 64)
    nkip = st_("nkip").transpose(1, 0, 2, 3)
    cvt = lambda a: a.transpose(1, 0, 4, 3, 2).reshape(2, 8, 2, 256)
    ncp = cvt(st_("ncp")); ncs = cvt(st_("ncs"))
    nks = st_("nks").transpose(1, 0, 2, 3).reshape(2, 8, TS, 4, 64)
    nvs = st_("nvs").transpose(1, 0, 2, 3).reshape(2, 8, TS, 4, 64)
    nkis = st_("nkis").transpose(1, 0, 2, 3)
    nsgv = st_("nsgv").transpose(1, 0, 2, 3)
    out = (yp, ys, nkp, nvp, nkip, ncp, nks, nvs, nkis, ncs, nsgv)
    return tuple(np.ascontiguousarray(o, dtype=np.float32) for o in out)
```
